# Optimizing a Trainium2 kernel written in Bass

```python
import math
import jax, jax.numpy as jnp
from jax import lax
import numpy as np

D_MODEL = 1024
BATCH = 32
SEQ = 256
DEPTH = 2
DEC_BATCH = 2
DEC_SEQ = 2048
PAST_LEN = 512

GRID_W = 64
N_AB_LAYERS = (DEPTH + 1) // 2
N_CD_LAYERS = DEPTH // 2
HEAD_DIM = 64
N_HEADS = (D_MODEL // 2) // HEAD_DIM
N_KV_HEADS = N_HEADS // 4
ROPE_PAIRS = HEAD_DIM // 4
ROPE_THETA = 10000.0
Q_BLOCK = 128
SSM_WIDTH = D_MODEL // 2
SSM_GROUP_CH = 16
SSM_GROUPS = SSM_WIDTH // SSM_GROUP_CH
SSM_STATE = 64
GMLP_WIDTH = D_MODEL // 2
GMLP_GROUPS = 8
GMLP_GROUP_CH = GMLP_WIDTH // GMLP_GROUPS
CHUNK = 128
FNET_WIDTH = D_MODEL // 2
FNET_GROUPS = 4
FNET_GROUP_CH = FNET_WIDTH // FNET_GROUPS
RMS_EPS = 1e-6
ATTN_Q = N_HEADS * HEAD_DIM
ATTN_KV = N_KV_HEADS * HEAD_DIM
AB_SECTIONS = (ATTN_Q, ATTN_KV, ATTN_KV, ATTN_Q, SSM_WIDTH, SSM_WIDTH)
CD_SECTIONS = (GMLP_WIDTH, GMLP_WIDTH, GMLP_WIDTH, FNET_WIDTH, FNET_WIDTH)
AB_IN = sum(AB_SECTIONS)
CD_IN = sum(CD_SECTIONS)
AB_MIX = ATTN_Q + SSM_WIDTH
CD_MIX = GMLP_WIDTH + FNET_WIDTH

kernel_name = "hybrid_prefix_diffusion_step"

f32 = jnp.float32


def split_sections(x, sizes):
    bounds = np.cumsum(sizes)[:-1]
    return jnp.split(x, [int(b) for b in bounds], axis=-1)


def rmsnorm(x, g):
    xf = x.astype(f32)
    y = xf * lax.rsqrt(jnp.mean(xf * xf, axis=-1, keepdims=True) + RMS_EPS)
    return (y * g.astype(f32)).astype(x.dtype)


def ada_modulation(cond, w, b, dtype):
    m = jax.nn.silu(cond.astype(f32)) @ w.astype(f32) + b.astype(f32)
    shift, scale, gate = jnp.split(m.astype(dtype), 3, axis=-1)
    return shift[:, None, :], scale[:, None, :], gate[:, None, :]


def axial_rope_tables(L):
    rows = L // GRID_W
    row = jnp.broadcast_to(jnp.arange(rows, dtype=f32)[:, None], (rows, GRID_W)).reshape(-1)
    col = jnp.broadcast_to(jnp.arange(GRID_W, dtype=f32)[None, :], (rows, GRID_W)).reshape(-1)
    inv = ROPE_THETA ** (-jnp.arange(ROPE_PAIRS, dtype=f32) / ROPE_PAIRS)
    ang = jnp.stack([row[:, None] * inv, col[:, None] * inv], axis=1)
    return jnp.cos(ang), jnp.sin(ang)


def apply_axial_rope(x, cos, sin):
    B, L, H, Dh = x.shape
    xf = x.astype(f32).reshape(B, L, H, 2, 2, ROPE_PAIRS)
    x1, x2 = xf[..., 0, :], xf[..., 1, :]
    cs, sn = cos[None, :, None], sin[None, :, None]
    out = jnp.stack([x1 * cs - x2 * sn, x1 * sn + x2 * cs], axis=-2)
    return out.reshape(B, L, H, Dh).astype(x.dtype)


def block_attention(q, k, v):
    B, Lq, H, Dh = q.shape
    KV = k.shape[2]
    G = H // KV
    nb = Lq // Q_BLOCK
    qb = q.reshape(B, nb, Q_BLOCK, KV, G, Dh).transpose(1, 0, 2, 3, 4, 5)
    scale = HEAD_DIM ** -0.5

    def one_block(qblk):
        s = jnp.einsum('bqkgd,bskd->bkgqs', qblk, k, preferred_element_type=f32) * scale
        p = jax.nn.softmax(s, axis=-1)
        return jnp.einsum('bkgqs,bskd->bqkgd', p.astype(v.dtype), v)

    o = lax.map(one_block, qb)
    return o.transpose(1, 0, 2, 3, 4, 5).reshape(B, Lq, H * Dh)


def _ssm_combine(e1, e2):
    a1, b1 = e1
    a2, b2 = e2
    return a1 * a2, a2 * b1 + b2


def s5_mixer(u, lam_re, lam_im, log_dt, b_re, b_im, c_re, c_im, d_skip, glu_w, glu_b, h0):
    Bsz, L, _ = u.shape
    uf = u.astype(f32)
    ug = uf.reshape(Bsz, L, SSM_GROUPS, SSM_GROUP_CH).astype(jnp.complex64)
    y = uf * d_skip.astype(f32)
    finals = []
    for di in range(2):
        reverse = di == 1
        lam = lax.complex(lam_re[di].astype(f32), lam_im[di].astype(f32))
        dt = jnp.exp(log_dt[di].astype(f32))[:, None]
        lam_bar = jnp.exp(lam * dt)
        b_bar = ((lam_bar - 1.0) / lam)[..., None] * lax.complex(b_re[di].astype(f32), b_im[di].astype(f32))
        bu = jnp.einsum('blgh,gph->blgp', ug, b_bar)
        if h0 is not None:
            edge = L - 1 if reverse else 0
            bu = bu.at[:, edge].add(lam_bar * h0[:, di])
        a = jnp.broadcast_to(lam_bar, (1, L) + lam_bar.shape)
        _, hs = lax.associative_scan(_ssm_combine, (a, bu), reverse=reverse, axis=1)
        finals.append(hs[:, 0] if reverse else hs[:, L - 1])
        cm = lax.complex(c_re[di].astype(f32), c_im[di].astype(f32))
        y = y + jnp.einsum('blgp,ghp->blgh', hs, cm).real.reshape(Bsz, L, SSM_WIDTH)
    y = jax.nn.gelu(y)
    y = y * jax.nn.sigmoid(y @ glu_w.astype(f32) + glu_b.astype(f32))
    return y.astype(u.dtype), jnp.stack(finals, axis=1)


def ab_mixer(h, w_in, q_gain, k_gain, lam_re, lam_im, log_dt, b_re, b_im, c_re, c_im,
             d_skip, glu_w, glu_b, w_out, ctx):
    Bsz, L, _ = h.shape
    q, k, v, g_a, u, g_b = split_sections(h @ w_in, AB_SECTIONS)
    q = rmsnorm(q.reshape(Bsz, L, N_HEADS, HEAD_DIM), q_gain)
    k = rmsnorm(k.reshape(Bsz, L, N_KV_HEADS, HEAD_DIM), k_gain)
    v = v.reshape(Bsz, L, N_KV_HEADS, HEAD_DIM)
    if ctx is None:
        attn = block_attention(q, k, v)
        h0 = None
    else:
        ctx_k, ctx_v, h0 = ctx
        cos, sin = axial_rope_tables(L)
        k_all = jnp.concatenate([ctx_k.astype(k.dtype), apply_axial_rope(k, cos, sin)], axis=1)
        v_all = jnp.concatenate([ctx_v.astype(v.dtype), v], axis=1)
        attn = block_attention(apply_axial_rope(q, cos, sin), k_all, v_all)
    ssm_out, h_final = s5_mixer(u, lam_re, lam_im, log_dt, b_re, b_im, c_re, c_im,
                                d_skip, glu_w, glu_b, h0)
    mixed = jnp.concatenate([attn * jax.nn.silu(g_a), ssm_out * jax.nn.silu(g_b)], axis=-1)
    state = jnp.stack([h_final.real, h_final.imag], axis=2)
    return mixed @ w_out, k, v, state


def cd_mixer(h, w_in, v_gain, w_s, b_s, fnet_w, w_out):
    Bsz, L, _ = h.shape
    cu, cv, g_c, dz, g_d = split_sections(h @ w_in, CD_SECTIONS)
    cu = jax.nn.gelu(cu)
    cv = rmsnorm(jax.nn.gelu(cv), v_gain)
    cvb = cv.reshape(Bsz, L // CHUNK, CHUNK, GMLP_GROUPS, GMLP_GROUP_CH)
    sp = jnp.einsum('hts,bnshc->bnthc', w_s, cvb) + b_s.T[None, None, :, :, None]
    out_c = cu * sp.reshape(Bsz, L, GMLP_WIDTH) * jax.nn.silu(g_c)
    z = dz.astype(f32).reshape(Bsz, L, FNET_GROUPS, FNET_GROUP_CH)
    fz = jnp.fft.fft2(z, axes=(1, 3), norm='ortho').real.reshape(Bsz, L, FNET_WIDTH)
    out_d = (fz.astype(h.dtype) @ fnet_w) * jax.nn.silu(g_d)
    return jnp.concatenate([out_c, out_d], axis=-1) @ w_out


def setup_inputs(seed: int = 0) -> dict:
    key = jax.random.key(seed)
    ks = iter(jax.random.split(key, 48))

    def nrm(shape, scale):
        return jax.random.normal(next(ks), shape, f32) * scale

    n_idx = jnp.arange(SSM_STATE, dtype=f32)
    return {
        'x_prompt': nrm((BATCH, SEQ, D_MODEL), 1.0),
        'x_sample': nrm((DEC_BATCH, DEC_SEQ, D_MODEL), 1.0),
        'cache_k': nrm((DEC_BATCH, N_AB_LAYERS, PAST_LEN, N_KV_HEADS, HEAD_DIM), 1.0),
        'cache_v': nrm((DEC_BATCH, N_AB_LAYERS, PAST_LEN, N_KV_HEADS, HEAD_DIM), 1.0),
        'state_ssm': nrm((DEC_BATCH, N_AB_LAYERS, 2, 2, SSM_GROUPS, SSM_STATE), 0.3),
        'c': nrm((DEC_BATCH, D_MODEL), 1.0),
        'c_ctx': nrm((D_MODEL,), 1.0),
        'ada_w': nrm((DEPTH, D_MODEL, 3 * D_MODEL), 0.5 * D_MODEL ** -0.5),
        'ada_b': nrm((DEPTH, 3 * D_MODEL), 0.02),
        'norm_pre': 1.0 + nrm((DEPTH, D_MODEL), 0.05),
        'norm_post': 1.0 + nrm((DEPTH, D_MODEL), 0.05),
        'ab_w_in': nrm((N_AB_LAYERS, D_MODEL, AB_IN), D_MODEL ** -0.5),
        'ab_q_norm': 1.0 + nrm((N_AB_LAYERS, HEAD_DIM), 0.05),
        'ab_k_norm': 1.0 + nrm((N_AB_LAYERS, HEAD_DIM), 0.05),
        'ssm_lambda_re': -0.5 + nrm((N_AB_LAYERS, 2, SSM_GROUPS, SSM_STATE), 0.01),
        'ssm_lambda_im': jnp.pi * n_idx + nrm((N_AB_LAYERS, 2, SSM_GROUPS, SSM_STATE), 0.01),
        'ssm_log_dt': jax.random.uniform(next(ks), (N_AB_LAYERS, 2, SSM_GROUPS), f32,
                                         math.log(1e-3), math.log(1e-1)),
        'ssm_b_re': nrm((N_AB_LAYERS, 2, SSM_GROUPS, SSM_STATE, SSM_GROUP_CH), (2 * SSM_GROUP_CH) ** -0.5),
        'ssm_b_im': nrm((N_AB_LAYERS, 2, SSM_GROUPS, SSM_STATE, SSM_GROUP_CH), (2 * SSM_GROUP_CH) ** -0.5),
        'ssm_c_re': nrm((N_AB_LAYERS, 2, SSM_GROUPS, SSM_GROUP_CH, SSM_STATE), SSM_STATE ** -0.5),
        'ssm_c_im': nrm((N_AB_LAYERS, 2, SSM_GROUPS, SSM_GROUP_CH, SSM_STATE), SSM_STATE ** -0.5),
        'ssm_d': nrm((N_AB_LAYERS, SSM_WIDTH), 1.0),
        'ssm_glu_w': nrm((N_AB_LAYERS, SSM_WIDTH, SSM_WIDTH), SSM_WIDTH ** -0.5),
        'ssm_glu_b': nrm((N_AB_LAYERS, SSM_WIDTH), 0.02),
        'ab_w_out': nrm((N_AB_LAYERS, AB_MIX, D_MODEL), AB_MIX ** -0.5),
        'cd_w_in': nrm((N_CD_LAYERS, D_MODEL, CD_IN), D_MODEL ** -0.5),
        'gmlp_v_norm': 1.0 + nrm((N_CD_LAYERS, GMLP_WIDTH), 0.05),
        'gmlp_w_s': nrm((N_CD_LAYERS, GMLP_GROUPS, CHUNK, CHUNK), CHUNK ** -0.5),
        'gmlp_b_s': 1.0 + nrm((N_CD_LAYERS, GMLP_GROUPS, CHUNK), 0.1),
        'fnet_w': nrm((N_CD_LAYERS, FNET_WIDTH, FNET_WIDTH), FNET_WIDTH ** -0.5),
        'cd_w_out': nrm((N_CD_LAYERS, CD_MIX, D_MODEL), CD_MIX ** -0.5),
    }


def reference(x_prompt, x_sample, cache_k, cache_v, state_ssm, c, c_ctx, ada_w, ada_b, norm_pre,
              norm_post, ab_w_in, ab_q_norm, ab_k_norm, ssm_lambda_re, ssm_lambda_im, ssm_log_dt,
              ssm_b_re, ssm_b_im, ssm_c_re, ssm_c_im, ssm_d, ssm_glu_w, ssm_glu_b, ab_w_out,
              cd_w_in, gmlp_v_norm, gmlp_w_s, gmlp_b_s, fnet_w, cd_w_out):

    def trunk(x, cond, ctx):
        ks, vs, ss = [], [], []
        for l in range(DEPTH):
            i = l // 2
            shift, scale, gate = ada_modulation(cond, ada_w[l], ada_b[l], x.dtype)
            hn = rmsnorm(x, norm_pre[l]) * (1 + scale) + shift
            if l % 2 == 0:
                if ctx is None:
                    layer_ctx = None
                else:
                    st = ctx[2][:, i]
                    h0 = lax.complex(st[:, :, 0].astype(f32), st[:, :, 1].astype(f32))
                    layer_ctx = (ctx[0][:, i], ctx[1][:, i], h0)
                o, k_l, v_l, s_l = ab_mixer(hn, ab_w_in[i], ab_q_norm[i], ab_k_norm[i],
                                            ssm_lambda_re[i], ssm_lambda_im[i], ssm_log_dt[i],
                                            ssm_b_re[i], ssm_b_im[i], ssm_c_re[i], ssm_c_im[i],
                                            ssm_d[i], ssm_glu_w[i], ssm_glu_b[i], ab_w_out[i], layer_ctx)
                ks.append(k_l)
                vs.append(v_l)
                ss.append(s_l)
            else:
                o = cd_mixer(hn, cd_w_in[i], gmlp_v_norm[i], gmlp_w_s[i], gmlp_b_s[i],
                             fnet_w[i], cd_w_out[i])
            x = x + gate * rmsnorm(o, norm_post[l])
        return x, ks, vs, ss

    y_prompt, ks, vs, ss = trunk(x_prompt, c_ctx[None, :], None)
    new_cache_k = jnp.stack(ks, axis=1)
    new_cache_v = jnp.stack(vs, axis=1)
    new_state_ssm = jnp.stack(ss, axis=1)
    y_sample, _, _, _ = trunk(x_sample, c, (cache_k, cache_v, state_ssm))
    return (y_prompt, y_sample, new_cache_k, new_cache_v, new_state_ssm)
```

```python
import contextlib
import numpy as np
import ml_dtypes
import concourse.bass as bass
import concourse.mybir as mybir
from concourse.bass_utils import run_bass_kernel_spmd

F32 = mybir.dt.float32
BF16 = mybir.dt.bfloat16
ALU = mybir.AluOpType
AF = mybir.ActivationFunctionType
AX = mybir.AxisListType

ENGS = ['pe', 'act', 'dve', 'pool', 'sp']
DMA_RING = 8
NCORES = 8
D = 1024
NPS = 4
SEQ = 256
EPS = 1e-6


class Sched:
    def __init__(self, nc):
        self.nc = nc
        self.ops = {e: [] for e in ENGS}
        self.lastw = {}
        self.readers = {}
        self.ndma = {e: 0 for e in ENGS}
        self.pending = {}
        self.groups = {}

    def expand(self, keys):
        out = []
        for k in keys:
            out.extend(self.groups.get(k, [k]))
        return out

    def barrier(self):
        deps = set()
        for e in ENGS:
            ops = self.ops[e]
            for i in range(len(ops) - 1, -1, -1):
                if not ops[i]['dma']:
                    deps.add((e, i))
                    break
            cnt = 0
            for i in range(len(ops) - 1, -1, -1):
                if ops[i]['dma']:
                    deps.add((e, i))
                    cnt += 1
                    if cnt >= DMA_RING:
                        break
        for e in ENGS:
            self.pending[e] = set(deps) | self.pending.get(e, set())

    def op(self, eng, fn, reads=(), writes=(), signal=True, dma=False):
        reads = self.expand(reads)
        writes = self.expand(writes)
        deps = set()
        if self.pending.get(eng):
            deps |= self.pending.pop(eng)
        for k in reads:
            if k in self.lastw:
                deps.add(self.lastw[k])
        for k in writes:
            if k in self.lastw:
                deps.add(self.lastw[k])
            for r in self.readers.get(k, ()):
                deps.add(r)
        idx = len(self.ops[eng])
        me = (eng, idx)
        deps.discard(me)
        if eng == 'pe':
            deps = {d for d in deps if d[0] != 'pe'}
        rec = dict(fn=fn, deps=deps, signal=signal or dma, dma=dma, dma_n=None)
        if dma:
            rec['dma_n'] = self.ndma[eng]
            self.ndma[eng] += 1
        self.ops[eng].append(rec)
        for k in writes:
            self.lastw[k] = me
            self.readers[k] = set()
        for k in reads:
            self.readers.setdefault(k, set()).add(me)
        return me

    def run(self):
        nc = self.nc
        with contextlib.ExitStack() as es:
            csem = {e: es.enter_context(nc.semaphore('c_' + e)) for e in ENGS}
            dsem = {e: [es.enter_context(nc.semaphore('d_%s%d' % (e, i))) for i in range(DMA_RING)]
                    for e in ENGS if self.ndma[e] > 0}
            ev = {}
            for e in ENGS:
                ops = self.ops[e]
                cnt = 0
                cum = []
                for o in ops:
                    if o['signal'] and not o['dma']:
                        cnt += 1
                    cum.append(cnt)
                nxt = None
                for i in range(len(ops) - 1, -1, -1):
                    o = ops[i]
                    if o['dma']:
                        n = o['dma_n']
                        ev[(e, i)] = (('d', e, n % DMA_RING), 16 * (n // DMA_RING + 1))
                    else:
                        if o['signal']:
                            nxt = cum[i]
                        assert nxt is not None, 'last compute op on engine %s must signal' % e
                        ev[(e, i)] = (('c', e), nxt)

            def semh(sid):
                return csem[sid[1]] if sid[0] == 'c' else dsem[sid[1]][sid[2]]

            final_waits = []
            for e in ENGS:
                n = self.ndma[e]
                for r in range(DMA_RING):
                    c = len(range(r, n, DMA_RING))
                    if c > 0:
                        final_waits.append((('d', e, r), 16 * c))

            def replay(e, eng):
                waited = {}
                ops = self.ops[e]
                for i, o in enumerate(ops):
                    need = {}
                    for d in o['deps']:
                        sid, val = ev[d]
                        if need.get(sid, 0) < val:
                            need[sid] = val
                    if o['dma'] and o['dma_n'] >= DMA_RING:
                        n0 = o['dma_n'] - DMA_RING
                        sid, val = ('d', e, n0 % DMA_RING), 16 * (n0 // DMA_RING + 1)
                        if need.get(sid, 0) < val:
                            need[sid] = val
                    for sid, val in need.items():
                        if waited.get(sid, 0) < val:
                            eng.wait_ge(semh(sid), val)
                            waited[sid] = val
                    ins = o['fn'](eng)
                    if o['dma']:
                        sid, val = ev[(e, i)]
                        ins.then_inc(semh(sid), 16)
                    elif o['signal']:
                        ins.then_inc(csem[e], 1)
                if e == 'sp':
                    for sid, val in final_waits:
                        if waited.get(sid, 0) < val:
                            eng.wait_ge(semh(sid), val)
                    for e2 in ENGS:
                        if e2 != 'sp':
                            c = sum(1 for o in self.ops[e2] if o['signal'] and not o['dma'])
                            if c > 0:
                                eng.wait_ge(csem[e2], c)

            with nc.Block() as block:
                @block.tensor
                def _(eng):
                    replay('pe', eng)

                @block.scalar
                def _(eng):
                    replay('act', eng)

                @block.vector
                def _(eng):
                    replay('dve', eng)

                @block.gpsimd
                def _(eng):
                    replay('pool', eng)

                @block.sync
                def _(eng):
                    replay('sp', eng)


class Pool:
    def __init__(self, kb, name, shape, dtype, n, psum=False):
        self.tiles = []
        for i in range(n):
            nm = '%s%d' % (name, i)
            t = kb.ps(nm, shape, dtype) if psum else kb.sb(nm, shape, dtype)
            self.tiles.append((t, nm))
        self.i = 0

    def get(self):
        t = self.tiles[self.i % len(self.tiles)]
        self.i += 1
        return t


class KB:
    def __init__(self):
        self.nc = bass.Bass("TRN2", target_bir_lowering=False)
        self.S = Sched(self.nc)
        self.es = contextlib.ExitStack()
        self.din = {}
        self.dout = {}

    def inp(self, name, shape, dtype=F32):
        t = self.nc.dram_tensor(name, list(shape), dtype, kind="ExternalInput")
        self.din[name] = t
        return t.ap()

    def outp(self, name, shape, dtype=F32):
        t = self.nc.dram_tensor(name, list(shape), dtype, kind="ExternalOutput")
        self.dout[name] = t
        return t.ap()

    def sb(self, name, shape, dtype):
        return self.es.enter_context(self.nc.sbuf_tensor(name, list(shape), dtype))

    def ps(self, name, shape, dtype):
        return self.es.enter_context(self.nc.psum_tensor(name, list(shape), dtype))

    def dma(self, out, in_, reads=(), writes=(), q='sp', slow=False):
        if slow:
            fn = lambda e: e.dma_start(out=out, in_=in_, allow_slow_non_contiguous=True)
        else:
            fn = lambda e: e.dma_start(out=out, in_=in_)
        return self.S.op(q, fn, reads=reads, writes=writes, dma=True)

    def mm(self, out, lhsT, rhs, start, stop, reads=(), writes=(), signal=None, sgc=False):
        if signal is None:
            signal = stop
        if sgc:
            return self.S.op('pe', lambda e: e.matmul(out, lhsT, rhs, start=start, stop=stop, skip_group_check=True),
                             reads=reads, writes=writes, signal=signal)
        return self.S.op('pe', lambda e: e.matmul(out, lhsT, rhs, start=start, stop=stop),
                         reads=reads, writes=writes, signal=signal)

    def tr(self, out, in_, ident, reads=(), writes=(), signal=True):
        return self.S.op('pe', lambda e: e.transpose(out=out, in_=in_, identity=ident),
                         reads=reads, writes=writes, signal=signal)

    def act(self, out, in_, func, reads=(), writes=(), bias=None, scale=None, accum_out=None):
        kw = {}
        if bias is not None:
            kw['bias'] = bias
        if scale is not None:
            kw['scale'] = scale
        if accum_out is not None:
            kw['accum_out'] = accum_out
        return self.S.op('act', lambda e: e.activation(out=out, in_=in_, func=func, **kw), reads=reads, writes=writes)

    def tt(self, eng, out, in0, in1, op, reads=(), writes=()):
        return self.S.op(eng, lambda e: e.tensor_tensor(out=out, in0=in0, in1=in1, op=op), reads=reads, writes=writes)

    def ts(self, eng, out, in0, s1, s2, op0, op1=None, reads=(), writes=()):
        if op1 is None:
            fn = lambda e: e.tensor_scalar(out=out, in0=in0, scalar1=s1, scalar2=0.0, op0=op0, op1=ALU.add)
        else:
            fn = lambda e: e.tensor_scalar(out=out, in0=in0, scalar1=s1, scalar2=s2, op0=op0, op1=op1)
        return self.S.op(eng, fn, reads=reads, writes=writes)

    def stt(self, eng, out, in0, scalar, in1, op0, op1, reads=(), writes=()):
        return self.S.op(eng, lambda e: e.scalar_tensor_tensor(out=out, in0=in0, scalar=scalar, in1=in1, op0=op0, op1=op1),
                         reads=reads, writes=writes)

    def cp(self, eng, out, in_, reads=(), writes=()):
        if eng == 'act':
            return self.S.op('act', lambda e: e.copy(out=out, in_=in_), reads=reads, writes=writes)
        return self.S.op(eng, lambda e: e.tensor_copy(out=out, in_=in_), reads=reads, writes=writes)

    def memset(self, eng, ap, val, writes=()):
        return self.S.op(eng, lambda e: e.memset(ap, val), writes=writes)

    def rstd(self, out, ss, tmp, inv_n, key):
        self.ts('dve', tmp, ss, inv_n, EPS, ALU.mult, ALU.add, reads=[key], writes=[key])
        self.S.op('act', lambda e: e.sqrt(out=tmp, in_=tmp), reads=[key], writes=[key])
        self.S.op('dve', lambda e: e.reciprocal(out=out, in_=tmp), reads=[key], writes=[key])

    def rsum(self, eng, out, in_, reads=(), writes=()):
        return self.S.op(eng, lambda e: e.reduce_sum(out=out, in_=in_, axis=AX.X), reads=reads, writes=writes)


def bcast_rows(ap_dram_row, nparts):
    t = ap_dram_row
    return bass.AP(tensor=t.tensor, offset=t.offset, ap=[[0, nparts]] + [list(x) for x in t.ap])


def V(ap, off, dims, p0=0, np_=None):
    pstride = ap.ap[0][0]
    npart = ap.ap[0][1] if np_ is None else np_
    return bass.AP(tensor=ap.tensor, offset=ap.offset + p0 * pstride + off,
                   ap=[[pstride, npart]] + [list(d) for d in dims])


class Arena:
    def __init__(self, kb, name, nelem, dtype):
        self.t = kb.sb(name, [128, nelem], dtype)
        self.n = nelem
        self.off = 0
        self.name = name
        self.cnt = 0

    def reset(self):
        self.off = 0

    def alloc(self, shape):
        n = 1
        for d in shape[1:]:
            n *= d
        a = self.off
        self.off = (a + n + 31) // 32 * 32
        assert self.off <= self.n, 'arena %s overflow: %d > %d' % (self.name, self.off, self.n)
        v = self.t[0:shape[0], a:a + n]
        if len(shape) > 2:
            names = 'abcdef'[:len(shape) - 1]
            pat = 'p (%s) -> p %s' % (' '.join(names), ' '.join(names))
            kw = {names[i]: shape[1 + i] for i in range(1, len(names))}
            v = v.rearrange(pat, **kw)
        self.cnt += 1
        return v, '%s_%d' % (self.name, self.cnt)

    def pool(self, shape, n):
        return APPool([self.alloc(shape) for _ in range(n)])


class APPool:
    def __init__(self, tiles):
        self.tiles = tiles
        self.i = 0

    def get(self):
        t = self.tiles[self.i % len(self.tiles)]
        self.i += 1
        return t

import math
import os

SL = 2048
NTOK = 1024 + SL
PI = math.pi
STAGE = int(os.environ.get('KSTAGE', '99'))


def dram_ap(t, off, dims):
    return bass.AP(tensor=t.tensor, offset=t.offset + off, ap=[list(d) for d in dims])


def build():
    kb = KB()
    nc = kb.nc
    S = kb.S
    xp = kb.inp('xp', [1024, D])
    xs = kb.inp('xs', [SL, D])
    cvec = kb.inp('cvec', [2, D])
    ck_d = kb.inp('ck', [512, 128])
    cv_d = kb.inp('cv', [512, 128])
    h0_d = kb.inp('h0', [2 * 2 * 32 * 64])
    ada_w = kb.inp('ada_w', [2, D, 3 * D])
    ada_b = kb.inp('ada_b', [2, 3 * D])
    norm_pre = kb.inp('norm_pre', [2, D])
    norm_post = kb.inp('norm_post', [2, D])
    ab_w_in = kb.inp('ab_w_in', [D, 2304])
    q_gain = kb.inp('q_gain', [64])
    k_gain = kb.inp('k_gain', [64])
    lam_re = kb.inp('lam_re', [4096])
    lam_im = kb.inp('lam_im', [4096])
    log_dt = kb.inp('log_dt', [64])
    b_re = kb.inp('b_re', [65536])
    b_im = kb.inp('b_im', [65536])
    c_re = kb.inp('c_re', [65536])
    c_im = kb.inp('c_im', [65536])
    ssm_d = kb.inp('ssm_d', [512])
    glu_w = kb.inp('glu_w', [512, 512])
    glu_b = kb.inp('glu_b', [512])
    ab_w_out = kb.inp('ab_w_out', [1024, 1024])
    ident_d = kb.inp('ident', [128, 128], BF16)
    identf_d = kb.inp('identf', [128, 128])
    maskf_d = kb.inp('maskf', [128, 128])
    maskb_d = kb.inp('maskb', [128, 128])
    rope_d = kb.inp('rope', [SL, 64])
    cd_w_in = kb.inp('cd_w_in', [D, 2560])
    v_gain = kb.inp('v_gain', [512])
    w_s = kb.inp('w_s', [8, 128, 128])
    b_s = kb.inp('b_s', [8, 128])
    fnet_w = kb.inp('fnet_w', [512, 512])
    cd_w_out = kb.inp('cd_w_out', [1024, 1024])
    c128_d = kb.inp('c128', [128, 128], BF16)
    s128_d = kb.inp('s128', [128, 128], BF16)
    clp_d = kb.inp('clp', [256, 256], BF16)
    slp_d = kb.inp('slp', [256, 256], BF16)
    cls_d = kb.inp('cls', [SL, SL], BF16)
    sls_d = kb.inp('sls', [SL, SL], BF16)
    yp = kb.outp('yp', [1024, D])
    ys = kb.outp('ys', [SL, D])
    nk = kb.outp('nk', [1024, 128])
    nv = kb.outp('nv', [1024, 128])
    ns = kb.outp('ns', [4 * 8192])
    MODS = nc.dram_tensor('MODS', [2, 2, 3, D], F32).ap()
    WBd = nc.dram_tensor('WBd', [128, 8192], BF16).ap()
    WCd = nc.dram_tensor('WCd', [128, 8192], BF16).ap()
    KMd = nc.dram_tensor('KMd', [128, 4096], BF16).ap()
    XSd = nc.dram_tensor('XSd', [3, 128, 4096], BF16).ap()
    MIXAd = nc.dram_tensor('MIXAd', [12, 64, 2048], BF16).ap()
    GBd = nc.dram_tensor('GBd', [12, 128, 1024], BF16).ap()
    X1 = nc.dram_tensor('X1', [NTOK, D], F32).ap()
    HNTd = nc.dram_tensor('HNTd', [3, 128, 8192], BF16).ap()
    ab_w_in_b = nc.dram_tensor('ab_w_in_b', [D, 2304], BF16).ap()
    ab_w_out_b = nc.dram_tensor('ab_w_out_b', [1024, 1024], BF16).ap()
    glu_w_b = nc.dram_tensor('glu_w_b', [512, 512], BF16).ap()
    cd_w_in_b = nc.dram_tensor('cd_w_in_b', [D, 2560], BF16).ap()
    cd_w_out_b = nc.dram_tensor('cd_w_out_b', [1024, 1024], BF16).ap()
    fnet_w_b = nc.dram_tensor('fnet_w_b', [512, 512], BF16).ap()
    w_s_b = nc.dram_tensor('w_s_b', [8, 128, 128], BF16).ap()
    MIXCd = nc.dram_tensor('MIXCd', [12, 128, 1024], BF16).ap()
    GDd = nc.dram_tensor('GDd', [12, 128, 1024], BF16).ap()

    with kb.es:
        sb = kb.sb
        ident = sb('ident_sb', [128, 128], BF16)[:]
        kb.dma(ident, ident_d, writes=['ident'])
        identf = sb('identf_sb', [128, 128], F32)[:]
        kb.dma(identf, identf_d, writes=['identf'])
        kg = sb('kg', [128, 64], F32)[:]
        kb.dma(kg, bcast_rows(k_gain, 128), writes=['kg'])
        qg = sb('qg', [128, 64], F32)[:]
        kb.dma(qg, bcast_rows(q_gain, 128), writes=['qg'])
        kb.ts('dve', qg, qg, 0.125, None, ALU.mult, reads=['qg'], writes=['qg'])
        ones_bf = sb('ones_bf', [128, 64], BF16)[:]
        kb.memset('dve', ones_bf, 1.0, writes=['ones_bf'])
        SHIFT = sb('SHIFT', [128, D], F32)[:]
        G1 = sb('G1', [128, D], F32)[:]
        G2 = sb('G2', [128, D], F32)[:]
        LLt = sb('LLt', [128, 64], F32)[:]
        LXt = sb('LXt', [128, 64], F32)[:]
        HL = sb('HL', [128, 2 * 32 * 257], BF16)[:]
        KT = sb('KT', [128, 2 * 2560], BF16)[:]
        VA = sb('VA', [128, 20 * 256], BF16)[:]
        AF_ = Arena(kb, 'AF', 12800, F32)
        AB_ = Arena(kb, 'AB', 46080, BF16)

        pacc = Pool(kb, 'pacc', [128, 512], F32, 6, psum=True)
        ptr = Pool(kb, 'ptr', [128, 1024], BF16, 2, psum=True)

        def new_phase():
            S.barrier()
            AF_.reset()
            AB_.reset()

        def load_T(dst, dstn, src_rows, n):
            st, stn = AF_.alloc([n, 128])
            kb.dma(st, src_rows, writes=[stn])
            pt, pn = pacc.get()
            kb.tr(pt[:, 0:n], st, identf[0:n, 0:n], reads=[stn, 'identf'], writes=[pn])
            kb.cp('act', dst, pt[:, 0:n], reads=[pn], writes=[dstn])

        condT, cTn = AF_.alloc([128, 8, 2])
        cst, cstn = AF_.alloc([2, D])
        kb.dma(cst, cvec, writes=[cstn])
        for k in range(8):
            pt, pn = pacc.get()
            kb.tr(pt[:, 0:2], cst[:, k * 128:(k + 1) * 128], identf[0:2, 0:2], reads=[cstn, 'identf'], writes=[pn])
            kb.cp('act', condT[:, k, :], pt[:, 0:2], reads=[pn], writes=[cTn])
        kb.act(condT, condT, AF.Silu, reads=[cTn], writes=[cTn])
        modkeys = []
        adaw_pool = AF_.pool([128, 8, 512], 2)
        mb_pool = AF_.pool([2, 512], 2)
        ab_pool = AF_.pool([2, 512], 2)
        np_pool = AF_.pool([2, 512], 2)
        for l in range(2):
            wv = ada_w[l].rearrange("(k p) n -> p k n", p=128)
            for cb in range(6):
                kind, hh = cb // 2, cb % 2
                wt, wn = adaw_pool.get()
                kb.dma(wt, wv[:, :, cb * 512:(cb + 1) * 512], writes=[wn])
                abt, abn = ab_pool.get()
                kb.dma(abt, bcast_rows(ada_b[l, cb * 512:(cb + 1) * 512], 2), writes=[abn])
                pt, pn = pacc.get()
                for k in range(8):
                    kb.mm(pt[0:2, :], condT[:, k, :], wt[:, k, :], start=(k == 0), stop=(k == 7),
                          reads=[cTn, wn], writes=[pn], signal=True)
                mb, mbn_ = mb_pool.get()
                kb.tt('dve', mb, pt[0:2, :], abt, ALU.add, reads=[pn, abn], writes=[mbn_])
                if kind > 0:
                    npt, npn = np_pool.get()
                    src = norm_pre if kind == 1 else norm_post
                    kb.dma(npt, bcast_rows(src[l, hh * 512:(hh + 1) * 512], 2), writes=[npn])
                    if kind == 1:
                        kb.stt('dve', mb, mb, 1.0, npt, ALU.add, ALU.mult, reads=[mbn_, npn], writes=[mbn_])
                    else:
                        kb.tt('dve', mb, mb, npt, ALU.mult, reads=[mbn_, npn], writes=[mbn_])
                mk_ = 'MODS_%d_%d' % (l, cb)
                modkeys.append(mk_)
                kb.dma(MODS[l, :, kind, hh * 512:(hh + 1) * 512], mb, reads=[mbn_], writes=[mk_], q='act')

        wkeys = {}

        def convert_w(name, src_w, dst_w, rows, after):
            ks = []
            for r in range(0, rows, 128):
                k_ = 'wc_%s_%d' % (name, r)
                kb.dma(dst_w[r:r + 128, :], src_w[r:r + 128, :], reads=after, writes=[k_], q='pool')
                ks.append(k_)
            wkeys[name] = ks


        def convert_late():
            convert_w('cd_in', cd_w_in, cd_w_in_b, 1024, [])
            convert_w('cd_out', cd_w_out, cd_w_out_b, 1024, [])
            convert_w('fnet', fnet_w, fnet_w_b, 512, [])
            convert_w('ws', w_s.rearrange("h t s -> (h t) s"), w_s_b.rearrange("h t s -> (h t) s"), 1024, [])

        def load_mod(l, ci):
            kb.dma(SHIFT, bcast_rows(MODS[l, ci, 0, :], 128), reads=modkeys, writes=['SHIFT'])
            kb.dma(G1, bcast_rows(MODS[l, ci, 1, :], 128), reads=modkeys, writes=['G1'])
            kb.dma(G2, bcast_rows(MODS[l, ci, 2, :], 128), reads=modkeys, writes=['G2'])

        new_phase()
        convert_w('ab_in', ab_w_in, ab_w_in_b, 1024, [])
        convert_w('ab_out', ab_w_out, ab_w_out_b, 1024, [])
        convert_w('glu', glu_w, glu_w_b, 512, [])

        def cmul(eng, o_r, o_i, a_r, a_i, b_r, b_i, t1, t2, rk, wk, tk, neg_im=False, t34=None):
            if t34 is not None:
                t3, t4 = t34
                k1, k2, k3, k4 = tk + '_1', tk + '_2', tk + '_3', tk + '_4'
                kb.tt(eng, t1, a_r, b_r, ALU.mult, reads=rk, writes=[k1])
                kb.tt(eng, t2, a_i, b_i, ALU.mult, reads=rk, writes=[k2])
                kb.tt(eng, t3, a_r, b_i, ALU.mult, reads=rk, writes=[k3])
                kb.tt(eng, t4, a_i, b_r, ALU.mult, reads=rk, writes=[k4])
                kb.tt(eng, o_r, t1, t2, ALU.subtract, reads=[k1, k2], writes=wk)
                kb.tt(eng, o_i, t3, t4, ALU.add, reads=[k3, k4], writes=wk)
                return
            kb.tt(eng, t1, a_r, b_r, ALU.mult, reads=rk, writes=[tk])
            kb.tt(eng, t2, a_i, b_i, ALU.mult, reads=rk, writes=[tk])
            kb.tt(eng, o_r, t1, t2, ALU.subtract, reads=[tk], writes=wk)
            kb.tt(eng, t1, a_r, b_i, ALU.mult, reads=rk + wk, writes=[tk])
            kb.tt(eng, t2, a_i, b_r, ALU.mult, reads=rk + wk, writes=[tk])
            if neg_im:
                kb.tt(eng, t1, t1, t2, ALU.add, reads=[tk], writes=[tk])
                kb.ts(eng, o_i, t1, -1.0, None, ALU.mult, reads=[tk], writes=wk)
            else:
                kb.tt(eng, o_i, t1, t2, ALU.add, reads=[tk], writes=wk)

        def sm(n=32):
            return AF_.alloc([128, n])

        LR, LRn = sm()
        LI, LIn = sm()
        LD, LDn = sm()
        load_T(LR, LRn, dram_ap(lam_re, 0, [[128, 32], [1, 128]]), 32)
        load_T(LI, LIn, dram_ap(lam_im, 0, [[128, 32], [1, 128]]), 32)
        ld0, ld0n = AF_.alloc([32, 2])
        kb.dma(ld0, dram_ap(log_dt, 0, [[2, 32], [1, 2]]), writes=[ld0n])
        ld1, ld1n = AF_.alloc([32, 128])
        kb.cp('dve', ld1.rearrange("p (g n) -> p g n", g=2), V(ld0, 0, [[1, 2], [0, 64]]), reads=[ld0n], writes=[ld1n])
        ptl, ptln = pacc.get()
        kb.tr(ptl[:, 0:32], ld1, identf[0:32, 0:32], reads=[ld1n, 'identf'], writes=[ptln])
        kb.cp('act', LD, ptl[:, 0:32], reads=[ptln], writes=[LDn])
        TK = 'tblsmall'
        dt_, _ = sm(); a_, _ = sm(); th, _ = sm(); mag, _ = sm(); arg, _ = sm(); sn, _ = sm(); cs, _ = sm()
        lbr, _ = sm(); lbi, _ = sm(); t1, _ = sm(); t2, _ = sm(); nre, _ = sm(); den, _ = sm()
        kr, _ = sm(); ki, _ = sm(); ibr, _ = sm(); ibi, _ = sm()
        R = [TK, LRn, LIn, LDn]
        W = [TK]
        kb.act(dt_, LD, AF.Exp, reads=R, writes=W)
        kb.tt('dve', a_, LR, dt_, ALU.mult, reads=R, writes=W)
        kb.tt('dve', th, LI, dt_, ALU.mult, reads=R, writes=W)
        kb.act(mag, a_, AF.Exp, reads=R, writes=W)
        kb.ts('dve', arg, th, 1.0 / 16, None, ALU.mult, reads=R, writes=W)
        kb.act(sn, arg, AF.Sin, reads=R, writes=W)
        kb.ts('dve', arg, arg, PI / 2, None, ALU.add, reads=R, writes=W)
        kb.act(cs, arg, AF.Sin, reads=R, writes=W)
        for _it in range(4):
            kb.tt('dve', t1, cs, cs, ALU.mult, reads=R, writes=W)
            kb.tt('dve', t2, sn, sn, ALU.mult, reads=R, writes=W)
            kb.tt('dve', sn, cs, sn, ALU.mult, reads=R, writes=W)
            kb.ts('dve', sn, sn, 2.0, None, ALU.mult, reads=R, writes=W)
            kb.tt('dve', cs, t1, t2, ALU.subtract, reads=R, writes=W)
        kb.tt('dve', lbr, mag, cs, ALU.mult, reads=R, writes=W)
        kb.tt('dve', lbi, mag, sn, ALU.mult, reads=R, writes=W)
        kb.ts('dve', nre, lbr, -1.0, None, ALU.add, reads=R, writes=W)
        kb.tt('dve', t1, LR, LR, ALU.mult, reads=R, writes=W)
        kb.tt('dve', t2, LI, LI, ALU.mult, reads=R, writes=W)
        kb.tt('dve', den, t1, t2, ALU.add, reads=R, writes=W)
        S.op('dve', lambda e: e.reciprocal(out=den, in_=den), reads=R, writes=W)
        kb.tt('dve', t1, nre, LR, ALU.mult, reads=R, writes=W)
        kb.tt('dve', t2, lbi, LI, ALU.mult, reads=R, writes=W)
        kb.tt('dve', t1, t1, t2, ALU.add, reads=R, writes=W)
        kb.tt('dve', kr, t1, den, ALU.mult, reads=R, writes=W)
        kb.tt('dve', t1, lbi, LR, ALU.mult, reads=R, writes=W)
        kb.tt('dve', t2, nre, LI, ALU.mult, reads=R, writes=W)
        kb.tt('dve', t1, t1, t2, ALU.subtract, reads=R, writes=W)
        kb.tt('dve', ki, t1, den, ALU.mult, reads=R, writes=W)
        kb.tt('dve', t1, lbr, lbr, ALU.mult, reads=R, writes=W)
        kb.tt('dve', t2, lbi, lbi, ALU.mult, reads=R, writes=W)
        kb.tt('dve', t1, t1, t2, ALU.add, reads=R, writes=W)
        S.op('dve', lambda e: e.reciprocal(out=t1, in_=t1), reads=R, writes=W)
        kb.tt('dve', ibr, lbr, t1, ALU.mult, reads=R, writes=W)
        kb.stt('dve', ibi, lbi, -1.0, t1, ALU.mult, ALU.mult, reads=R, writes=W)
        PR, _ = AF_.alloc([128, 9, 32]); PIm, _ = AF_.alloc([128, 9, 32])
        QR, _ = AF_.alloc([128, 8, 32]); QI, _ = AF_.alloc([128, 8, 32])
        kb.memset('dve', PR[:, 0, :], 1.0, writes=W)
        kb.memset('dve', PIm[:, 0, :], 0.0, writes=W)
        kb.memset('dve', QR[:, 0, :], 1.0, writes=W)
        kb.memset('dve', QI[:, 0, :], 0.0, writes=W)
        w1, _ = AF_.alloc([128, 4, 32]); w2, _ = AF_.alloc([128, 4, 32]); w3, _ = AF_.alloc([128, 4, 32]); w4, _ = AF_.alloc([128, 4, 32])

        def pw_double(XR, XI, base_r, base_i, nmax):
            kb.cp('dve', XR[:, 1, :], base_r, reads=[TK], writes=['PW'])
            kb.cp('dve', XI[:, 1, :], base_i, reads=[TK], writes=['PW'])
            steps = [(2, 1, 1), (3, 2, 2), (5, 4, nmax - 4)]
            for (d0, mi, n) in steps:
                br_ = V(XR, mi * 32, [[0, n], [1, 32]])
                bi_ = V(XI, mi * 32, [[0, n], [1, 32]])
                cmul('dve', XR[:, d0:d0 + n, :], XI[:, d0:d0 + n, :], XR[:, 1:1 + n, :], XI[:, 1:1 + n, :], br_, bi_,
                     w1[:, 0:n, :], w2[:, 0:n, :], ['PW'], ['PW'], 'pwt', t34=(w3[:, 0:n, :], w4[:, 0:n, :]))

        pw_double(PR, PIm, lbr, lbi, 8)
        pw_double(QR, QI, ibr, ibi, 7)
        kb.cp('dve', t1, t1, reads=['PW', TK], writes=[TK])
        LL3 = LLt.rearrange("p (a c) -> p a c", c=2)
        LX3 = LXt.rearrange("p (a c) -> p a c", c=2)
        kb.cp('dve', LL3[:, :, 0], PR[:, 8, :], reads=[TK], writes=['LLt'])
        kb.cp('dve', LL3[:, :, 1], PR[:, 8, :], reads=[TK], writes=['LLt'])
        kb.ts('dve', LX3[:, :, 0], PIm[:, 8, :], -1.0, None, ALU.mult, reads=[TK], writes=['LXt'])
        kb.cp('dve', LX3[:, :, 1], PIm[:, 8, :], reads=[TK], writes=['LXt'])
        BR, BRn = AF_.alloc([128, 32, 16]); BI, BIn = AF_.alloc([128, 32, 16])
        CR, CRn = AF_.alloc([128, 32, 16]); CI, CIn = AF_.alloc([128, 32, 16])
        pt1, _ = AF_.alloc([128, 16, 8, 16]); pt2, _ = AF_.alloc([128, 16, 8, 16])
        bst, bstn = V(pt1, 0, [[1, 2048]], np_=32), 'tbtmp'
        for (src_b, dstB, dstBn) in ((b_re, BR, BRn), (b_im, BI, BIn)):
            kb.dma(bst, dram_ap(src_b, 0, [[2048, 32], [1, 2048]]), writes=[bstn])
            pt, pn = pacc.get()
            for h in range(16):
                kb.tr(pt[:, h * 32:(h + 1) * 32], V(bst, h, [[16, 128]]), identf[0:32, 0:32], reads=[bstn, 'identf'], writes=[pn], signal=(h == 15))
            kb.cp('act', V(dstB, 0, [[1, 16], [16, 32]]), pt[:, :].rearrange("p (h a) -> p h a", h=16), reads=[pn], writes=[dstBn])
        for (src_c, dstC, dstn) in ((c_re, CR, CRn), (c_im, CI, CIn)):
            Cn, Cnn = AF_.alloc([128, 4, 128])
            for d in range(2):
                for gqb in range(2):
                    for gql in range(8):
                        kb.dma(Cn[gql * 16:(gql + 1) * 16, d * 2 + gqb, :],
                               dram_ap(src_c, d * 32768 + (gqb * 8 + gql) * 2048, [[64, 16], [1024, 2], [1, 64]]),
                               writes=['%s_%d_%d' % (Cnn, d * 2 + gqb, gql)])
            for blk in range(4):
                pt, pn = pacc.get()
                kb.tr(pt[:, 0:128], Cn[:, blk, :], identf, reads=['%s_%d_%d' % (Cnn, blk, q_) for q_ in range(8)] + ['identf'], writes=[pn])
                kb.cp('act', dstC.rearrange("p a h -> p (a h)")[:, blk * 128:(blk + 1) * 128], pt[:, 0:128], reads=[pn], writes=[dstn])
        BBR, _ = AF_.alloc([128, 32, 16]); BBI, _ = AF_.alloc([128, 32, 16])
        tb1 = V(pt2, 0, [[16, 32], [1, 16]])
        tb2 = V(pt2, 512, [[16, 32], [1, 16]])
        bc = lambda t, off, n: V(t, off, [[1, n], [0, 16]])
        cmul('dve', BBR, BBI, bc(kr, 0, 32), bc(ki, 0, 32), BR, BI, tb1, tb2, [TK, BRn, BIn], [TK], 'tbtmp')
        RaR, _ = AF_.alloc([128, 8, 32]); RaI, _ = AF_.alloc([128, 8, 32])
        RbR, _ = AF_.alloc([128, 8, 32]); RbI, _ = AF_.alloc([128, 8, 32])
        for k in range(8):
            kb.cp('dve', RaR[:, k, :], PR[:, 7 - k, :], reads=[TK], writes=['Rrev'])
            kb.cp('dve', RaI[:, k, :], PIm[:, 7 - k, :], reads=[TK], writes=['Rrev'])
            kb.cp('dve', RbR[:, k, :], PR[:, 8 - k, :], reads=[TK], writes=['Rrev'])
            kb.cp('dve', RbI[:, k, :], PIm[:, 8 - k, :], reads=[TK], writes=['Rrev'])
        TB, TBn = AB_.alloc([128, 2, 2, 16, 8, 16])
        AFt, AFn = AB_.alloc([128, 2, 16, 8, 16])
        BMF, BMFn = AB_.alloc([128, 2, 16, 8, 16])
        BMB, BMBn = AB_.alloc([128, 2, 16, 8, 16])
        WC, WCn = AB_.alloc([128, 2, 16, 2, 8, 16])

        def prod(o_r, o_i, pw_r, pw_i, koff, d, x_r, x_i, wkey, neg_im):
            pr = V(pw_r, koff * 32 + d * 16, [[1, 16], [32, 8], [0, 16]])
            pi_ = V(pw_i, koff * 32 + d * 16, [[1, 16], [32, 8], [0, 16]])
            xr = V(x_r, d * 256, [[16, 16], [0, 8], [1, 16]])
            xi = V(x_i, d * 256, [[16, 16], [0, 8], [1, 16]])
            cmul('dve', o_r, o_i, pr, pi_, xr, xi, pt1, pt2, [TK, CRn, CIn, 'Rrev'], [wkey], 'tbtmp', neg_im=neg_im)

        prod(TB[:, 0, 0], TB[:, 0, 1], RaR, RaI, 0, 0, BBR, BBI, TBn, False)
        prod(TB[:, 1, 0], TB[:, 1, 1], PR, PIm, 0, 1, BBR, BBI, TBn, False)
        prod(AFt[:, 0], AFt[:, 1], QR, QI, 0, 0, BBR, BBI, AFn, False)
        prod(BMF[:, 0], BMF[:, 1], PR, PIm, 0, 0, CR, CI, BMFn, True)
        prod(BMB[:, 0], BMB[:, 1], QR, QI, 0, 1, CR, CI, BMBn, True)
        prod(WC[:, 0, :, 0], WC[:, 0, :, 1], PR, PIm, 1, 0, CR, CI, WCn, True)
        prod(WC[:, 1, :, 0], WC[:, 1, :, 1], RbR, RbI, 0, 1, CR, CI, WCn, True)
        kb.dma(WCd, WC.rearrange("p d q c t h -> p (d q c t h)"), reads=[WCn], writes=['WCd'])
        WBt, WBn = AB_.alloc([128, 2, 16, 2, 128])
        for d in range(2):
            for qb in range(4):
                pt, pn = ptr.get()
                for qi in range(4):
                    gq = qb * 4 + qi
                    for c in range(2):
                        kb.tr(pt[:, (qi * 2 + c) * 128:(qi * 2 + c + 1) * 128],
                              TB[:, d, c, gq, :, :].rearrange("p s h -> p (s h)"), ident,
                              reads=[TBn, 'ident'], writes=[pn], signal=(qi == 3 and c == 1))
                kb.cp('act', WBt[:, d, qb * 4:(qb + 1) * 4, :, :].rearrange("p q c n -> p (q c n)"), pt[:, :],
                      reads=[pn], writes=[WBn])
        kb.dma(WBd, WBt.rearrange("p d q c n -> p (d q c n)"), reads=[WBn], writes=['WBd'])
        maskf, mfn = AF_.alloc([128, 128]); maskb, mbn = AF_.alloc([128, 128])
        kb.dma(maskf, maskf_d, writes=[mfn])
        kb.dma(maskb, maskb_d, writes=[mbn])
        dcol, dcn = AF_.alloc([128, 32])
        dc0, dc0n = AF_.alloc([32, 16])
        kb.dma(dc0, dram_ap(ssm_d, 0, [[16, 32], [1, 16]]), writes=[dc0n])
        dc1, dc1n = AF_.alloc([32, 128])
        kb.cp('dve', dc1.rearrange("p (s h) -> p s h", s=8), V(dc0, 0, [[0, 8], [1, 16]]), reads=[dc0n], writes=[dc1n])
        ptd, ptdn = pacc.get()
        kb.tr(ptd[:, 0:32], dc1, identf[0:32, 0:32], reads=[dc1n, 'identf'], writes=[ptdn])
        kb.cp('act', dcol, ptd[:, 0:32], reads=[ptdn], writes=[dcn])
        KM, KMn = AB_.alloc([128, 32, 128])
        km1, km1n = AF_.alloc([128, 128]); km2, km2n = AF_.alloc([128, 128])
        for g in range(32):
            gq, gh = g // 2, g % 2
            hs = slice(gh * 64, (gh + 1) * 64)
            pt, pn = pacc.get()
            fl = lambda t: t.rearrange("p s h -> p (s h)")
            kb.mm(pt[:, 0:128], fl(AFt[hs, 0, gq]), fl(BMF[hs, 0, gq]), True, False, reads=[AFn, BMFn], writes=[pn], signal=False)
            kb.mm(pt[:, 0:128], fl(AFt[hs, 1, gq]), fl(BMF[hs, 1, gq]), False, True, reads=[AFn, BMFn], writes=[pn], signal=False)
            kb.mm(pt[:, 128:256], fl(TB[hs, 1, 0, gq]), fl(BMB[hs, 0, gq]), True, False, reads=[TBn, BMBn], writes=[pn], signal=False)
            kb.mm(pt[:, 128:256], fl(TB[hs, 1, 1, gq]), fl(BMB[hs, 1, gq]), False, True, reads=[TBn, BMBn], writes=[pn], signal=True)
            kb.tt('dve', km1, pt[:, 0:128], maskf, ALU.mult, reads=[pn, mfn], writes=[km1n])
            kb.tt('dve', km2, pt[:, 128:256], maskb, ALU.mult, reads=[pn, mbn], writes=[km2n])
            kb.tt('dve', km1, km1, km2, ALU.add, reads=[km1n, km2n], writes=[km1n])
            kb.stt('dve', KM[:, g, :], identf, dcol[:, g:g + 1], km1, ALU.mult, ALU.add, reads=['identf', dcn, km1n], writes=[KMn])
        kb.dma(KMd, KM.rearrange("p g n -> p (g n)"), reads=[KMn], writes=['KMd'])

        HL4 = HL.rearrange("p (d a n) -> p d a n", d=2, a=32)

        def hn_tile(xrows, hnT, hnTn, col0, pools):
            xt_pool, junk_pool, hn_pool, sm_pool = pools
            xt, xn = xt_pool.get()
            kb.dma(xt, xrows, writes=[xn])
            jk, jn = junk_pool.get()
            smt, sn_ = sm_pool.get()
            kb.memset('dve', smt[:, 0:1], 0.0, writes=[sn_])
            kb.act(jk, xt, AF.Square, reads=[xn, sn_], writes=[jn, sn_], accum_out=smt[:, 0:1])
            kb.rstd(smt[:, 2:3], smt[:, 0:1], smt[:, 1:2], 1.0 / D, sn_)
            kb.stt('dve', xt, xt, smt[:, 2:3], G1, ALU.mult, ALU.mult, reads=[xn, sn_, 'G1'], writes=[xn])
            hn, hnn = hn_pool.get()
            kb.tt('dve', hn, xt, SHIFT, ALU.add, reads=[xn, 'SHIFT'], writes=[hnn])
            pt, pn = ptr.get()
            for k in range(8):
                kb.tr(pt[:, k * 128:(k + 1) * 128], hn[:, k * 128:(k + 1) * 128], ident,
                      reads=[hnn, 'ident'], writes=[pn], signal=(k == 7))
            kb.cp('act', hnT[:, :, col0:col0 + 128], pt[:, :].rearrange("p (k t) -> p k t", k=8),
                  reads=[pn], writes=[hnTn])

        def rope(eng, out, x, nh, cs_tile, t1_, t2_, rk, wk, tk):
            def xv(t, half):
                return V(t, half * 16, [[64, nh], [32, 2], [1, 16]])
            cosv = V(cs_tile, 0, [[0, nh], [16, 2], [1, 16]])
            sinv = V(cs_tile, 32, [[0, nh], [16, 2], [1, 16]])
            a = V(t1_, 0, [[32, nh], [16, 2], [1, 16]])
            b = V(t2_, 0, [[32, nh], [16, 2], [1, 16]])
            kb.tt(eng, a, xv(x, 0), cosv, ALU.mult, reads=rk, writes=[tk])
            kb.tt(eng, b, xv(x, 1), sinv, ALU.mult, reads=rk, writes=[tk])
            kb.tt(eng, xv(out, 0), a, b, ALU.subtract, reads=[tk], writes=wk)
            kb.tt(eng, a, xv(x, 0), sinv, ALU.mult, reads=rk + wk, writes=[tk])
            kb.tt(eng, b, xv(x, 1), cosv, ALU.mult, reads=rk + wk, writes=[tk])
            kb.tt(eng, xv(out, 1), a, b, ALU.add, reads=[tk], writes=wk)

        KT3 = KT.rearrange("p (h n) -> p h n", h=2)
        kb.memset('pool', KT[64:128, :], 0.0, writes=['KT'])
        VO = VA.rearrange("p (t h n) -> p t h n", h=2, n=128)
        kb.memset('pool', VO[:, :, :, 64:128], 1.0, writes=['VA'])

        def phase_A1(sus, is_sample, hook=None):
            new_phase()
            if hook is not None:
                hook()
            ci = 1 if is_sample else 0
            load_mod(0, ci)
            WA, WAn = AB_.alloc([128, 8, 768])
            S.groups[WAn] = [WAn + '#0', WAn + '#1']
            w0v = ab_w_in_b.rearrange("(k p) n -> p k n", p=128)
            kb.dma(WA[:, :, 0:256], w0v[:, :, 512:768], reads=wkeys['ab_in'], writes=[WAn + '#0'])
            kb.dma(WA[:, :, 256:768], w0v[:, :, 1280:1792], reads=wkeys['ab_in'], writes=[WAn + '#1'], q='act')
            WB, WBn_ = AB_.alloc([128, 2, 16, 2, 128])
            kb.dma(WB.rearrange("p d q c n -> p (d q c n)"), WBd, reads=['WBd'], writes=[WBn_])
            hnT, hnTn = AB_.alloc([128, 8, 1024])
            Ub, Ubn = AB_.alloc([128, 32, 8, 16])
            X, Xn = AB_.alloc([128, 32, 128])
            pools = (AF_.pool([128, D], 4), AB_.pool([128, D], 2), AB_.pool([128, D], 2), AF_.pool([128, 8], 4))
            sq_pool = AF_.pool([128, 128], 3)
            kvf_pool = AF_.pool([128, 256], 3)
            kb_pool = AB_.pool([128, 128], 8)
            rp_pool = AF_.pool([128, 64], 2)
            rt_pool = AF_.pool([128, 64], 2)
            sm_pool = pools[3]
            for su in sus:
                xsrc = xs if is_sample else xp
                row0 = (su - 1) * 1024 if is_sample else 0
                deferred = []
                for t in range(8):
                    hn_tile(xsrc[row0 + t * 128: row0 + (t + 1) * 128, :], hnT, hnTn, t * 128, pools)
                kb.dma(HNTd[su], hnT.rearrange("p k n -> p (k n)"), reads=[hnTn], writes=['HNTd%d' % su], q='act')
                for t in range(8):
                    r0 = row0 + t * 128
                    pt, pn = pacc.get()
                    for k in range(8):
                        kb.mm(pt[:, 0:256], hnT[:, k, t * 128:(t + 1) * 128], WA[:, k, 0:256], start=(k == 0), stop=(k == 7),
                              reads=[hnTn, WAn], writes=[pn])
                    sq, sqn = sq_pool.get()
                    smt, sn_ = sm_pool.get()
                    kb.act(sq, pt[:, 0:128], AF.Square, reads=[pn], writes=[sqn])
                    kb.rsum('dve', smt[:, 0:2], sq.rearrange("p (h d) -> p h d", h=2), reads=[sqn], writes=[sn_])
                    kb.rstd(smt[:, 4:6], smt[:, 0:2], smt[:, 2:4], 1.0 / 64, sn_)
                    kvf, kvn = kvf_pool.get()
                    for h in range(2):
                        kb.stt('dve', kvf[:, h * 64:(h + 1) * 64], pt[:, h * 64:(h + 1) * 64], smt[:, 4 + h:5 + h], kg,
                               ALU.mult, ALU.mult, reads=[pn, sn_, 'kg'], writes=[kvn])
                    kb.cp('act', kvf[:, 128:256], pt[:, 128:256], reads=[pn], writes=[kvn])
                    kbt, kbn = kb_pool.get()
                    if is_sample:
                        rp, rpn = rp_pool.get()
                        kb.dma(rp, rope_d[r0:r0 + 128, :], writes=[rpn])
                        rt, rtn = rt_pool.get()
                        ra, ran = rt_pool.get()
                        rope('dve', kbt, kvf[:, 0:128], 2, rp, rt, ra, [kvn, rpn], [kbn], rtn)
                        key0 = 512 + r0
                    else:
                        kb.dma(nk[r0:r0 + 128, :], kvf[:, 0:128], reads=[kvn], q='act')
                        kb.dma(nv[r0:r0 + 128, :], kvf[:, 128:256], reads=[kvn], q='act')
                        kb.cp('act', kbt, kvf[:, 0:128], reads=[kvn], writes=[kbn])
                        key0 = r0
                    deferred.append((kbt, kbn, key0))
                    kb.cp('dve', VO[:, key0 // 128, :, 0:64], kvf[:, 128:256].rearrange("p (h d) -> p h d", h=2), reads=[kvn], writes=['VA'])
                for s_ in range(8):
                    pt, pn = pacc.get()
                    for k in range(8):
                        lhs = V(hnT, k * 1024 + s_, [[8, 128]])
                        kb.mm(pt[:, :], lhs, WA[:, k, 256:768], start=(k == 0), stop=(k == 7), reads=[hnTn, WAn], writes=[pn])
                    eng = 'act' if s_ % 2 == 0 else 'dve'
                    kb.cp(eng, Ub[:, :, s_, :], pt[:, :].rearrange("p (g h) -> p g h", g=32), reads=[pn], writes=[Ubn])
                for (kbt, kbn, key0) in deferred:
                    ptt, ptn = ptr.get()
                    for h in range(2):
                        kb.tr(ptt[0:64, h * 128:(h + 1) * 128], kbt[:, h * 64:(h + 1) * 64], ident,
                              reads=[kbn, 'ident'], writes=[ptn], signal=(h == 1))
                    kb.cp('act', KT3[0:64, :, key0:key0 + 128], ptt[0:64, 0:256].rearrange("p (h n) -> p h n", h=2),
                          reads=[ptn], writes=['KT'])
                for gb in range(4):
                    pt, pn = ptr.get()
                    for gi in range(8):
                        g = gb * 8 + gi
                        kb.tr(pt[:, gi * 128:(gi + 1) * 128], Ub[:, g, :, :].rearrange("p s h -> p (s h)"), ident,
                              reads=[Ubn, 'ident'], writes=[pn], signal=(gi == 7))
                    eng = 'act' if gb % 2 == 0 else 'dve'
                    kb.cp(eng, X[:, gb * 8:(gb + 1) * 8, :].rearrange("p g n -> p (g n)"), pt[:, :], reads=[pn], writes=[Xn])
                kb.dma(XSd[su], X.rearrange("p g n -> p (g n)"), reads=[Xn], writes=['XSd%d' % su], q='act')
                if is_sample:
                    NC_ = 257
                else:
                    NC_ = 132
                for d in range(2):
                    for qb in range(8):
                        pt, pn = pacc.get()
                        for qi in range(2):
                            gq = qb * 2 + qi
                            for c in range(2):
                                col = (qi * 2 + c) * 128
                                for gh in range(2):
                                    kb.mm(pt[gh * 64:(gh + 1) * 64, col:col + 128], WB[:, d, gq, c, gh * 64:(gh + 1) * 64],
                                          X[:, 2 * gq + gh, :], start=True, stop=True, reads=[WBn_, Xn], writes=[pn],
                                          signal=(qi == 1 and c == 1 and gh == 1))
                        base = d * 32 * NC_ + (qb * 4) * NC_
                        if is_sample:
                            c0 = (su - 1) * 128 + (1 if d == 0 else 0)
                            outv = V(HL, base + c0, [[NC_, 4], [1, 128]])
                            inv = pt[:, :].rearrange("p (a n) -> p a n", a=4)
                        else:
                            outv = V(HL, (d * 32 + qb * 4) * 128, [[128, 4], [1, 128]])
                            inv = pt[:, :].rearrange("p (a n) -> p a n", a=4)
                        eng = 'act' if qb % 2 == 0 else 'dve'
                        kb.cp(eng, outv, inv, reads=[pn], writes=['HL'])

        def scan_body(is_sample):
            nseq = 1 if is_sample else 4
            J = 256 if is_sample else 32
            NC_ = 257 if is_sample else 132
            ST, STn = AF_.alloc([128, 2, 32, nseq])
            T1, T1n = AF_.alloc([128, 2, 32, nseq])
            T2, T2n = AF_.alloc([128, 2, 32, nseq])
            if is_sample:
                Hn, Hnn = AF_.alloc([64, 128])
                kb.dma(Hn, dram_ap(h0_d, 0, [[128, 64], [1, 128]]), writes=[Hnn])
                pt, pn = pacc.get()
                kb.tr(pt[:, 0:64], Hn, identf[0:64, 0:64], reads=[Hnn, 'identf'], writes=[pn])
                kb.cp('act', V(ST, 0, [[32, 2], [1, 2], [2, 16]]), pt[:, 0:64].rearrange("p (d c q) -> p d c q", d=2, c=2),
                      reads=[pn], writes=[STn])
            else:
                kb.memset('pool', ST, 0.0, writes=[STn])
            LLv = V(LLt, 0, [[32, 2], [1, 32], [0, nseq]])
            if is_sample:
                for d in range(2):
                    colv = V(HL, d * 32 * NC_ + (0 if d == 0 else J), [[NC_, 32], [J + 1, nseq]])
                    kb.cp('pool', colv, ST[:, d, :, :], reads=[STn], writes=['HL'])
            else:
                kb.memset('pool', V(HL, 8192, [[128, 32], [32, 4]]), 0.0, writes=['HL'])
                kb.memset('pool', V(HL, 8192 + 32 * 128 + 31, [[128, 32], [32, 4]]), 0.0, writes=['HL'])
            for i in range(J):
                if is_sample:
                    cf, cb_ = i + 1, J - 1 - i
                    hv_in = V(HL, cf, [[32 * NC_ + cb_ - cf, 2], [NC_, 32], [J + 1, nseq]])
                    hv_out = hv_in
                else:
                    hv_in = V(HL, i, [[32 * 128 + 31 - 2 * i, 2], [128, 32], [32, 4]])
                    hv_out = V(HL, 8192 + i + 1, [[32 * 128 + 29 - 2 * i, 2], [128, 32], [32, 4]]) if i < J - 1 else None
                kb.tt('pool', T1, ST, LLv, ALU.mult, reads=[STn, 'LLt'], writes=[T1n])
                for c in range(2):
                    stv = V(ST, (1 - c) * nseq, [[32 * nseq, 2], [2 * nseq, 16], [1, nseq]])
                    t2v = V(T2, c * nseq, [[32 * nseq, 2], [2 * nseq, 16], [1, nseq]])
                    lxv = V(LXt, c, [[32, 2], [2, 16], [0, nseq]])
                    kb.tt('pool', t2v, stv, lxv, ALU.mult, reads=[STn, 'LXt'], writes=[T2n])
                kb.tt('pool', T1, T1, T2, ALU.add, reads=[T1n, T2n], writes=[T1n])
                kb.tt('pool', ST, T1, hv_in, ALU.add, reads=[T1n, 'HL'], writes=[STn])
                if hv_out is not None:
                    kb.cp('pool', hv_out, ST, reads=[STn], writes=['HL'])
            return ST, STn

        def scan_finals(ST, STn):
            if True:
                STp, STpn = AF_.alloc([128, 256])
                nseq = 4
                for sq_ in range(4):
                    kb.cp('pool', V(STp, sq_ * 64, [[32, 2], [16, 2], [1, 16]]), V(ST, sq_, [[128, 2], [4, 2], [8, 16]]),
                          reads=[STn], writes=[STpn])
                FT, FTn = AF_.alloc([128, 256])
                for ch in range(2):
                    pt, pn = pacc.get()
                    kb.tr(pt[:, 0:128], STp[:, ch * 128:(ch + 1) * 128], identf, reads=[STpn, 'identf'], writes=[pn])
                    kb.cp('act', FT[:, ch * 128:(ch + 1) * 128], pt[:, 0:128], reads=[pn], writes=[FTn])
                    kb.dma(dram_ap(ns, ch * 16384, [[128, 128], [1, 128]]), FT[:, ch * 128:(ch + 1) * 128], reads=[FTn])

        pS = APPool(pacc.tiles[0:2])
        pO = APPool(pacc.tiles[2:4])
        pD = APPool(pacc.tiles[4:6])

        def load_cache():
            ckb, ckn = AB_.alloc([128, 4, 128])
            for t in range(4):
                kb.dma(ckb[:, t, :], ck_d[t * 128:(t + 1) * 128, :], writes=[ckn], q='pool')
                kb.dma(VO[:, t, :, 0:64], cv_d[t * 128:(t + 1) * 128, :].rearrange("p (h d) -> p h d", h=2), writes=['VA'], q='pool')
            for t in range(4):
                ptt, ptn = ptr.get()
                for h in range(2):
                    kb.tr(ptt[0:64, h * 128:(h + 1) * 128], ckb[:, t, h * 64:(h + 1) * 64], ident,
                          reads=[ckn, 'ident'], writes=[ptn], signal=(h == 1))
                kb.cp('act', KT3[0:64, :, t * 128:(t + 1) * 128], ptt[0:64, 0:256].rearrange("p (h n) -> p h n", h=2),
                      reads=[ptn], writes=['KT'])

        def phase_A2a(units, is_sample):
            new_phase()
            ci = 1 if is_sample else 0
            load_mod(0, ci)
            WQ, WQn = AB_.alloc([128, 8, 1536])
            S.groups[WQn] = [WQn + '#0', WQn + '#1', WQn + '#2']
            w0v = ab_w_in_b.rearrange("(k p) n -> p k n", p=128)
            kb.dma(WQ[:, :, 0:512], w0v[:, :, 0:512], reads=wkeys['ab_in'], writes=[WQn + '#0'])
            kb.dma(WQ[:, :, 512:1024], w0v[:, :, 768:1280], reads=wkeys['ab_in'], writes=[WQn + '#1'], q='act')
            kb.dma(WQ[:, :, 1024:1536], w0v[:, :, 1792:2304], reads=wkeys['ab_in'], writes=[WQn + '#2'])
            ST_, STn_ = scan_body(is_sample)
            NQ = 512 if is_sample else 256
            nu = NQ // 256
            cpb = 512 // NQ
            pools = (AF_.pool([128, D], 3), AB_.pool([128, D], 2), AB_.pool([128, D], 2), AF_.pool([128, 32], 4))
            sm_pool = pools[3]
            hnT_pool = AB_.pool([128, 8, NQ], 1)
            sq_pool = AF_.pool([128, 512], 2)
            qn_pool = AF_.pool([128, 512], 2)
            qb_pool = AB_.pool([128, 512], 2)
            rp_pool = AF_.pool([128, 64], 2)
            rt_pool = AF_.pool([128, 256], 2)
            qT_pool = AB_.pool([128, 8, NQ], 1)
            kb.memset('dve', qT_pool.tiles[0][0][64:128], 0.0, writes=[qT_pool.tiles[0][1]])
            GA_pool = AB_.pool([64, 8, NQ], 1)
            GB_pool = AB_.pool([128, 4, NQ], 1)
            MA_pool = AB_.pool([64, 8, NQ], 1)
            PT_pool = AB_.pool([128, 512], 4)
            of_pool = AF_.pool([128, NQ], 2)
            rd_pool = AF_.pool([64, NQ], 2)
            ot_pool = AF_.pool([64, NQ], 2)
            for gi in range(len(units) // nu):
                gu = units[gi * nu:(gi + 1) * nu]
                u = gu[0]
                row0 = (u - 4) * 256 if is_sample else u * 256
                xsrc = xs if is_sample else xp
                hnT, hnTn = hnT_pool.get()
                su_ = (1 + (u - 4) // 4) if is_sample else 0
                col0_ = ((u - 4) % 4) * 256 if is_sample else u * 256
                kb.dma(hnT, HNTd[su_].rearrange("p (k n) -> p k n", k=8)[:, :, col0_:col0_ + NQ], reads=['HNTd%d' % su_], writes=[hnTn])
                qT, qTn = qT_pool.get()
                for t in range(NQ // 128):
                    pt, pn = pS.get()
                    for k in range(8):
                        kb.mm(pt[:, :], hnT[:, k, t * 128:(t + 1) * 128], WQ[:, k, 0:512], start=(k == 0), stop=(k == 7),
                              reads=[hnTn, WQn], writes=[pn])
                    sq, sqn = sq_pool.get()
                    smt, sn_ = sm_pool.get()
                    kb.act(sq, pt[:, :], AF.Square, reads=[pn], writes=[sqn])
                    kb.rsum('dve', smt[:, 0:8], sq.rearrange("p (h d) -> p h d", h=8), reads=[sqn], writes=[sn_])
                    kb.rstd(smt[:, 16:24], smt[:, 0:8], smt[:, 8:16], 1.0 / 64, sn_)
                    qn, qnn = qn_pool.get()
                    kb.tt('dve', qn.rearrange("p (h d) -> p h d", h=8), pt[:, :].rearrange("p (h d) -> p h d", h=8),
                          V(smt, 16, [[1, 8], [0, 64]]), ALU.mult, reads=[pn, sn_], writes=[qnn])
                    qb, qbn = qb_pool.get()
                    if is_sample:
                        kb.tt('dve', qn.rearrange("p (h d) -> p h d", h=8), qn.rearrange("p (h d) -> p h d", h=8),
                              V(qg, 0, [[0, 8], [1, 64]]), ALU.mult, reads=[qnn, 'qg'], writes=[qnn])
                        rp, rpn = rp_pool.get()
                        kb.dma(rp, rope_d[row0 + t * 128: row0 + (t + 1) * 128, :], writes=[rpn])
                        rt, rtn = rt_pool.get()
                        ra, ran = rt_pool.get()
                        rope('dve', qb, qn, 8, rp, rt, ra, [qnn, rpn], [qbn], rtn)
                    else:
                        kb.tt('dve', qb.rearrange("p (h d) -> p h d", h=8), qn.rearrange("p (h d) -> p h d", h=8),
                              V(qg, 0, [[0, 8], [1, 64]]), ALU.mult, reads=[qnn, 'qg'], writes=[qbn])
                    ptt, ptn = ptr.get()
                    for h in range(8):
                        kb.tr(ptt[0:64, h * 128:(h + 1) * 128], qb[:, h * 64:(h + 1) * 64], ident,
                              reads=[qbn, 'ident'], writes=[ptn], signal=(h == 7))
                    kb.cp('act', qT[0:64, :, t * 128:(t + 1) * 128], ptt[0:64, :].rearrange("p (h n) -> p h n", h=8),
                          reads=[ptn], writes=[qTn])
                GA, GAn = GA_pool.get()
                for hb in range(8 // cpb):
                    pt, pn = pS.get()
                    for hi in range(cpb):
                        h = hb * cpb + hi
                        for k in range(8):
                            kb.mm(pt[0:64, hi * NQ:(hi + 1) * NQ], WQ[:, k, 512 + h * 64:512 + (h + 1) * 64], hnT[:, k, :],
                                  start=(k == 0), stop=(k == 7), reads=[hnTn, WQn], writes=[pn], signal=(k == 7 and hi == cpb - 1))
                    kb.act(GA[:, hb * cpb:(hb + 1) * cpb, :].rearrange("p h n -> p (h n)"), pt[0:64, :], AF.Silu, reads=[pn], writes=[GAn])
                GB, GBn = GB_pool.get()
                for cb in range(4 // cpb):
                    pt, pn = pS.get()
                    for ci_ in range(cpb):
                        cc = cb * cpb + ci_
                        for k in range(8):
                            kb.mm(pt[:, ci_ * NQ:(ci_ + 1) * NQ], WQ[:, k, 1024 + cc * 128:1024 + (cc + 1) * 128], hnT[:, k, :],
                                  start=(k == 0), stop=(k == 7), reads=[hnTn, WQn], writes=[pn], signal=(k == 7 and ci_ == cpb - 1))
                    kb.act(GB[:, cb * cpb:(cb + 1) * cpb, :].rearrange("p c n -> p (c n)"), pt[:, :], AF.Silu, reads=[pn], writes=[GBn])
                for i_, uu in enumerate(gu):
                    kb.dma(GBd[uu].rearrange("p (c n) -> p c n", c=4), GB[:, :, i_ * 256:(i_ + 1) * 256], reads=[GBn], writes=['GBd%d' % uu], q='act')
                if is_sample:
                    key0, nkc = 0, 20
                else:
                    key0, nkc = u * 256, 2
                MA, MAn = MA_pool.get()
                nb = nkc // cpb
                pending = [None]

                def emit_S(h, b):
                    kvh = h // 4
                    pt, pn = pS.get()
                    for j in range(cpb):
                        kk = key0 + (b * cpb + j) * 128
                        kb.mm(pt[:, j * NQ:(j + 1) * NQ], KT3[:, kvh, kk:kk + 128], qT[:, h, :], start=True, stop=True,
                              reads=['KT', qTn], writes=[pn], signal=(j == cpb - 1))
                    return pt, pn

                def make_epilogue(h, po, pon):
                    def epi():
                        of, ofn = of_pool.get()
                        kb.cp('dve', of, po[:, 0:NQ], reads=[pon], writes=[ofn])
                        pd, pdn = pD.get()
                        kb.mm(pd[0:64, 0:NQ], identf[:, 64:128], of, start=True, stop=True, reads=['identf', ofn], writes=[pdn])
                        rd, rdn = rd_pool.get()
                        S.op('dve', (lambda o_, i_: (lambda e: e.reciprocal(out=o_, in_=i_)))(rd, pd[0:64, 0:NQ]), reads=[pdn], writes=[rdn])
                        ot, otn = ot_pool.get()
                        kb.tt('dve', ot, of[0:64, :], rd, ALU.mult, reads=[ofn, rdn], writes=[otn])
                        kb.tt('dve', MA[:, h, :], ot, GA[:, h, :], ALU.mult, reads=[otn, GAn], writes=[MAn])
                    return epi

                for h in range(8):
                    kvh = h // 4
                    po, pon = pO.get()
                    sq_ = [emit_S(h, 0)]
                    if nb > 1:
                        sq_.append(emit_S(h, 1))
                    if pending[0] is not None:
                        pending[0]()
                        pending[0] = None
                    for b in range(nb):
                        pt, pn = sq_[b]
                        PT, PTn = PT_pool.get()
                        kb.act(PT, pt[:, :], AF.Exp, reads=[pn], writes=[PTn])
                        for j in range(cpb):
                            kc = b * cpb + j
                            vt = (key0 // 128) + kc
                            first, last = (kc == 0), (kc == nkc - 1)
                            kb.mm(po[:, 0:NQ], VO[:, vt, kvh, :], PT[:, j * NQ:(j + 1) * NQ],
                                  start=first, stop=last, reads=['VA', PTn], writes=[pon], signal=last)
                        if b + 2 < nb:
                            sq_.append(emit_S(h, b + 2))
                    pending[0] = make_epilogue(h, po, pon)
                pending[0]()
                for i_, uu in enumerate(gu):
                    kb.dma(MIXAd[uu].rearrange("p (h n) -> p h n", h=8), MA[:, :, i_ * 256:(i_ + 1) * 256], reads=[MAn], writes=['MIXAd%d' % uu], q='act')
            if not is_sample:
                scan_finals(ST_, STn_)

        def phase_A2b(sus, is_sample, out_ap):
            new_phase()
            ci = 1 if is_sample else 0
            load_mod(0, ci)
            NC_ = 257 if is_sample else 132
            WCt, WCtn = AB_.alloc([128, 2, 16, 2, 128])
            kb.dma(WCt.rearrange("p d q c n -> p (d q c n)"), WCd, reads=['WCd'], writes=[WCtn])
            KMt, KMtn = AB_.alloc([128, 32, 128])
            kb.dma(KMt.rearrange("p g n -> p (g n)"), KMd, reads=['KMd'], writes=[KMtn], q='act')
            X, Xn = AB_.alloc([128, 32, 128])
            WOa, WOan = AB_.alloc([128, 4, 1024])
            WOb, WObn = AB_.alloc([128, 4, 1024])
            WG, WGn = AB_.alloc([128, 4, 512])
            wov = ab_w_out_b.rearrange("(k p) n -> p k n", p=128)
            kb.dma(WOa, wov[:, 0:4, :], reads=wkeys['ab_out'], writes=[WOan])
            kb.dma(WOb, wov[:, 4:8, :], reads=wkeys['ab_out'], writes=[WObn], q='act')
            kb.dma(WG, glu_w_b.rearrange("(k p) n -> p k n", p=128), reads=wkeys['glu'], writes=[WGn])
            glub, glubn = AF_.alloc([128, 4])
            load_T(glub, glubn, dram_ap(glu_b, 0, [[128, 4], [1, 128]]), 4)
            GBt, GBtn = AB_.alloc([128, 4, 1024])
            MAt, MAtn = AB_.alloc([128, 4, 1024])
            Ybm, Ybn = AB_.alloc([128, 8, 512])
            yT, yTn = AB_.alloc([128, 4, 1024])
            mixB, mixBn = GBt, GBtn
            sg_pool = AB_.pool([128, 512], 2)
            xt_pool = AF_.pool([128, D], 2)
            ot_pool = AF_.pool([128, D], 2)
            jk_pool = AB_.pool([128, 512], 2)
            sm_pool = AF_.pool([128, 8], 4)
            for su in sus:
              u0 = su * 4
              kb.dma(X.rearrange("p g n -> p (g n)"), XSd[su], reads=['XSd%d' % su], writes=[Xn])
              for ul in range(4):
                  kb.dma(GBt[:, :, ul * 256:(ul + 1) * 256], GBd[u0 + ul].rearrange("p (c n) -> p c n", c=4),
                         reads=['GBd%d' % (u0 + ul)], writes=[GBtn])
                  for h in range(8):
                      kb.dma(MAt[(h % 2) * 64:(h % 2) * 64 + 64, h // 2, ul * 256:(ul + 1) * 256], MIXAd[u0 + ul][:, h * 256:(h + 1) * 256],
                             reads=['MIXAd%d' % (u0 + ul)], writes=['%s_%d_%d' % (MAtn, ul, h)])
              for gb in range(8):
                  pt, pn = pacc.get()
                  gh = gb % 2
                  hs = slice(gh * 64, (gh + 1) * 64)
                  firstmm = True
                  for gi in range(4):
                      gq = (gb // 2) * 4 + gi
                      osl = slice(gi * 128, (gi + 1) * 128)
                      for d in range(2):
                          for c in range(2):
                              if is_sample:
                                  off = (d * 32 + gq * 2 + c) * 257 + (su - 1) * 128 + (1 if d == 1 else 0)
                              else:
                                  off = 8192 + (d * 32 + gq * 2 + c) * 128
                              lhs = V(HL, off, [[1, 128]], p0=gh * 64, np_=64)
                              kb.mm(pt[:, osl], lhs, WCt[hs, d, gq, c, :], start=firstmm, stop=False,
                                    reads=['HL', WCtn], writes=[pn], signal=False, sgc=True)
                              firstmm = False
                  for gi in range(4):
                      gq = (gb // 2) * 4 + gi
                      g = 2 * gq + gh
                      osl = slice(gi * 128, (gi + 1) * 128)
                      kb.mm(pt[:, osl], X[:, g, 0:128], KMt[:, g, :], start=False, stop=True,
                            reads=[Xn, KMtn], writes=[pn], signal=(gi == 3), sgc=True)
                  outv = V(Ybm, (2 * ((gb // 2) * 4) + (gb % 2)) * 16, [[32, 4], [512, 8], [1, 16]])
                  inv = V(pt[:, :], 0, [[128, 4], [16, 8], [1, 16]])
                  kb.act(outv, inv, AF.Gelu_apprx_tanh, reads=[pn], writes=[Ybn])
              for cc in range(4):
                  ptt, ptn = ptr.get()
                  for t in range(8):
                      kb.tr(ptt[:, t * 128:(t + 1) * 128], Ybm[:, t, cc * 128:(cc + 1) * 128], ident,
                            reads=[Ybn, 'ident'], writes=[ptn], signal=(t == 7))
                  eng = 'act' if cc % 2 == 0 else 'dve'
                  kb.cp(eng, V(yT, cc * 1024, [[1, 8], [8, 128]]), ptt[:, :].rearrange("p (t j) -> p t j", t=8),
                        reads=[ptn], writes=[yTn])
              for nb in range(2):
                  cols = slice(nb * 512, (nb + 1) * 512)
                  for oc in range(4):
                      pt, pn = pacc.get()
                      for cc in range(4):
                          kb.mm(pt[:, :], WG[:, cc, oc * 128:(oc + 1) * 128], yT[:, cc, cols], start=(cc == 0), stop=(cc == 3),
                                reads=[WGn, yTn], writes=[pn])
                      sg, sgn = sg_pool.get()
                      kb.act(sg, pt[:, :], AF.Sigmoid, reads=[pn, glubn], writes=[sgn], bias=glub[:, oc:oc + 1])
                      kb.tt('dve', sg, sg, yT[:, oc, cols], ALU.mult, reads=[sgn, yTn], writes=[sgn])
                      kb.tt('dve', GBt[:, oc, cols], sg, GBt[:, oc, cols], ALU.mult, reads=[sgn, GBtn], writes=[GBtn])
              xsrc = xs if is_sample else xp
              row0 = (su - 1) * 1024 if is_sample else 0
              for t in range(8):
                  tc_ = slice(t * 128, (t + 1) * 128)
                  xt, xn = xt_pool.get()
                  kb.dma(xt, xsrc[row0 + t * 128: row0 + (t + 1) * 128, :], writes=[xn])
                  smt, sn_ = sm_pool.get()
                  kb.memset('dve', smt[:, 0:2], 0.0, writes=[sn_])
                  pts = []
                  for hf in range(2):
                      pt, pn = pacc.get()
                      oc = slice(hf * 512, (hf + 1) * 512)
                      for h in range(4):
                          kb.mm(pt[:, :], MAt[:, h, tc_], WOa[:, h, oc], start=(h == 0), stop=False, reads=['%s_%d_%d' % (MAtn, t // 2, 2 * h + e_) for e_ in range(2)] + [WOan], writes=[pn], signal=False)
                      for cc in range(4):
                          kb.mm(pt[:, :], mixB[:, cc, tc_], WOb[:, cc, oc], start=False, stop=(cc == 3), reads=[mixBn, WObn], writes=[pn], signal=(cc == 3))
                      jk, jn = jk_pool.get()
                      kb.act(jk, pt[:, :], AF.Square, reads=[pn, sn_], writes=[jn, sn_], accum_out=smt[:, hf:hf + 1])
                      pts.append((pt, pn))
                  kb.tt('dve', smt[:, 2:3], smt[:, 0:1], smt[:, 1:2], ALU.add, reads=[sn_], writes=[sn_])
                  kb.rstd(smt[:, 4:5], smt[:, 2:3], smt[:, 3:4], 1.0 / D, sn_)
                  ot, otn = ot_pool.get()
                  for hf in range(2):
                      pt, pn = pts[hf]
                      oc = slice(hf * 512, (hf + 1) * 512)
                      kb.stt('dve', ot[:, oc], pt[:, :], smt[:, 4:5], G2[:, oc], ALU.mult, ALU.mult, reads=[pn, sn_, 'G2'], writes=[otn])
                  kb.tt('dve', ot, ot, xt, ALU.add, reads=[otn, xn], writes=[otn])
                  kb.dma(out_ap[row0 + t * 128: row0 + (t + 1) * 128, :], ot, reads=[otn], writes=['X1_%d_%d' % (su, t)], q='act')

        L0_ONLY = (STAGE == 1)
        phase_A1([0], False)
        phase_A2a([0, 1, 2, 3], False)
        phase_A2b([0], False, yp if L0_ONLY else X1[0:1024, :])
        phase_A1([1, 2], True, hook=lambda: (load_cache(), convert_late()))
        phase_A2a(list(range(4, 12)), True)
        phase_A2b([1, 2], True, ys if L0_ONLY else X1[1024:NTOK, :])

        AA = HL[:, 0:16384].rearrange("p (t c n) -> p t c n", t=16, c=2)

        def phase_B1(sus, is_sample):
            new_phase()
            ci = 1 if is_sample else 0
            load_mod(1, ci)
            W1, W1n = AB_.alloc([128, 8, 2560])
            S.groups[W1n] = [W1n + '#0', W1n + '#1']
            w1v = cd_w_in_b.rearrange("(k p) n -> p k n", p=128)
            kb.dma(W1[:, 0:4, :], w1v[:, 0:4, :], reads=wkeys['cd_in'], writes=[W1n + '#0'])
            kb.dma(W1[:, 4:8, :], w1v[:, 4:8, :], reads=wkeys['cd_in'], writes=[W1n + '#1'], q='act')
            Wsb, Wsbn = AB_.alloc([128, 8, 128])
            kb.dma(Wsb, w_s_b.rearrange("h t s -> t h s"), reads=wkeys['ws'], writes=[Wsbn])
            WsT, WsTn = AB_.alloc([128, 8, 128])
            ptt, ptn = ptr.get()
            for hg in range(8):
                kb.tr(ptt[:, hg * 128:(hg + 1) * 128], Wsb[:, hg, :], ident, reads=[Wsbn, 'ident'], writes=[ptn], signal=(hg == 7))
            kb.cp('act', WsT.rearrange("p h n -> p (h n)"), ptt[:, :], reads=[ptn], writes=[WsTn])
            C128, C128n = AB_.alloc([128, 128])
            S128, S128n = AB_.alloc([128, 128])
            kb.dma(C128, c128_d, writes=[C128n])
            kb.dma(S128, s128_d, writes=[S128n])
            VG, VGn = AF_.alloc([128, 512])
            kb.dma(VG, bcast_rows(v_gain, 128), writes=[VGn])
            bs, bsn = AF_.alloc([128, 8])
            load_T(bs, bsn, b_s, 8)
            pools = (AF_.pool([128, D], 3), AB_.pool([128, D], 2), AB_.pool([128, D], 2), AF_.pool([128, 8], 4))
            sm_pool = pools[3]
            hnT_pool = AB_.pool([128, 8, 256], 2)
            cuG_pool = AB_.pool([128, 512], 2)
            cvg_pool = AF_.pool([128, 512], 2)
            cvn_pool = AB_.pool([128, 512], 2)
            gcS_pool = AB_.pool([128, 512], 2)
            oc_pool = AF_.pool([128, 512], 2)
            ocb_pool = AB_.pool([128, 512], 2)
            jk_pool = AB_.pool([128, 512], 2)
            mixC_pool = AB_.pool([128, 4, 256], 2)
            dzT_pool = AB_.pool([128, 4, 256], 2)
            GD_pool = AB_.pool([128, 4, 256], 2)
            def proj(hnT, hnTn, t, c0):
                pt, pn = pacc.get()
                for k in range(8):
                    kb.mm(pt[:, :], hnT[:, k, t * 128:(t + 1) * 128], W1[:, k, c0:c0 + 512], start=(k == 0), stop=(k == 7),
                          reads=[hnTn, W1n], writes=[pn])
                return pt, pn

            for ul4 in range(4 * len(sus)):
                su, ul = sus[ul4 // 4], ul4 % 4
                g0 = 0 if not is_sample else 1024 + (su - 1) * 1024
                u = su * 4 + ul
                hnT, hnTn = hnT_pool.get()
                for t in range(2):
                    r0 = g0 + ul * 256 + t * 128
                    hn_tile(X1[r0:r0 + 128, :], hnT, hnTn, t * 128, pools)
                mixC, mixCn = mixC_pool.get()
                tiles_ = []
                for t in range(2):
                    pt, pn = proj(hnT, hnTn, t, 0)
                    cuG, cuGn = cuG_pool.get()
                    kb.act(cuG, pt[:, :], AF.Gelu_apprx_tanh, reads=[pn], writes=[cuGn])
                    pt, pn = proj(hnT, hnTn, t, 512)
                    cvg, cvgn = cvg_pool.get()
                    kb.act(cvg, pt[:, :], AF.Gelu_apprx_tanh, reads=[pn], writes=[cvgn])
                    smt, sn_ = sm_pool.get()
                    kb.memset('dve', smt[:, 0:1], 0.0, writes=[sn_])
                    jk, jn = jk_pool.get()
                    kb.act(jk, cvg, AF.Square, reads=[cvgn, sn_], writes=[jn, sn_], accum_out=smt[:, 0:1])
                    kb.rstd(smt[:, 2:3], smt[:, 0:1], smt[:, 1:2], 1.0 / 512, sn_)
                    cvn, cvnn = cvn_pool.get()
                    kb.stt('dve', cvn, cvg, smt[:, 2:3], VG, ALU.mult, ALU.mult, reads=[cvgn, sn_, VGn], writes=[cvnn])
                    pt, pn = proj(hnT, hnTn, t, 1024)
                    gcS, gcSn = gcS_pool.get()
                    kb.act(gcS, pt[:, :], AF.Silu, reads=[pn], writes=[gcSn])
                    tiles_.append((cuG, cuGn, cvn, cvnn, gcS, gcSn))
                dzT, dzTn = dzT_pool.get()
                GD, GDn = GD_pool.get()
                for (c0, dst, dstn, fn_) in ((1536, dzT, dzTn, None), (2048, GD, GDn, AF.Silu)):
                    for cb in range(2):
                        pt, pn = pacc.get()
                        for ci_ in range(2):
                            cc = cb * 2 + ci_
                            for k in range(8):
                                kb.mm(pt[:, ci_ * 256:(ci_ + 1) * 256], W1[:, k, c0 + cc * 128:c0 + (cc + 1) * 128], hnT[:, k, :],
                                      start=(k == 0), stop=(k == 7), reads=[hnTn, W1n], writes=[pn], signal=(k == 7 and ci_ == 1))
                        dv = dst[:, cb * 2:(cb + 1) * 2, :].rearrange("p c n -> p (c n)")
                        if fn_ is None:
                            kb.cp('dve', dv, pt[:, :], reads=[pn], writes=[dstn])
                        else:
                            kb.act(dv, pt[:, :], fn_, reads=[pn], writes=[dstn])
                kb.dma(GDd[u], GD.rearrange("p c n -> p (c n)"), reads=[GDn], writes=['GDd%d' % u], q='act')
                for t in range(2):
                    cuG, cuGn, cvn, cvnn, gcS, gcSn = tiles_[t]
                    pt, pn = pacc.get()
                    for hg in range(8):
                        kb.mm(pt[:, hg * 64:(hg + 1) * 64], WsT[:, hg, :], cvn[:, hg * 64:(hg + 1) * 64], start=True, stop=True,
                              reads=[WsTn, cvnn], writes=[pn], signal=(hg == 7))
                    oc, ocn = oc_pool.get()
                    for hg in range(8):
                        kb.stt('dve', oc[:, hg * 64:(hg + 1) * 64], pt[:, hg * 64:(hg + 1) * 64], bs[:, hg:hg + 1],
                               cuG[:, hg * 64:(hg + 1) * 64], ALU.add, ALU.mult, reads=[pn, bsn, cuGn], writes=[ocn])
                    ocb, ocbn = ocb_pool.get()
                    kb.tt('dve', ocb, oc, gcS, ALU.mult, reads=[ocn, gcSn], writes=[ocbn])
                    tiles_[t] = (ocb, ocbn)
                for t in range(2):
                    tix = (ul * 2 + t) if not is_sample else ((su - 1) * 8 + ul * 2 + t)
                    for cs_, tab, tabn in ((0, C128, C128n), (1, S128, S128n)):
                        pt, pn = pacc.get()
                        for grp in range(4):
                            kb.mm(pt[:, grp * 128:(grp + 1) * 128], dzT[:, grp, t * 128:(t + 1) * 128], tab, start=True, stop=True,
                                  reads=[dzTn, tabn], writes=[pn], signal=(grp == 3))
                        eng = 'act' if cs_ == 0 else 'dve'
                        kb.cp(eng, AA[:, tix, cs_, :], pt[:, :], reads=[pn], writes=['AA'])
                for t in range(2):
                    ocb, ocbn = tiles_[t]
                    ptt, ptn = ptr.get()
                    for cc in range(4):
                        kb.tr(ptt[:, cc * 128:(cc + 1) * 128], ocb[:, cc * 128:(cc + 1) * 128], ident,
                              reads=[ocbn, 'ident'], writes=[ptn], signal=(cc == 3))
                    kb.cp('act', mixC[:, :, t * 128:(t + 1) * 128], ptt[:, 0:512].rearrange("p (c n) -> p c n", c=4),
                          reads=[ptn], writes=[mixCn])
                kb.dma(MIXCd[u], mixC.rearrange("p c n -> p (c n)"), reads=[mixCn], writes=['MIXCd%d' % u], q='act')

        def phase_B2(is_sample):
            new_phase()
            ci = 1 if is_sample else 0
            load_mod(1, ci)
            WF, WFn = AB_.alloc([128, 4, 512])
            WO, WOn = AB_.alloc([128, 8, 1024])
            S.groups[WOn] = [WOn + '#0', WOn + '#1']
            kb.dma(WF, fnet_w_b.rearrange("(k p) n -> p k n", p=128), reads=wkeys['fnet'], writes=[WFn])
            wo1v = cd_w_out_b.rearrange("(k p) n -> p k n", p=128)
            kb.dma(WO[:, 0:4, :], wo1v[:, 0:4, :], reads=wkeys['cd_out'], writes=[WOn + '#0'])
            kb.dma(WO[:, 4:8, :], wo1v[:, 4:8, :], reads=wkeys['cd_out'], writes=[WOn + '#1'], q='act')
            nst = 16 if is_sample else 2
            NT = 512 if is_sample else 256
            H2 = 2 if is_sample else 1
            nsh = nst // H2
            CL_pool = AB_.pool([128, nsh, NT], H2)
            SL_pool = AB_.pool([128, nsh, NT], H2)
            if not is_sample:
                CLt, CLn = CL_pool.get()
                SLt, SLn_ = SL_pool.get()
                kb.dma(CLt, clp_d.rearrange("(st p) t -> p st t", p=128), writes=[CLn])
                kb.dma(SLt, slp_d.rearrange("(st p) t -> p st t", p=128), writes=[SLn_])
            fzT_pool = AB_.pool([128, 4, NT], 2)
            mixC_pool = AB_.pool([128, 4, NT], 2)
            GD_pool = AB_.pool([128, 4, NT], 2)
            xt_pool = AF_.pool([128, D], 2)
            ot_pool = AF_.pool([128, D], 2)
            jk_pool = AB_.pool([128, 512], 2)
            sm_pool = AF_.pool([128, 8], 4)
            nblk = 4
            for blk in range(nblk):
                if is_sample:
                    units = [4 + blk * 2, 4 + blk * 2 + 1]
                    t0 = 0
                    g0 = 1024 + blk * 512
                    out_ap, orow0 = ys, blk * 512
                else:
                    units = [blk]
                    t0 = blk * 2
                    g0 = blk * 256
                    out_ap, orow0 = yp, blk * 256
                mixC, mixCn = mixC_pool.get()
                GD, GDn = GD_pool.get()
                for i_, u in enumerate(units):
                    kb.dma(mixC[:, :, i_ * 256:(i_ + 1) * 256], MIXCd[u].rearrange("p (c n) -> p c n", c=4), reads=['MIXCd%d' % u], writes=[mixCn])
                    kb.dma(GD[:, :, i_ * 256:(i_ + 1) * 256], GDd[u].rearrange("p (c n) -> p c n", c=4), reads=['GDd%d' % u], writes=[GDn])
                fzT, fzTn = fzT_pool.get()
                accs = [pacc.get() for _ in range(4)]
                for hf in range(H2):
                    if is_sample:
                        CLt, CLn = CL_pool.get()
                        SLt, SLn_ = SL_pool.get()
                        rows = slice(hf * nsh * 128, (hf + 1) * nsh * 128)
                        kb.dma(CLt, cls_d[rows, :].rearrange("(st p) t -> p st t", p=128)[:, :, blk * 512:(blk + 1) * 512], writes=[CLn])
                        kb.dma(SLt, sls_d[rows, :].rearrange("(st p) t -> p st t", p=128)[:, :, blk * 512:(blk + 1) * 512], writes=[SLn_])
                    for chk in range(4):
                        pt, pn = accs[chk]
                        for st in range(nsh):
                            for cs_, tab, tabn in ((0, CLt, CLn), (1, SLt, SLn_)):
                                first = (hf == 0 and st == 0 and cs_ == 0)
                                lastg = (st == nsh - 1 and cs_ == 1)
                                kb.mm(pt[:, 0:NT], AA[:, t0 + hf * nsh + st, cs_, chk * 128:(chk + 1) * 128], tab[:, st, :],
                                      start=first, stop=(lastg and hf == H2 - 1), reads=['AA', tabn], writes=[pn], signal=lastg)
                for chk in range(4):
                    pt, pn = accs[chk]
                    eng = 'act' if chk % 2 == 0 else 'dve'
                    kb.cp(eng, fzT[:, chk, :], pt[:, 0:NT], reads=[pn], writes=[fzTn])
                for oc in range(4):
                    pt, pn = pacc.get()
                    for chk in range(4):
                        kb.mm(pt[:, 0:NT], WF[:, chk, oc * 128:(oc + 1) * 128], fzT[:, chk, :], start=(chk == 0), stop=(chk == 3),
                              reads=[WFn, fzTn], writes=[pn])
                    kb.tt('dve', GD[:, oc, :], pt[:, 0:NT], GD[:, oc, :], ALU.mult, reads=[pn, GDn], writes=[GDn])
                for t in range(NT // 128):
                    tc_ = slice(t * 128, (t + 1) * 128)
                    xt, xn = xt_pool.get()
                    kb.dma(xt, X1[g0 + t * 128: g0 + (t + 1) * 128, :], writes=[xn])
                    smt, sn_ = sm_pool.get()
                    kb.memset('dve', smt[:, 0:2], 0.0, writes=[sn_])
                    pts = []
                    for hf in range(2):
                        pt, pn = pacc.get()
                        oc = slice(hf * 512, (hf + 1) * 512)
                        for kc in range(4):
                            kb.mm(pt[:, :], mixC[:, kc, tc_], WO[:, kc, oc], start=(kc == 0), stop=False, reads=[mixCn, WOn], writes=[pn], signal=False)
                        for kc in range(4):
                            kb.mm(pt[:, :], GD[:, kc, tc_], WO[:, 4 + kc, oc], start=False, stop=(kc == 3), reads=[GDn, WOn], writes=[pn], signal=(kc == 3))
                        jk, jn = jk_pool.get()
                        kb.act(jk, pt[:, :], AF.Square, reads=[pn, sn_], writes=[jn, sn_], accum_out=smt[:, hf:hf + 1])
                        pts.append((pt, pn))
                    kb.tt('dve', smt[:, 2:3], smt[:, 0:1], smt[:, 1:2], ALU.add, reads=[sn_], writes=[sn_])
                    kb.rstd(smt[:, 4:5], smt[:, 2:3], smt[:, 3:4], 1.0 / D, sn_)
                    ot, otn = ot_pool.get()
                    for hf in range(2):
                        pt, pn = pts[hf]
                        oc = slice(hf * 512, (hf + 1) * 512)
                        kb.stt('dve', ot[:, oc], pt[:, :], smt[:, 4:5], G2[:, oc], ALU.mult, ALU.mult, reads=[pn, sn_, 'G2'], writes=[otn])
                    kb.tt('dve', ot, ot, xt, ALU.add, reads=[otn, xn], writes=[otn])
                    kb.dma(out_ap[orow0 + t * 128: orow0 + (t + 1) * 128, :], ot, reads=[otn], q='act')

        if not L0_ONLY:
            phase_B1([0], False)
            phase_B2(False)
            phase_B1([1, 2], True)
            phase_B2(True)

        with nc.allow_low_precision("bf16 matmul operands, fp32 accumulation"):
            kb.S.run()
    return kb


_CACHE = {}


def _dft_tables():
    if 'dft' in _CACHE:
        return _CACHE['dft']
    bf = lambda a: np.ascontiguousarray(a.astype(np.float32)).astype(ml_dtypes.bfloat16)
    i128 = np.arange(128, dtype=np.int64)
    m = (i128[:, None] * i128[None, :]) % 128
    a = 2 * np.pi * m / 128.0
    out = {'c128': bf(np.cos(a)), 's128': bf(np.sin(a))}
    for nm, L in (('p', 256), ('s', SL)):
        i = np.arange(L, dtype=np.int64)
        m = (i[:, None] * i[None, :]) % L
        a = 2 * np.pi * m / float(L)
        sc = 1.0 / math.sqrt(128.0 * L)
        out['cl' + nm] = bf(np.cos(a) * sc)
        out['sl' + nm] = bf(-np.sin(a) * sc)
    _CACHE['dft'] = out
    return out


def kernel(**inp):
    if 'kb' not in _CACHE:
        _CACHE['kb'] = build()
    kb = _CACHE['kb']
    f = lambda a: np.ascontiguousarray(np.asarray(a, dtype=np.float32))
    x_prompt = f(inp['x_prompt'])
    x_sample = f(inp['x_sample'])
    c = f(inp['c'])
    c_ctx = f(inp['c_ctx'])
    ident = np.eye(128, dtype=np.float32)
    si = np.arange(128) // 16
    maskf = (si[None, :] >= si[:, None]).astype(np.float32)
    maskb = (si[:, None] >= si[None, :]).astype(np.float32)
    tpos = np.arange(SL)
    inv = (10000.0 ** (-np.arange(16, dtype=np.float32) / 16)).astype(np.float32)
    row = (tpos // 64).astype(np.float32)
    col = (tpos % 64).astype(np.float32)
    ang = np.concatenate([row[:, None] * inv[None, :], col[:, None] * inv[None, :]], 1).astype(np.float32)
    rope = np.concatenate([np.cos(ang), np.sin(ang)], 1).astype(np.float32)
    shared = {
        'ada_w': f(inp['ada_w']), 'ada_b': f(inp['ada_b']), 'norm_pre': f(inp['norm_pre']), 'norm_post': f(inp['norm_post']),
        'ab_w_in': f(inp['ab_w_in'])[0], 'q_gain': f(inp['ab_q_norm'])[0], 'k_gain': f(inp['ab_k_norm'])[0],
        'lam_re': f(inp['ssm_lambda_re'])[0].reshape(-1), 'lam_im': f(inp['ssm_lambda_im'])[0].reshape(-1),
        'log_dt': f(inp['ssm_log_dt'])[0].reshape(-1),
        'b_re': f(inp['ssm_b_re'])[0].reshape(-1), 'b_im': f(inp['ssm_b_im'])[0].reshape(-1),
        'c_re': f(inp['ssm_c_re'])[0].reshape(-1), 'c_im': f(inp['ssm_c_im'])[0].reshape(-1),
        'ssm_d': f(inp['ssm_d'])[0], 'glu_w': f(inp['ssm_glu_w'])[0], 'glu_b': f(inp['ssm_glu_b'])[0],
        'ab_w_out': f(inp['ab_w_out'])[0],
        'ident': ident.astype(ml_dtypes.bfloat16), 'identf': ident, 'maskf': maskf, 'maskb': maskb, 'rope': rope,
        'cd_w_in': f(inp['cd_w_in'])[0], 'v_gain': f(inp['gmlp_v_norm'])[0], 'w_s': f(inp['gmlp_w_s'])[0], 'b_s': f(inp['gmlp_b_s'])[0],
        'fnet_w': f(inp['fnet_w'])[0], 'cd_w_out': f(inp['cd_w_out'])[0],
    }
    shared.update(_dft_tables())
    used = set(kb.din.keys())
    maps = []
    for core in range(NCORES):
        b = core // 4
        m = dict(shared)
        m['xp'] = x_prompt[core * NPS:(core + 1) * NPS].reshape(NPS * SEQ, D)
        m['xs'] = x_sample[b]
        m['cvec'] = np.stack([c_ctx, c[b]], 0)
        m['ck'] = f(inp['cache_k'])[b, 0].reshape(512, 128)
        m['cv'] = f(inp['cache_v'])[b, 0].reshape(512, 128)
        m['h0'] = f(inp['state_ssm'])[b, 0].reshape(-1)
        maps.append({k: v for k, v in m.items() if k in used})
    res = run_bass_kernel_spmd(kb.nc, maps, core_ids=list(range(NCORES)))
    r = res.results
    B = 32
    cat = lambda name, shp: np.concatenate([np.asarray(r[c_][name], dtype=np.float32).reshape(shp) for c_ in range(NCORES)], 0)
    y_prompt = cat('yp', (NPS, SEQ, D))
    y_sample = np.stack([np.asarray(r[0]['ys'], dtype=np.float32), np.asarray(r[4]['ys'], dtype=np.float32)], 0).reshape(2, SL, D)
    nk = cat('nk', (NPS, 1, SEQ, 2, 64))
    nv = cat('nv', (NPS, 1, SEQ, 2, 64))
    ns = cat('ns', (NPS, 1, 2, 2, 32, 64))
    return (y_prompt, y_sample, nk, nv, ns)
```

```python
import contextlib
import numpy as np
import ml_dtypes
import concourse.bass as bass
import concourse.mybir as mybir
from concourse.bass_utils import run_bass_kernel_spmd

F32 = mybir.dt.float32
BF16 = mybir.dt.bfloat16
ALU = mybir.AluOpType
AF = mybir.ActivationFunctionType
AX = mybir.AxisListType

ENGS = ['pe', 'act', 'dve', 'pool', 'sp']
DMA_RING = 8
NCORES = 8
D = 1024
NPS = 4
SEQ = 256
EPS = 1e-6


class Sched:
    def __init__(self, nc):
        self.nc = nc
        self.ops = {e: [] for e in ENGS}
        self.lastw = {}
        self.readers = {}
        self.ndma = {e: 0 for e in ENGS}
        self.pending = {}
        self.groups = {}

    def expand(self, keys):
        out = []
        for k in keys:
            out.extend(self.groups.get(k, [k]))
        return out

    def barrier(self):
        deps = set()
        for e in ENGS:
            ops = self.ops[e]
            for i in range(len(ops) - 1, -1, -1):
                if not ops[i]['dma']:
                    deps.add((e, i))
                    break
            cnt = 0
            for i in range(len(ops) - 1, -1, -1):
                if ops[i]['dma']:
                    deps.add((e, i))
                    cnt += 1
                    if cnt >= DMA_RING:
                        break
        for e in ENGS:
            self.pending[e] = set(deps) | self.pending.get(e, set())

    def op(self, eng, fn, reads=(), writes=(), signal=True, dma=False):
        reads = self.expand(reads)
        writes = self.expand(writes)
        deps = set()
        if self.pending.get(eng):
            deps |= self.pending.pop(eng)
        for k in reads:
            if k in self.lastw:
                deps.add(self.lastw[k])
        for k in writes:
            if k in self.lastw:
                deps.add(self.lastw[k])
            for r in self.readers.get(k, ()):
                deps.add(r)
        idx = len(self.ops[eng])
        me = (eng, idx)
        deps.discard(me)
        if eng == 'pe':
            deps = {d for d in deps if d[0] != 'pe'}
        rec = dict(fn=fn, deps=deps, signal=signal or dma, dma=dma, dma_n=None)
        if dma:
            rec['dma_n'] = self.ndma[eng]
            self.ndma[eng] += 1
        self.ops[eng].append(rec)
        for k in writes:
            self.lastw[k] = me
            self.readers[k] = set()
        for k in reads:
            self.readers.setdefault(k, set()).add(me)
        return me

    def run(self):
        nc = self.nc
        with contextlib.ExitStack() as es:
            csem = {e: es.enter_context(nc.semaphore('c_' + e)) for e in ENGS}
            dsem = {e: [es.enter_context(nc.semaphore('d_%s%d' % (e, i))) for i in range(DMA_RING)]
                    for e in ENGS if self.ndma[e] > 0}
            ev = {}
            for e in ENGS:
                ops = self.ops[e]
                cnt = 0
                cum = []
                for o in ops:
                    if o['signal'] and not o['dma']:
                        cnt += 1
                    cum.append(cnt)
                nxt = None
                for i in range(len(ops) - 1, -1, -1):
                    o = ops[i]
                    if o['dma']:
                        n = o['dma_n']
                        ev[(e, i)] = (('d', e, n % DMA_RING), 16 * (n // DMA_RING + 1))
                    else:
                        if o['signal']:
                            nxt = cum[i]
                        assert nxt is not None, 'last compute op on engine %s must signal' % e
                        ev[(e, i)] = (('c', e), nxt)

            def semh(sid):
                return csem[sid[1]] if sid[0] == 'c' else dsem[sid[1]][sid[2]]

            final_waits = []
            for e in ENGS:
                n = self.ndma[e]
                for r in range(DMA_RING):
                    c = len(range(r, n, DMA_RING))
                    if c > 0:
                        final_waits.append((('d', e, r), 16 * c))

            def replay(e, eng):
                waited = {}
                ops = self.ops[e]
                for i, o in enumerate(ops):
                    need = {}
                    for d in o['deps']:
                        sid, val = ev[d]
                        if need.get(sid, 0) < val:
                            need[sid] = val
                    if o['dma'] and o['dma_n'] >= DMA_RING:
                        n0 = o['dma_n'] - DMA_RING
                        sid, val = ('d', e, n0 % DMA_RING), 16 * (n0 // DMA_RING + 1)
                        if need.get(sid, 0) < val:
                            need[sid] = val
                    for sid, val in need.items():
                        if waited.get(sid, 0) < val:
                            eng.wait_ge(semh(sid), val)
                            waited[sid] = val
                    ins = o['fn'](eng)
                    if o['dma']:
                        sid, val = ev[(e, i)]
                        ins.then_inc(semh(sid), 16)
                    elif o['signal']:
                        ins.then_inc(csem[e], 1)
                if e == 'sp':
                    for sid, val in final_waits:
                        if waited.get(sid, 0) < val:
                            eng.wait_ge(semh(sid), val)
                    for e2 in ENGS:
                        if e2 != 'sp':
                            c = sum(1 for o in self.ops[e2] if o['signal'] and not o['dma'])
                            if c > 0:
                                eng.wait_ge(csem[e2], c)

            with nc.Block() as block:
                @block.tensor
                def _(eng):
                    replay('pe', eng)

                @block.scalar
                def _(eng):
                    replay('act', eng)

                @block.vector
                def _(eng):
                    replay('dve', eng)

                @block.gpsimd
                def _(eng):
                    replay('pool', eng)

                @block.sync
                def _(eng):
                    replay('sp', eng)


class Pool:
    def __init__(self, kb, name, shape, dtype, n, psum=False):
        self.tiles = []
        for i in range(n):
            nm = '%s%d' % (name, i)
            t = kb.ps(nm, shape, dtype) if psum else kb.sb(nm, shape, dtype)
            self.tiles.append((t, nm))
        self.i = 0

    def get(self):
        t = self.tiles[self.i % len(self.tiles)]
        self.i += 1
        return t


class KB:
    def __init__(self):
        self.nc = bass.Bass("TRN2", target_bir_lowering=False)
        self.S = Sched(self.nc)
        self.es = contextlib.ExitStack()
        self.din = {}
        self.dout = {}

    def inp(self, name, shape, dtype=F32):
        t = self.nc.dram_tensor(name, list(shape), dtype, kind="ExternalInput")
        self.din[name] = t
        return t.ap()

    def outp(self, name, shape, dtype=F32):
        t = self.nc.dram_tensor(name, list(shape), dtype, kind="ExternalOutput")
        self.dout[name] = t
        return t.ap()

    def sb(self, name, shape, dtype):
        return self.es.enter_context(self.nc.sbuf_tensor(name, list(shape), dtype))

    def ps(self, name, shape, dtype):
        return self.es.enter_context(self.nc.psum_tensor(name, list(shape), dtype))

    def dma(self, out, in_, reads=(), writes=(), q='sp', slow=False):
        if slow:
            fn = lambda e: e.dma_start(out=out, in_=in_, allow_slow_non_contiguous=True)
        else:
            fn = lambda e: e.dma_start(out=out, in_=in_)
        return self.S.op(q, fn, reads=reads, writes=writes, dma=True)

    def mm(self, out, lhsT, rhs, start, stop, reads=(), writes=(), signal=None, sgc=False):
        if signal is None:
            signal = stop
        if sgc:
            return self.S.op('pe', lambda e: e.matmul(out, lhsT, rhs, start=start, stop=stop, skip_group_check=True),
                             reads=reads, writes=writes, signal=signal)
        return self.S.op('pe', lambda e: e.matmul(out, lhsT, rhs, start=start, stop=stop),
                         reads=reads, writes=writes, signal=signal)

    def tr(self, out, in_, ident, reads=(), writes=(), signal=True):
        return self.S.op('pe', lambda e: e.transpose(out=out, in_=in_, identity=ident),
                         reads=reads, writes=writes, signal=signal)

    def act(self, out, in_, func, reads=(), writes=(), bias=None, scale=None, accum_out=None):
        kw = {}
        if bias is not None:
            kw['bias'] = bias
        if scale is not None:
            kw['scale'] = scale
        if accum_out is not None:
            kw['accum_out'] = accum_out
        return self.S.op('act', lambda e: e.activation(out=out, in_=in_, func=func, **kw), reads=reads, writes=writes)

    def tt(self, eng, out, in0, in1, op, reads=(), writes=()):
        return self.S.op(eng, lambda e: e.tensor_tensor(out=out, in0=in0, in1=in1, op=op), reads=reads, writes=writes)

    def ts(self, eng, out, in0, s1, s2, op0, op1=None, reads=(), writes=()):
        if op1 is None:
            fn = lambda e: e.tensor_scalar(out=out, in0=in0, scalar1=s1, scalar2=0.0, op0=op0, op1=ALU.add)
        else:
            fn = lambda e: e.tensor_scalar(out=out, in0=in0, scalar1=s1, scalar2=s2, op0=op0, op1=op1)
        return self.S.op(eng, fn, reads=reads, writes=writes)

    def stt(self, eng, out, in0, scalar, in1, op0, op1, reads=(), writes=()):
        return self.S.op(eng, lambda e: e.scalar_tensor_tensor(out=out, in0=in0, scalar=scalar, in1=in1, op0=op0, op1=op1),
                         reads=reads, writes=writes)

    def cp(self, eng, out, in_, reads=(), writes=()):
        if eng == 'act':
            return self.S.op('act', lambda e: e.copy(out=out, in_=in_), reads=reads, writes=writes)
        return self.S.op(eng, lambda e: e.tensor_copy(out=out, in_=in_), reads=reads, writes=writes)

    def memset(self, eng, ap, val, writes=()):
        return self.S.op(eng, lambda e: e.memset(ap, val), writes=writes)

    def rstd(self, out, ss, tmp, inv_n, key):
        self.ts('dve', tmp, ss, inv_n, EPS, ALU.mult, ALU.add, reads=[key], writes=[key])
        self.S.op('act', lambda e: e.sqrt(out=tmp, in_=tmp), reads=[key], writes=[key])
        self.S.op('dve', lambda e: e.reciprocal(out=out, in_=tmp), reads=[key], writes=[key])

    def rsum(self, eng, out, in_, reads=(), writes=()):
        return self.S.op(eng, lambda e: e.reduce_sum(out=out, in_=in_, axis=AX.X), reads=reads, writes=writes)


def bcast_rows(ap_dram_row, nparts):
    t = ap_dram_row
    return bass.AP(tensor=t.tensor, offset=t.offset, ap=[[0, nparts]] + [list(x) for x in t.ap])


def V(ap, off, dims, p0=0, np_=None):
    pstride = ap.ap[0][0]
    npart = ap.ap[0][1] if np_ is None else np_
    return bass.AP(tensor=ap.tensor, offset=ap.offset + p0 * pstride + off,
                   ap=[[pstride, npart]] + [list(d) for d in dims])


class Arena:
    def __init__(self, kb, name, nelem, dtype):
        self.t = kb.sb(name, [128, nelem], dtype)
        self.n = nelem
        self.off = 0
        self.name = name
        self.cnt = 0

    def reset(self):
        self.off = 0

    def alloc(self, shape):
        n = 1
        for d in shape[1:]:
            n *= d
        a = self.off
        self.off = (a + n + 31) // 32 * 32
        assert self.off <= self.n, 'arena %s overflow: %d > %d' % (self.name, self.off, self.n)
        v = self.t[0:shape[0], a:a + n]
        if len(shape) > 2:
            names = 'abcdef'[:len(shape) - 1]
            pat = 'p (%s) -> p %s' % (' '.join(names), ' '.join(names))
            kw = {names[i]: shape[1 + i] for i in range(1, len(names))}
            v = v.rearrange(pat, **kw)
        self.cnt += 1
        return v, '%s_%d' % (self.name, self.cnt)

    def pool(self, shape, n):
        return APPool([self.alloc(shape) for _ in range(n)])


class APPool:
    def __init__(self, tiles):
        self.tiles = tiles
        self.i = 0

    def get(self):
        t = self.tiles[self.i % len(self.tiles)]
        self.i += 1
        return t

import math
import os

SL = 2048
NTOK = 1024 + SL
PI = math.pi
STAGE = int(os.environ.get('KSTAGE', '99'))


def dram_ap(t, off, dims):
    return bass.AP(tensor=t.tensor, offset=t.offset + off, ap=[list(d) for d in dims])


def build():
    kb = KB()
    nc = kb.nc
    S = kb.S
    xp = kb.inp('xp', [1024, D])
    xs = kb.inp('xs', [SL, D])
    cvec = kb.inp('cvec', [2, D])
    ck_d = kb.inp('ck', [512, 128])
    cv_d = kb.inp('cv', [512, 128])
    h0_d = kb.inp('h0', [2 * 2 * 32 * 64])
    ada_w = kb.inp('ada_w', [2, D, 3 * D])
    ada_b = kb.inp('ada_b', [2, 3 * D])
    norm_pre = kb.inp('norm_pre', [2, D])
    norm_post = kb.inp('norm_post', [2, D])
    ab_w_in = kb.inp('ab_w_in', [D, 2304])
    q_gain = kb.inp('q_gain', [64])
    k_gain = kb.inp('k_gain', [64])
    lam_re = kb.inp('lam_re', [4096])
    lam_im = kb.inp('lam_im', [4096])
    log_dt = kb.inp('log_dt', [64])
    b_re = kb.inp('b_re', [65536])
    b_im = kb.inp('b_im', [65536])
    c_re = kb.inp('c_re', [65536])
    c_im = kb.inp('c_im', [65536])
    ssm_d = kb.inp('ssm_d', [512])
    glu_w = kb.inp('glu_w', [512, 512])
    glu_b = kb.inp('glu_b', [512])
    ab_w_out = kb.inp('ab_w_out', [1024, 1024])
    ident_d = kb.inp('ident', [128, 128], BF16)
    identf_d = kb.inp('identf', [128, 128])
    maskf_d = kb.inp('maskf', [128, 128])
    maskb_d = kb.inp('maskb', [128, 128])
    rope_d = kb.inp('rope', [SL, 64])
    cd_w_in = kb.inp('cd_w_in', [D, 2560])
    v_gain = kb.inp('v_gain', [512])
    w_s = kb.inp('w_s', [8, 128, 128])
    b_s = kb.inp('b_s', [8, 128])
    fnet_w = kb.inp('fnet_w', [512, 512])
    cd_w_out = kb.inp('cd_w_out', [1024, 1024])
    c128_d = kb.inp('c128', [128, 128], BF16)
    s128_d = kb.inp('s128', [128, 128], BF16)
    clp_d = kb.inp('clp', [256, 256], BF16)
    slp_d = kb.inp('slp', [256, 256], BF16)
    cls_d = kb.inp('cls', [SL, SL], BF16)
    sls_d = kb.inp('sls', [SL, SL], BF16)
    yp = kb.outp('yp', [1024, D])
    ys = kb.outp('ys', [SL, D])
    nk = kb.outp('nk', [1024, 128])
    nv = kb.outp('nv', [1024, 128])
    ns = kb.outp('ns', [4 * 8192])
    MODS = nc.dram_tensor('MODS', [2, 2, 3, D], F32).ap()
    WBd = nc.dram_tensor('WBd', [128, 8192], BF16).ap()
    WCd = nc.dram_tensor('WCd', [128, 8192], BF16).ap()
    KMd = nc.dram_tensor('KMd', [128, 4096], BF16).ap()
    XSd = nc.dram_tensor('XSd', [3, 128, 4096], BF16).ap()
    MIXAd = nc.dram_tensor('MIXAd', [12, 64, 2048], BF16).ap()
    GBd = nc.dram_tensor('GBd', [12, 128, 1024], BF16).ap()
    X1 = nc.dram_tensor('X1', [NTOK, D], F32).ap()
    HNTd = nc.dram_tensor('HNTd', [3, 128, 8192], BF16).ap()
    ab_w_in_b = nc.dram_tensor('ab_w_in_b', [D, 2304], BF16).ap()
    ab_w_out_b = nc.dram_tensor('ab_w_out_b', [1024, 1024], BF16).ap()
    glu_w_b = nc.dram_tensor('glu_w_b', [512, 512], BF16).ap()
    cd_w_in_b = nc.dram_tensor('cd_w_in_b', [D, 2560], BF16).ap()
    cd_w_out_b = nc.dram_tensor('cd_w_out_b', [1024, 1024], BF16).ap()
    fnet_w_b = nc.dram_tensor('fnet_w_b', [512, 512], BF16).ap()
    w_s_b = nc.dram_tensor('w_s_b', [8, 128, 128], BF16).ap()
    MIXCd = nc.dram_tensor('MIXCd', [12, 128, 1024], BF16).ap()
    GDd = nc.dram_tensor('GDd', [12, 128, 1024], BF16).ap()

    with kb.es:
        sb = kb.sb
        ident = sb('ident_sb', [128, 128], BF16)[:]
        kb.dma(ident, ident_d, writes=['ident'])
        identf = sb('identf_sb', [128, 128], F32)[:]
        kb.dma(identf, identf_d, writes=['identf'])
        kg = sb('kg', [128, 64], F32)[:]
        kb.dma(kg, bcast_rows(k_gain, 128), writes=['kg'])
        qg = sb('qg', [128, 64], F32)[:]
        kb.dma(qg, bcast_rows(q_gain, 128), writes=['qg'])
        kb.ts('dve', qg, qg, 0.125, None, ALU.mult, reads=['qg'], writes=['qg'])
        ones_bf = sb('ones_bf', [128, 64], BF16)[:]
        kb.memset('dve', ones_bf, 1.0, writes=['ones_bf'])
        SHIFT = sb('SHIFT', [128, D], F32)[:]
        G1 = sb('G1', [128, D], F32)[:]
        G2 = sb('G2', [128, D], F32)[:]
        LLt = sb('LLt', [128, 64], F32)[:]
        LXt = sb('LXt', [128, 64], F32)[:]
        HL = sb('HL', [128, 2 * 32 * 257], BF16)[:]
        KT = sb('KT', [128, 2 * 2560], BF16)[:]
        VA = sb('VA', [128, 20 * 256], BF16)[:]
        AF_ = Arena(kb, 'AF', 12800, F32)
        AB_ = Arena(kb, 'AB', 46080, BF16)

        pacc = Pool(kb, 'pacc', [128, 512], F32, 6, psum=True)
        ptr = Pool(kb, 'ptr', [128, 1024], BF16, 2, psum=True)

        def new_phase():
            S.barrier()
            AF_.reset()
            AB_.reset()

        def load_T(dst, dstn, src_rows, n):
            st, stn = AF_.alloc([n, 128])
            kb.dma(st, src_rows, writes=[stn])
            pt, pn = pacc.get()
            kb.tr(pt[:, 0:n], st, identf[0:n, 0:n], reads=[stn, 'identf'], writes=[pn])
            kb.cp('act', dst, pt[:, 0:n], reads=[pn], writes=[dstn])

        condT, cTn = AF_.alloc([128, 8, 2])
        cst, cstn = AF_.alloc([2, D])
        kb.dma(cst, cvec, writes=[cstn])
        for k in range(8):
            pt, pn = pacc.get()
            kb.tr(pt[:, 0:2], cst[:, k * 128:(k + 1) * 128], identf[0:2, 0:2], reads=[cstn, 'identf'], writes=[pn])
            kb.cp('act', condT[:, k, :], pt[:, 0:2], reads=[pn], writes=[cTn])
        kb.act(condT, condT, AF.Silu, reads=[cTn], writes=[cTn])
        modkeys = []
        adaw_pool = AF_.pool([128, 8, 512], 2)
        mb_pool = AF_.pool([2, 512], 2)
        ab_pool = AF_.pool([2, 512], 2)
        np_pool = AF_.pool([2, 512], 2)
        for l in range(2):
            wv = ada_w[l].rearrange("(k p) n -> p k n", p=128)
            for cb in range(6):
                kind, hh = cb // 2, cb % 2
                wt, wn = adaw_pool.get()
                kb.dma(wt, wv[:, :, cb * 512:(cb + 1) * 512], writes=[wn])
                abt, abn = ab_pool.get()
                kb.dma(abt, bcast_rows(ada_b[l, cb * 512:(cb + 1) * 512], 2), writes=[abn])
                pt, pn = pacc.get()
                for k in range(8):
                    kb.mm(pt[0:2, :], condT[:, k, :], wt[:, k, :], start=(k == 0), stop=(k == 7),
                          reads=[cTn, wn], writes=[pn], signal=True)
                mb, mbn_ = mb_pool.get()
                kb.tt('dve', mb, pt[0:2, :], abt, ALU.add, reads=[pn, abn], writes=[mbn_])
                if kind > 0:
                    npt, npn = np_pool.get()
                    src = norm_pre if kind == 1 else norm_post
                    kb.dma(npt, bcast_rows(src[l, hh * 512:(hh + 1) * 512], 2), writes=[npn])
                    if kind == 1:
                        kb.stt('dve', mb, mb, 1.0, npt, ALU.add, ALU.mult, reads=[mbn_, npn], writes=[mbn_])
                    else:
                        kb.tt('dve', mb, mb, npt, ALU.mult, reads=[mbn_, npn], writes=[mbn_])
                mk_ = 'MODS_%d_%d' % (l, cb)
                modkeys.append(mk_)
                kb.dma(MODS[l, :, kind, hh * 512:(hh + 1) * 512], mb, reads=[mbn_], writes=[mk_], q='act')

        wkeys = {}

        def convert_w(name, src_w, dst_w, rows, after):
            ks = []
            for r in range(0, rows, 128):
                k_ = 'wc_%s_%d' % (name, r)
                kb.dma(dst_w[r:r + 128, :], src_w[r:r + 128, :], reads=after, writes=[k_], q='pool')
                ks.append(k_)
            wkeys[name] = ks


        def convert_late():
            convert_w('cd_in', cd_w_in, cd_w_in_b, 1024, [])
            convert_w('cd_out', cd_w_out, cd_w_out_b, 1024, [])
            convert_w('fnet', fnet_w, fnet_w_b, 512, [])
            convert_w('ws', w_s.rearrange("h t s -> (h t) s"), w_s_b.rearrange("h t s -> (h t) s"), 1024, [])

        def load_mod(l, ci):
            kb.dma(SHIFT, bcast_rows(MODS[l, ci, 0, :], 128), reads=modkeys, writes=['SHIFT'])
            kb.dma(G1, bcast_rows(MODS[l, ci, 1, :], 128), reads=modkeys, writes=['G1'])
            kb.dma(G2, bcast_rows(MODS[l, ci, 2, :], 128), reads=modkeys, writes=['G2'])

        new_phase()
        convert_w('ab_in', ab_w_in, ab_w_in_b, 1024, [])
        convert_w('ab_out', ab_w_out, ab_w_out_b, 1024, [])
        convert_w('glu', glu_w, glu_w_b, 512, [])

        def cmul(eng, o_r, o_i, a_r, a_i, b_r, b_i, t1, t2, rk, wk, tk, neg_im=False, t34=None):
            if t34 is not None:
                t3, t4 = t34
                k1, k2, k3, k4 = tk + '_1', tk + '_2', tk + '_3', tk + '_4'
                kb.tt(eng, t1, a_r, b_r, ALU.mult, reads=rk, writes=[k1])
                kb.tt(eng, t2, a_i, b_i, ALU.mult, reads=rk, writes=[k2])
                kb.tt(eng, t3, a_r, b_i, ALU.mult, reads=rk, writes=[k3])
                kb.tt(eng, t4, a_i, b_r, ALU.mult, reads=rk, writes=[k4])
                kb.tt(eng, o_r, t1, t2, ALU.subtract, reads=[k1, k2], writes=wk)
                kb.tt(eng, o_i, t3, t4, ALU.add, reads=[k3, k4], writes=wk)
                return
            kb.tt(eng, t1, a_r, b_r, ALU.mult, reads=rk, writes=[tk])
            kb.tt(eng, t2, a_i, b_i, ALU.mult, reads=rk, writes=[tk])
            kb.tt(eng, o_r, t1, t2, ALU.subtract, reads=[tk], writes=wk)
            kb.tt(eng, t1, a_r, b_i, ALU.mult, reads=rk + wk, writes=[tk])
            kb.tt(eng, t2, a_i, b_r, ALU.mult, reads=rk + wk, writes=[tk])
            if neg_im:
                kb.tt(eng, t1, t1, t2, ALU.add, reads=[tk], writes=[tk])
                kb.ts(eng, o_i, t1, -1.0, None, ALU.mult, reads=[tk], writes=wk)
            else:
                kb.tt(eng, o_i, t1, t2, ALU.add, reads=[tk], writes=wk)

        def sm(n=32):
            return AF_.alloc([128, n])

        LR, LRn = sm()
        LI, LIn = sm()
        LD, LDn = sm()
        load_T(LR, LRn, dram_ap(lam_re, 0, [[128, 32], [1, 128]]), 32)
        load_T(LI, LIn, dram_ap(lam_im, 0, [[128, 32], [1, 128]]), 32)
        ld0, ld0n = AF_.alloc([32, 2])
        kb.dma(ld0, dram_ap(log_dt, 0, [[2, 32], [1, 2]]), writes=[ld0n])
        ld1, ld1n = AF_.alloc([32, 128])
        kb.cp('dve', ld1.rearrange("p (g n) -> p g n", g=2), V(ld0, 0, [[1, 2], [0, 64]]), reads=[ld0n], writes=[ld1n])
        ptl, ptln = pacc.get()
        kb.tr(ptl[:, 0:32], ld1, identf[0:32, 0:32], reads=[ld1n, 'identf'], writes=[ptln])
        kb.cp('act', LD, ptl[:, 0:32], reads=[ptln], writes=[LDn])
        TK = 'tblsmall'
        dt_, _ = sm(); a_, _ = sm(); th, _ = sm(); mag, _ = sm(); arg, _ = sm(); sn, _ = sm(); cs, _ = sm()
        lbr, _ = sm(); lbi, _ = sm(); t1, _ = sm(); t2, _ = sm(); nre, _ = sm(); den, _ = sm()
        kr, _ = sm(); ki, _ = sm(); ibr, _ = sm(); ibi, _ = sm()
        R = [TK, LRn, LIn, LDn]
        W = [TK]
        kb.act(dt_, LD, AF.Exp, reads=R, writes=W)
        kb.tt('dve', a_, LR, dt_, ALU.mult, reads=R, writes=W)
        kb.tt('dve', th, LI, dt_, ALU.mult, reads=R, writes=W)
        kb.act(mag, a_, AF.Exp, reads=R, writes=W)
        kb.ts('dve', arg, th, 1.0 / 16, None, ALU.mult, reads=R, writes=W)
        kb.act(sn, arg, AF.Sin, reads=R, writes=W)
        kb.ts('dve', arg, arg, PI / 2, None, ALU.add, reads=R, writes=W)
        kb.act(cs, arg, AF.Sin, reads=R, writes=W)
        for _it in range(4):
            kb.tt('dve', t1, cs, cs, ALU.mult, reads=R, writes=W)
            kb.tt('dve', t2, sn, sn, ALU.mult, reads=R, writes=W)
            kb.tt('dve', sn, cs, sn, ALU.mult, reads=R, writes=W)
            kb.ts('dve', sn, sn, 2.0, None, ALU.mult, reads=R, writes=W)
            kb.tt('dve', cs, t1, t2, ALU.subtract, reads=R, writes=W)
        kb.tt('dve', lbr, mag, cs, ALU.mult, reads=R, writes=W)
        kb.tt('dve', lbi, mag, sn, ALU.mult, reads=R, writes=W)
        kb.ts('dve', nre, lbr, -1.0, None, ALU.add, reads=R, writes=W)
        kb.tt('dve', t1, LR, LR, ALU.mult, reads=R, writes=W)
        kb.tt('dve', t2, LI, LI, ALU.mult, reads=R, writes=W)
        kb.tt('dve', den, t1, t2, ALU.add, reads=R, writes=W)
        S.op('dve', lambda e: e.reciprocal(out=den, in_=den), reads=R, writes=W)
        kb.tt('dve', t1, nre, LR, ALU.mult, reads=R, writes=W)
        kb.tt('dve', t2, lbi, LI, ALU.mult, reads=R, writes=W)
        kb.tt('dve', t1, t1, t2, ALU.add, reads=R, writes=W)
        kb.tt('dve', kr, t1, den, ALU.mult, reads=R, writes=W)
        kb.tt('dve', t1, lbi, LR, ALU.mult, reads=R, writes=W)
        kb.tt('dve', t2, nre, LI, ALU.mult, reads=R, writes=W)
        kb.tt('dve', t1, t1, t2, ALU.subtract, reads=R, writes=W)
        kb.tt('dve', ki, t1, den, ALU.mult, reads=R, writes=W)
        kb.tt('dve', t1, lbr, lbr, ALU.mult, reads=R, writes=W)
        kb.tt('dve', t2, lbi, lbi, ALU.mult, reads=R, writes=W)
        kb.tt('dve', t1, t1, t2, ALU.add, reads=R, writes=W)
        S.op('dve', lambda e: e.reciprocal(out=t1, in_=t1), reads=R, writes=W)
        kb.tt('dve', ibr, lbr, t1, ALU.mult, reads=R, writes=W)
        kb.stt('dve', ibi, lbi, -1.0, t1, ALU.mult, ALU.mult, reads=R, writes=W)
        PR, _ = AF_.alloc([128, 9, 32]); PIm, _ = AF_.alloc([128, 9, 32])
        QR, _ = AF_.alloc([128, 8, 32]); QI, _ = AF_.alloc([128, 8, 32])
        kb.memset('dve', PR[:, 0, :], 1.0, writes=W)
        kb.memset('dve', PIm[:, 0, :], 0.0, writes=W)
        kb.memset('dve', QR[:, 0, :], 1.0, writes=W)
        kb.memset('dve', QI[:, 0, :], 0.0, writes=W)
        w1, _ = AF_.alloc([128, 4, 32]); w2, _ = AF_.alloc([128, 4, 32]); w3, _ = AF_.alloc([128, 4, 32]); w4, _ = AF_.alloc([128, 4, 32])

        def pw_double(XR, XI, base_r, base_i, nmax):
            kb.cp('dve', XR[:, 1, :], base_r, reads=[TK], writes=['PW'])
            kb.cp('dve', XI[:, 1, :], base_i, reads=[TK], writes=['PW'])
            steps = [(2, 1, 1), (3, 2, 2), (5, 4, nmax - 4)]
            for (d0, mi, n) in steps:
                br_ = V(XR, mi * 32, [[0, n], [1, 32]])
                bi_ = V(XI, mi * 32, [[0, n], [1, 32]])
                cmul('dve', XR[:, d0:d0 + n, :], XI[:, d0:d0 + n, :], XR[:, 1:1 + n, :], XI[:, 1:1 + n, :], br_, bi_,
                     w1[:, 0:n, :], w2[:, 0:n, :], ['PW'], ['PW'], 'pwt', t34=(w3[:, 0:n, :], w4[:, 0:n, :]))

        pw_double(PR, PIm, lbr, lbi, 8)
        pw_double(QR, QI, ibr, ibi, 7)
        kb.cp('dve', t1, t1, reads=['PW', TK], writes=[TK])
        LL3 = LLt.rearrange("p (a c) -> p a c", c=2)
        LX3 = LXt.rearrange("p (a c) -> p a c", c=2)
        kb.cp('dve', LL3[:, :, 0], PR[:, 8, :], reads=[TK], writes=['LLt'])
        kb.cp('dve', LL3[:, :, 1], PR[:, 8, :], reads=[TK], writes=['LLt'])
        kb.ts('dve', LX3[:, :, 0], PIm[:, 8, :], -1.0, None, ALU.mult, reads=[TK], writes=['LXt'])
        kb.cp('dve', LX3[:, :, 1], PIm[:, 8, :], reads=[TK], writes=['LXt'])
        BR, BRn = AF_.alloc([128, 32, 16]); BI, BIn = AF_.alloc([128, 32, 16])
        CR, CRn = AF_.alloc([128, 32, 16]); CI, CIn = AF_.alloc([128, 32, 16])
        pt1, _ = AF_.alloc([128, 16, 8, 16]); pt2, _ = AF_.alloc([128, 16, 8, 16])
        bst, bstn = V(pt1, 0, [[1, 2048]], np_=32), 'tbtmp'
        for (src_b, dstB, dstBn) in ((b_re, BR, BRn), (b_im, BI, BIn)):
            kb.dma(bst, dram_ap(src_b, 0, [[2048, 32], [1, 2048]]), writes=[bstn])
            pt, pn = pacc.get()
            for h in range(16):
                kb.tr(pt[:, h * 32:(h + 1) * 32], V(bst, h, [[16, 128]]), identf[0:32, 0:32], reads=[bstn, 'identf'], writes=[pn], signal=(h == 15))
            kb.cp('act', V(dstB, 0, [[1, 16], [16, 32]]), pt[:, :].rearrange("p (h a) -> p h a", h=16), reads=[pn], writes=[dstBn])
        for (src_c, dstC, dstn) in ((c_re, CR, CRn), (c_im, CI, CIn)):
            Cn, Cnn = AF_.alloc([128, 4, 128])
            for d in range(2):
                for gqb in range(2):
                    for gql in range(8):
                        kb.dma(Cn[gql * 16:(gql + 1) * 16, d * 2 + gqb, :],
                               dram_ap(src_c, d * 32768 + (gqb * 8 + gql) * 2048, [[64, 16], [1024, 2], [1, 64]]),
                               writes=['%s_%d_%d' % (Cnn, d * 2 + gqb, gql)])
            for blk in range(4):
                pt, pn = pacc.get()
                kb.tr(pt[:, 0:128], Cn[:, blk, :], identf, reads=['%s_%d_%d' % (Cnn, blk, q_) for q_ in range(8)] + ['identf'], writes=[pn])
                kb.cp('act', dstC.rearrange("p a h -> p (a h)")[:, blk * 128:(blk + 1) * 128], pt[:, 0:128], reads=[pn], writes=[dstn])
        BBR, _ = AF_.alloc([128, 32, 16]); BBI, _ = AF_.alloc([128, 32, 16])
        tb1 = V(pt2, 0, [[16, 32], [1, 16]])
        tb2 = V(pt2, 512, [[16, 32], [1, 16]])
        bc = lambda t, off, n: V(t, off, [[1, n], [0, 16]])
        cmul('dve', BBR, BBI, bc(kr, 0, 32), bc(ki, 0, 32), BR, BI, tb1, tb2, [TK, BRn, BIn], [TK], 'tbtmp')
        RaR, _ = AF_.alloc([128, 8, 32]); RaI, _ = AF_.alloc([128, 8, 32])
        RbR, _ = AF_.alloc([128, 8, 32]); RbI, _ = AF_.alloc([128, 8, 32])
        for k in range(8):
            kb.cp('dve', RaR[:, k, :], PR[:, 7 - k, :], reads=[TK], writes=['Rrev'])
            kb.cp('dve', RaI[:, k, :], PIm[:, 7 - k, :], reads=[TK], writes=['Rrev'])
            kb.cp('dve', RbR[:, k, :], PR[:, 8 - k, :], reads=[TK], writes=['Rrev'])
            kb.cp('dve', RbI[:, k, :], PIm[:, 8 - k, :], reads=[TK], writes=['Rrev'])
        TB, TBn = AB_.alloc([128, 2, 2, 16, 8, 16])
        AFt, AFn = AB_.alloc([128, 2, 16, 8, 16])
        BMF, BMFn = AB_.alloc([128, 2, 16, 8, 16])
        BMB, BMBn = AB_.alloc([128, 2, 16, 8, 16])
        WC, WCn = AB_.alloc([128, 2, 16, 2, 8, 16])

        def prod(o_r, o_i, pw_r, pw_i, koff, d, x_r, x_i, wkey, neg_im):
            pr = V(pw_r, koff * 32 + d * 16, [[1, 16], [32, 8], [0, 16]])
            pi_ = V(pw_i, koff * 32 + d * 16, [[1, 16], [32, 8], [0, 16]])
            xr = V(x_r, d * 256, [[16, 16], [0, 8], [1, 16]])
            xi = V(x_i, d * 256, [[16, 16], [0, 8], [1, 16]])
            cmul('dve', o_r, o_i, pr, pi_, xr, xi, pt1, pt2, [TK, CRn, CIn, 'Rrev'], [wkey], 'tbtmp', neg_im=neg_im)

        prod(TB[:, 0, 0], TB[:, 0, 1], RaR, RaI, 0, 0, BBR, BBI, TBn, False)
        prod(TB[:, 1, 0], TB[:, 1, 1], PR, PIm, 0, 1, BBR, BBI, TBn, False)
        prod(AFt[:, 0], AFt[:, 1], QR, QI, 0, 0, BBR, BBI, AFn, False)
        prod(BMF[:, 0], BMF[:, 1], PR, PIm, 0, 0, CR, CI, BMFn, True)
        prod(BMB[:, 0], BMB[:, 1], QR, QI, 0, 1, CR, CI, BMBn, True)
        prod(WC[:, 0, :, 0], WC[:, 0, :, 1], PR, PIm, 1, 0, CR, CI, WCn, True)
        prod(WC[:, 1, :, 0], WC[:, 1, :, 1], RbR, RbI, 0, 1, CR, CI, WCn, True)
        kb.dma(WCd, WC.rearrange("p d q c t h -> p (d q c t h)"), reads=[WCn], writes=['WCd'])
        WBt, WBn = AB_.alloc([128, 2, 16, 2, 128])
        for d in range(2):
            for qb in range(4):
                pt, pn = ptr.get()
                for qi in range(4):
                    gq = qb * 4 + qi
                    for c in range(2):
                        kb.tr(pt[:, (qi * 2 + c) * 128:(qi * 2 + c + 1) * 128],
                              TB[:, d, c, gq, :, :].rearrange("p s h -> p (s h)"), ident,
                              reads=[TBn, 'ident'], writes=[pn], signal=(qi == 3 and c == 1))
                kb.cp('act', WBt[:, d, qb * 4:(qb + 1) * 4, :, :].rearrange("p q c n -> p (q c n)"), pt[:, :],
                      reads=[pn], writes=[WBn])
        kb.dma(WBd, WBt.rearrange("p d q c n -> p (d q c n)"), reads=[WBn], writes=['WBd'])
        maskf, mfn = AF_.alloc([128, 128]); maskb, mbn = AF_.alloc([128, 128])
        kb.dma(maskf, maskf_d, writes=[mfn])
        kb.dma(maskb, maskb_d, writes=[mbn])
        dcol, dcn = AF_.alloc([128, 32])
        dc0, dc0n = AF_.alloc([32, 16])
        kb.dma(dc0, dram_ap(ssm_d, 0, [[16, 32], [1, 16]]), writes=[dc0n])
        dc1, dc1n = AF_.alloc([32, 128])
        kb.cp('dve', dc1.rearrange("p (s h) -> p s h", s=8), V(dc0, 0, [[0, 8], [1, 16]]), reads=[dc0n], writes=[dc1n])
        ptd, ptdn = pacc.get()
        kb.tr(ptd[:, 0:32], dc1, identf[0:32, 0:32], reads=[dc1n, 'identf'], writes=[ptdn])
        kb.cp('act', dcol, ptd[:, 0:32], reads=[ptdn], writes=[dcn])
        KM, KMn = AB_.alloc([128, 32, 128])
        km1, km1n = AF_.alloc([128, 128]); km2, km2n = AF_.alloc([128, 128])
        for g in range(32):
            gq, gh = g // 2, g % 2
            hs = slice(gh * 64, (gh + 1) * 64)
            pt, pn = pacc.get()
            fl = lambda t: t.rearrange("p s h -> p (s h)")
            kb.mm(pt[:, 0:128], fl(AFt[hs, 0, gq]), fl(BMF[hs, 0, gq]), True, False, reads=[AFn, BMFn], writes=[pn], signal=False)
            kb.mm(pt[:, 0:128], fl(AFt[hs, 1, gq]), fl(BMF[hs, 1, gq]), False, True, reads=[AFn, BMFn], writes=[pn], signal=False)
            kb.mm(pt[:, 128:256], fl(TB[hs, 1, 0, gq]), fl(BMB[hs, 0, gq]), True, False, reads=[TBn, BMBn], writes=[pn], signal=False)
            kb.mm(pt[:, 128:256], fl(TB[hs, 1, 1, gq]), fl(BMB[hs, 1, gq]), False, True, reads=[TBn, BMBn], writes=[pn], signal=True)
            kb.tt('dve', km1, pt[:, 0:128], maskf, ALU.mult, reads=[pn, mfn], writes=[km1n])
            kb.tt('dve', km2, pt[:, 128:256], maskb, ALU.mult, reads=[pn, mbn], writes=[km2n])
            kb.tt('dve', km1, km1, km2, ALU.add, reads=[km1n, km2n], writes=[km1n])
            kb.stt('dve', KM[:, g, :], identf, dcol[:, g:g + 1], km1, ALU.mult, ALU.add, reads=['identf', dcn, km1n], writes=[KMn])
        kb.dma(KMd, KM.rearrange("p g n -> p (g n)"), reads=[KMn], writes=['KMd'])

        HL4 = HL.rearrange("p (d a n) -> p d a n", d=2, a=32)

        def hn_tile(xrows, hnT, hnTn, col0, pools):
            xt_pool, junk_pool, hn_pool, sm_pool = pools
            xt, xn = xt_pool.get()
            kb.dma(xt, xrows, writes=[xn])
            jk, jn = junk_pool.get()
            smt, sn_ = sm_pool.get()
            kb.memset('dve', smt[:, 0:1], 0.0, writes=[sn_])
            kb.act(jk, xt, AF.Square, reads=[xn, sn_], writes=[jn, sn_], accum_out=smt[:, 0:1])
            kb.rstd(smt[:, 2:3], smt[:, 0:1], smt[:, 1:2], 1.0 / D, sn_)
            kb.stt('dve', xt, xt, smt[:, 2:3], G1, ALU.mult, ALU.mult, reads=[xn, sn_, 'G1'], writes=[xn])
            hn, hnn = hn_pool.get()
            kb.tt('dve', hn, xt, SHIFT, ALU.add, reads=[xn, 'SHIFT'], writes=[hnn])
            pt, pn = ptr.get()
            for k in range(8):
                kb.tr(pt[:, k * 128:(k + 1) * 128], hn[:, k * 128:(k + 1) * 128], ident,
                      reads=[hnn, 'ident'], writes=[pn], signal=(k == 7))
            kb.cp('act', hnT[:, :, col0:col0 + 128], pt[:, :].rearrange("p (k t) -> p k t", k=8),
                  reads=[pn], writes=[hnTn])

        def rope(eng, out, x, nh, cs_tile, t1_, t2_, rk, wk, tk):
            def xv(t, half):
                return V(t, half * 16, [[64, nh], [32, 2], [1, 16]])
            cosv = V(cs_tile, 0, [[0, nh], [16, 2], [1, 16]])
            sinv = V(cs_tile, 32, [[0, nh], [16, 2], [1, 16]])
            a = V(t1_, 0, [[32, nh], [16, 2], [1, 16]])
            b = V(t2_, 0, [[32, nh], [16, 2], [1, 16]])
            kb.tt(eng, a, xv(x, 0), cosv, ALU.mult, reads=rk, writes=[tk])
            kb.tt(eng, b, xv(x, 1), sinv, ALU.mult, reads=rk, writes=[tk])
            kb.tt(eng, xv(out, 0), a, b, ALU.subtract, reads=[tk], writes=wk)
            kb.tt(eng, a, xv(x, 0), sinv, ALU.mult, reads=rk + wk, writes=[tk])
            kb.tt(eng, b, xv(x, 1), cosv, ALU.mult, reads=rk + wk, writes=[tk])
            kb.tt(eng, xv(out, 1), a, b, ALU.add, reads=[tk], writes=wk)

        KT3 = KT.rearrange("p (h n) -> p h n", h=2)
        kb.memset('pool', KT[64:128, :], 0.0, writes=['KT'])
        VO = VA.rearrange("p (t h n) -> p t h n", h=2, n=128)
        kb.memset('pool', VO[:, :, :, 64:128], 1.0, writes=['VA'])

        def phase_A1(sus, is_sample, hook=None):
            new_phase()
            if hook is not None:
                hook()
            ci = 1 if is_sample else 0
            load_mod(0, ci)
            WA, WAn = AB_.alloc([128, 8, 768])
            S.groups[WAn] = [WAn + '#0', WAn + '#1']
            w0v = ab_w_in_b.rearrange("(k p) n -> p k n", p=128)
            kb.dma(WA[:, :, 0:256], w0v[:, :, 512:768], reads=wkeys['ab_in'], writes=[WAn + '#0'])
            kb.dma(WA[:, :, 256:768], w0v[:, :, 1280:1792], reads=wkeys['ab_in'], writes=[WAn + '#1'], q='act')
            WB, WBn_ = AB_.alloc([128, 2, 16, 2, 128])
            kb.dma(WB.rearrange("p d q c n -> p (d q c n)"), WBd, reads=['WBd'], writes=[WBn_])
            hnT, hnTn = AB_.alloc([128, 8, 1024])
            Ub, Ubn = AB_.alloc([128, 32, 8, 16])
            X, Xn = AB_.alloc([128, 32, 128])
            pools = (AF_.pool([128, D], 3), AB_.pool([128, D], 2), AB_.pool([128, D], 2), AF_.pool([128, 8], 4))
            sq_pool = AF_.pool([128, 128], 2)
            kvf_pool = AF_.pool([128, 256], 2)
            kb_pool = AB_.pool([128, 128], 8)
            rp_pool = AF_.pool([128, 64], 2)
            rt_pool = AF_.pool([128, 64], 2)
            sm_pool = pools[3]
            for su in sus:
                xsrc = xs if is_sample else xp
                row0 = (su - 1) * 1024 if is_sample else 0
                deferred = []
                for t in range(8):
                    hn_tile(xsrc[row0 + t * 128: row0 + (t + 1) * 128, :], hnT, hnTn, t * 128, pools)
                kb.dma(HNTd[su], hnT.rearrange("p k n -> p (k n)"), reads=[hnTn], writes=['HNTd%d' % su], q='act')
                for t in range(8):
                    r0 = row0 + t * 128
                    pt, pn = pacc.get()
                    for k in range(8):
                        kb.mm(pt[:, 0:256], hnT[:, k, t * 128:(t + 1) * 128], WA[:, k, 0:256], start=(k == 0), stop=(k == 7),
                              reads=[hnTn, WAn], writes=[pn])
                    sq, sqn = sq_pool.get()
                    smt, sn_ = sm_pool.get()
                    kb.act(sq, pt[:, 0:128], AF.Square, reads=[pn], writes=[sqn])
                    kb.rsum('dve', smt[:, 0:2], sq.rearrange("p (h d) -> p h d", h=2), reads=[sqn], writes=[sn_])
                    kb.rstd(smt[:, 4:6], smt[:, 0:2], smt[:, 2:4], 1.0 / 64, sn_)
                    kvf, kvn = kvf_pool.get()
                    for h in range(2):
                        kb.stt('dve', kvf[:, h * 64:(h + 1) * 64], pt[:, h * 64:(h + 1) * 64], smt[:, 4 + h:5 + h], kg,
                               ALU.mult, ALU.mult, reads=[pn, sn_, 'kg'], writes=[kvn])
                    kb.cp('act', kvf[:, 128:256], pt[:, 128:256], reads=[pn], writes=[kvn])
                    kbt, kbn = kb_pool.get()
                    if is_sample:
                        rp, rpn = rp_pool.get()
                        kb.dma(rp, rope_d[r0:r0 + 128, :], writes=[rpn])
                        rt, rtn = rt_pool.get()
                        ra, ran = rt_pool.get()
                        rope('dve', kbt, kvf[:, 0:128], 2, rp, rt, ra, [kvn, rpn], [kbn], rtn)
                        key0 = 512 + r0
                    else:
                        kb.dma(nk[r0:r0 + 128, :], kvf[:, 0:128], reads=[kvn], q='act')
                        kb.dma(nv[r0:r0 + 128, :], kvf[:, 128:256], reads=[kvn], q='act')
                        kb.cp('act', kbt, kvf[:, 0:128], reads=[kvn], writes=[kbn])
                        key0 = r0
                    deferred.append((kbt, kbn, key0))
                    kb.cp('dve', VO[:, key0 // 128, :, 0:64], kvf[:, 128:256].rearrange("p (h d) -> p h d", h=2), reads=[kvn], writes=['VA'])
                for s_ in range(8):
                    pt, pn = pacc.get()
                    for k in range(8):
                        lhs = V(hnT, k * 1024 + s_, [[8, 128]])
                        kb.mm(pt[:, :], lhs, WA[:, k, 256:768], start=(k == 0), stop=(k == 7), reads=[hnTn, WAn], writes=[pn])
                    eng = 'act' if s_ % 2 == 0 else 'dve'
                    kb.cp(eng, Ub[:, :, s_, :], pt[:, :].rearrange("p (g h) -> p g h", g=32), reads=[pn], writes=[Ubn])
                for (kbt, kbn, key0) in deferred:
                    ptt, ptn = ptr.get()
                    for h in range(2):
                        kb.tr(ptt[0:64, h * 128:(h + 1) * 128], kbt[:, h * 64:(h + 1) * 64], ident,
                              reads=[kbn, 'ident'], writes=[ptn], signal=(h == 1))
                    kb.cp('act', KT3[0:64, :, key0:key0 + 128], ptt[0:64, 0:256].rearrange("p (h n) -> p h n", h=2),
                          reads=[ptn], writes=['KT'])
                for gb in range(4):
                    pt, pn = ptr.get()
                    for gi in range(8):
                        g = gb * 8 + gi
                        kb.tr(pt[:, gi * 128:(gi + 1) * 128], Ub[:, g, :, :].rearrange("p s h -> p (s h)"), ident,
                              reads=[Ubn, 'ident'], writes=[pn], signal=(gi == 7))
                    eng = 'act' if gb % 2 == 0 else 'dve'
                    kb.cp(eng, X[:, gb * 8:(gb + 1) * 8, :].rearrange("p g n -> p (g n)"), pt[:, :], reads=[pn], writes=[Xn])
                kb.dma(XSd[su], X.rearrange("p g n -> p (g n)"), reads=[Xn], writes=['XSd%d' % su], q='act')
                if is_sample:
                    NC_ = 257
                else:
                    NC_ = 132
                for d in range(2):
                    for qb in range(8):
                        pt, pn = pacc.get()
                        for qi in range(2):
                            gq = qb * 2 + qi
                            for c in range(2):
                                col = (qi * 2 + c) * 128
                                for gh in range(2):
                                    kb.mm(pt[gh * 64:(gh + 1) * 64, col:col + 128], WB[:, d, gq, c, gh * 64:(gh + 1) * 64],
                                          X[:, 2 * gq + gh, :], start=True, stop=True, reads=[WBn_, Xn], writes=[pn],
                                          signal=(qi == 1 and c == 1 and gh == 1))
                        base = d * 32 * NC_ + (qb * 4) * NC_
                        if is_sample:
                            c0 = (su - 1) * 128 + (1 if d == 0 else 0)
                            outv = V(HL, base + c0, [[NC_, 4], [1, 128]])
                            inv = pt[:, :].rearrange("p (a n) -> p a n", a=4)
                        else:
                            outv = V(HL, (d * 32 + qb * 4) * 128, [[128, 4], [1, 128]])
                            inv = pt[:, :].rearrange("p (a n) -> p a n", a=4)
                        eng = 'act' if qb % 2 == 0 else 'dve'
                        kb.cp(eng, outv, inv, reads=[pn], writes=['HL'])

        def scan_body(is_sample):
            nseq = 1 if is_sample else 4
            J = 256 if is_sample else 32
            NC_ = 257 if is_sample else 132
            ST, STn = AF_.alloc([128, 2, 32, nseq])
            T1, T1n = AF_.alloc([128, 2, 32, nseq])
            T2, T2n = AF_.alloc([128, 2, 32, nseq])
            if is_sample:
                Hn, Hnn = AF_.alloc([64, 128])
                kb.dma(Hn, dram_ap(h0_d, 0, [[128, 64], [1, 128]]), writes=[Hnn])
                pt, pn = pacc.get()
                kb.tr(pt[:, 0:64], Hn, identf[0:64, 0:64], reads=[Hnn, 'identf'], writes=[pn])
                kb.cp('act', V(ST, 0, [[32, 2], [1, 2], [2, 16]]), pt[:, 0:64].rearrange("p (d c q) -> p d c q", d=2, c=2),
                      reads=[pn], writes=[STn])
            else:
                kb.memset('pool', ST, 0.0, writes=[STn])
            LLv = V(LLt, 0, [[32, 2], [1, 32], [0, nseq]])
            if is_sample:
                for d in range(2):
                    colv = V(HL, d * 32 * NC_ + (0 if d == 0 else J), [[NC_, 32], [J + 1, nseq]])
                    kb.cp('pool', colv, ST[:, d, :, :], reads=[STn], writes=['HL'])
            else:
                kb.memset('pool', V(HL, 8192, [[128, 32], [32, 4]]), 0.0, writes=['HL'])
                kb.memset('pool', V(HL, 8192 + 32 * 128 + 31, [[128, 32], [32, 4]]), 0.0, writes=['HL'])
            for i in range(J):
                if is_sample:
                    cf, cb_ = i + 1, J - 1 - i
                    hv_in = V(HL, cf, [[32 * NC_ + cb_ - cf, 2], [NC_, 32], [J + 1, nseq]])
                    hv_out = hv_in
                else:
                    hv_in = V(HL, i, [[32 * 128 + 31 - 2 * i, 2], [128, 32], [32, 4]])
                    hv_out = V(HL, 8192 + i + 1, [[32 * 128 + 29 - 2 * i, 2], [128, 32], [32, 4]]) if i < J - 1 else None
                kb.tt('pool', T1, ST, LLv, ALU.mult, reads=[STn, 'LLt'], writes=[T1n])
                for c in range(2):
                    stv = V(ST, (1 - c) * nseq, [[32 * nseq, 2], [2 * nseq, 16], [1, nseq]])
                    t2v = V(T2, c * nseq, [[32 * nseq, 2], [2 * nseq, 16], [1, nseq]])
                    lxv = V(LXt, c, [[32, 2], [2, 16], [0, nseq]])
                    kb.tt('pool', t2v, stv, lxv, ALU.mult, reads=[STn, 'LXt'], writes=[T2n])
                kb.tt('pool', T1, T1, T2, ALU.add, reads=[T1n, T2n], writes=[T1n])
                kb.tt('pool', ST, T1, hv_in, ALU.add, reads=[T1n, 'HL'], writes=[STn])
                if hv_out is not None:
                    kb.cp('pool', hv_out, ST, reads=[STn], writes=['HL'])
            return ST, STn

        def scan_finals(ST, STn):
            if True:
                STp, STpn = AF_.alloc([128, 256])
                nseq = 4
                for sq_ in range(4):
                    kb.cp('pool', V(STp, sq_ * 64, [[32, 2], [16, 2], [1, 16]]), V(ST, sq_, [[128, 2], [4, 2], [8, 16]]),
                          reads=[STn], writes=[STpn])
                FT, FTn = AF_.alloc([128, 256])
                for ch in range(2):
                    pt, pn = pacc.get()
                    kb.tr(pt[:, 0:128], STp[:, ch * 128:(ch + 1) * 128], identf, reads=[STpn, 'identf'], writes=[pn])
                    kb.cp('act', FT[:, ch * 128:(ch + 1) * 128], pt[:, 0:128], reads=[pn], writes=[FTn])
                    kb.dma(dram_ap(ns, ch * 16384, [[128, 128], [1, 128]]), FT[:, ch * 128:(ch + 1) * 128], reads=[FTn])

        pS = APPool(pacc.tiles[0:3])
        pO = APPool(pacc.tiles[3:5])
        pD = APPool(pacc.tiles[5:6])

        def load_cache():
            ckb, ckn = AB_.alloc([128, 4, 128])
            for t in range(4):
                kb.dma(ckb[:, t, :], ck_d[t * 128:(t + 1) * 128, :], writes=[ckn], q='pool')
                kb.dma(VO[:, t, :, 0:64], cv_d[t * 128:(t + 1) * 128, :].rearrange("p (h d) -> p h d", h=2), writes=['VA'], q='pool')
            for t in range(4):
                ptt, ptn = ptr.get()
                for h in range(2):
                    kb.tr(ptt[0:64, h * 128:(h + 1) * 128], ckb[:, t, h * 64:(h + 1) * 64], ident,
                          reads=[ckn, 'ident'], writes=[ptn], signal=(h == 1))
                kb.cp('act', KT3[0:64, :, t * 128:(t + 1) * 128], ptt[0:64, 0:256].rearrange("p (h n) -> p h n", h=2),
                      reads=[ptn], writes=['KT'])

        def phase_A2a(units, is_sample):
            new_phase()
            ci = 1 if is_sample else 0
            load_mod(0, ci)
            WQ, WQn = AB_.alloc([128, 8, 1536])
            S.groups[WQn] = [WQn + '#0', WQn + '#1', WQn + '#2']
            w0v = ab_w_in_b.rearrange("(k p) n -> p k n", p=128)
            kb.dma(WQ[:, :, 0:512], w0v[:, :, 0:512], reads=wkeys['ab_in'], writes=[WQn + '#0'])
            kb.dma(WQ[:, :, 512:1024], w0v[:, :, 768:1280], reads=wkeys['ab_in'], writes=[WQn + '#1'], q='act')
            kb.dma(WQ[:, :, 1024:1536], w0v[:, :, 1792:2304], reads=wkeys['ab_in'], writes=[WQn + '#2'])
            ST_, STn_ = scan_body(is_sample)
            NQ = 512 if is_sample else 256
            nu = NQ // 256
            cpb = 512 // NQ
            pools = (AF_.pool([128, D], 3), AB_.pool([128, D], 2), AB_.pool([128, D], 2), AF_.pool([128, 32], 4))
            sm_pool = pools[3]
            hnT_pool = AB_.pool([128, 8, NQ], 1)
            sq_pool = AF_.pool([128, 512], 2)
            qn_pool = AF_.pool([128, 512], 2)
            qb_pool = AB_.pool([128, 512], 2)
            rp_pool = AF_.pool([128, 64], 2)
            rt_pool = AF_.pool([128, 256], 2)
            qT_pool = AB_.pool([128, 8, NQ], 1)
            kb.memset('dve', qT_pool.tiles[0][0][64:128], 0.0, writes=[qT_pool.tiles[0][1]])
            GA_pool = AB_.pool([64, 8, NQ], 1)
            GB_pool = AB_.pool([128, 4, NQ], 1)
            MA_pool = AB_.pool([64, 8, NQ], 1)
            PT_pool = AB_.pool([128, 512], 4)
            of_pool = AF_.pool([128, NQ], 2)
            rd_pool = AF_.pool([64, NQ], 2)
            ot_pool = AF_.pool([64, NQ], 2)
            for gi in range(len(units) // nu):
                gu = units[gi * nu:(gi + 1) * nu]
                u = gu[0]
                row0 = (u - 4) * 256 if is_sample else u * 256
                xsrc = xs if is_sample else xp
                hnT, hnTn = hnT_pool.get()
                su_ = (1 + (u - 4) // 4) if is_sample else 0
                col0_ = ((u - 4) % 4) * 256 if is_sample else u * 256
                kb.dma(hnT, HNTd[su_].rearrange("p (k n) -> p k n", k=8)[:, :, col0_:col0_ + NQ], reads=['HNTd%d' % su_], writes=[hnTn])
                qT, qTn = qT_pool.get()
                for t in range(NQ // 128):
                    pt, pn = pS.get()
                    for k in range(8):
                        kb.mm(pt[:, :], hnT[:, k, t * 128:(t + 1) * 128], WQ[:, k, 0:512], start=(k == 0), stop=(k == 7),
                              reads=[hnTn, WQn], writes=[pn])
                    sq, sqn = sq_pool.get()
                    smt, sn_ = sm_pool.get()
                    kb.act(sq, pt[:, :], AF.Square, reads=[pn], writes=[sqn])
                    kb.rsum('dve', smt[:, 0:8], sq.rearrange("p (h d) -> p h d", h=8), reads=[sqn], writes=[sn_])
                    kb.rstd(smt[:, 16:24], smt[:, 0:8], smt[:, 8:16], 1.0 / 64, sn_)
                    qn, qnn = qn_pool.get()
                    kb.tt('dve', qn.rearrange("p (h d) -> p h d", h=8), pt[:, :].rearrange("p (h d) -> p h d", h=8),
                          V(smt, 16, [[1, 8], [0, 64]]), ALU.mult, reads=[pn, sn_], writes=[qnn])
                    qb, qbn = qb_pool.get()
                    if is_sample:
                        kb.tt('dve', qn.rearrange("p (h d) -> p h d", h=8), qn.rearrange("p (h d) -> p h d", h=8),
                              V(qg, 0, [[0, 8], [1, 64]]), ALU.mult, reads=[qnn, 'qg'], writes=[qnn])
                        rp, rpn = rp_pool.get()
                        kb.dma(rp, rope_d[row0 + t * 128: row0 + (t + 1) * 128, :], writes=[rpn])
                        rt, rtn = rt_pool.get()
                        ra, ran = rt_pool.get()
                        rope('dve', qb, qn, 8, rp, rt, ra, [qnn, rpn], [qbn], rtn)
                    else:
                        kb.tt('dve', qb.rearrange("p (h d) -> p h d", h=8), qn.rearrange("p (h d) -> p h d", h=8),
                              V(qg, 0, [[0, 8], [1, 64]]), ALU.mult, reads=[qnn, 'qg'], writes=[qbn])
                    ptt, ptn = ptr.get()
                    for h in range(8):
                        kb.tr(ptt[0:64, h * 128:(h + 1) * 128], qb[:, h * 64:(h + 1) * 64], ident,
                              reads=[qbn, 'ident'], writes=[ptn], signal=(h == 7))
                    kb.cp('act', qT[0:64, :, t * 128:(t + 1) * 128], ptt[0:64, :].rearrange("p (h n) -> p h n", h=8),
                          reads=[ptn], writes=[qTn])
                GA, GAn = GA_pool.get()
                for hb in range(8 // cpb):
                    pt, pn = pS.get()
                    for hi in range(cpb):
                        h = hb * cpb + hi
                        for k in range(8):
                            kb.mm(pt[0:64, hi * NQ:(hi + 1) * NQ], WQ[:, k, 512 + h * 64:512 + (h + 1) * 64], hnT[:, k, :],
                                  start=(k == 0), stop=(k == 7), reads=[hnTn, WQn], writes=[pn], signal=(k == 7 and hi == cpb - 1))
                    kb.act(GA[:, hb * cpb:(hb + 1) * cpb, :].rearrange("p h n -> p (h n)"), pt[0:64, :], AF.Silu, reads=[pn], writes=[GAn])
                GB, GBn = GB_pool.get()
                for cb in range(4 // cpb):
                    pt, pn = pS.get()
                    for ci_ in range(cpb):
                        cc = cb * cpb + ci_
                        for k in range(8):
                            kb.mm(pt[:, ci_ * NQ:(ci_ + 1) * NQ], WQ[:, k, 1024 + cc * 128:1024 + (cc + 1) * 128], hnT[:, k, :],
                                  start=(k == 0), stop=(k == 7), reads=[hnTn, WQn], writes=[pn], signal=(k == 7 and ci_ == cpb - 1))
                    kb.act(GB[:, cb * cpb:(cb + 1) * cpb, :].rearrange("p c n -> p (c n)"), pt[:, :], AF.Silu, reads=[pn], writes=[GBn])
                for i_, uu in enumerate(gu):
                    kb.dma(GBd[uu].rearrange("p (c n) -> p c n", c=4), GB[:, :, i_ * 256:(i_ + 1) * 256], reads=[GBn], writes=['GBd%d' % uu], q='act')
                if is_sample:
                    key0, nkc = 0, 20
                else:
                    key0, nkc = u * 256, 2
                MA, MAn = MA_pool.get()
                nb = nkc // cpb
                pending = [None]

                def emit_S(h, b):
                    kvh = h // 4
                    pt, pn = pS.get()
                    for j in range(cpb):
                        kk = key0 + (b * cpb + j) * 128
                        kb.mm(pt[:, j * NQ:(j + 1) * NQ], KT3[:, kvh, kk:kk + 128], qT[:, h, :], start=True, stop=True,
                              reads=['KT', qTn], writes=[pn], signal=(j == cpb - 1))
                    return pt, pn

                def make_epilogue(h, po, pon):
                    def epi():
                        of, ofn = of_pool.get()
                        kb.cp('dve', of, po[:, 0:NQ], reads=[pon], writes=[ofn])
                        pd, pdn = pD.get()
                        kb.mm(pd[0:64, 0:NQ], identf[:, 64:128], of, start=True, stop=True, reads=['identf', ofn], writes=[pdn])
                        rd, rdn = rd_pool.get()
                        S.op('dve', (lambda o_, i_: (lambda e: e.reciprocal(out=o_, in_=i_)))(rd, pd[0:64, 0:NQ]), reads=[pdn], writes=[rdn])
                        ot, otn = ot_pool.get()
                        kb.tt('dve', ot, of[0:64, :], rd, ALU.mult, reads=[ofn, rdn], writes=[otn])
                        kb.tt('dve', MA[:, h, :], ot, GA[:, h, :], ALU.mult, reads=[otn, GAn], writes=[MAn])
                    return epi

                for h in range(8):
                    kvh = h // 4
                    po, pon = pO.get()
                    sq_ = [emit_S(h, 0)]
                    for b_ in range(1, min(3, nb)):
                        sq_.append(emit_S(h, b_))
                    if pending[0] is not None:
                        pending[0]()
                        pending[0] = None
                    for b in range(nb):
                        pt, pn = sq_[b]
                        PT, PTn = PT_pool.get()
                        kb.act(PT, pt[:, :], AF.Exp, reads=[pn], writes=[PTn])
                        if b + 3 < nb:
                            sq_.append(emit_S(h, b + 3))
                        for j in range(cpb):
                            kc = b * cpb + j
                            vt = (key0 // 128) + kc
                            first, last = (kc == 0), (kc == nkc - 1)
                            kb.mm(po[:, 0:NQ], VO[:, vt, kvh, :], PT[:, j * NQ:(j + 1) * NQ],
                                  start=first, stop=last, reads=['VA', PTn], writes=[pon], signal=last)
                    pending[0] = make_epilogue(h, po, pon)
                pending[0]()
                for i_, uu in enumerate(gu):
                    kb.dma(MIXAd[uu].rearrange("p (h n) -> p h n", h=8), MA[:, :, i_ * 256:(i_ + 1) * 256], reads=[MAn], writes=['MIXAd%d' % uu], q='act')
            if not is_sample:
                scan_finals(ST_, STn_)

        def phase_A2b(sus, is_sample, out_ap):
            new_phase()
            ci = 1 if is_sample else 0
            load_mod(0, ci)
            NC_ = 257 if is_sample else 132
            WCt, WCtn = AB_.alloc([128, 2, 16, 2, 128])
            kb.dma(WCt.rearrange("p d q c n -> p (d q c n)"), WCd, reads=['WCd'], writes=[WCtn])
            KMt, KMtn = AB_.alloc([128, 32, 128])
            kb.dma(KMt.rearrange("p g n -> p (g n)"), KMd, reads=['KMd'], writes=[KMtn], q='act')
            X, Xn = AB_.alloc([128, 32, 128])
            WOa, WOan = AB_.alloc([128, 4, 1024])
            WOb, WObn = AB_.alloc([128, 4, 1024])
            WG, WGn = AB_.alloc([128, 4, 512])
            wov = ab_w_out_b.rearrange("(k p) n -> p k n", p=128)
            kb.dma(WOa, wov[:, 0:4, :], reads=wkeys['ab_out'], writes=[WOan])
            kb.dma(WOb, wov[:, 4:8, :], reads=wkeys['ab_out'], writes=[WObn], q='act')
            kb.dma(WG, glu_w_b.rearrange("(k p) n -> p k n", p=128), reads=wkeys['glu'], writes=[WGn])
            glub, glubn = AF_.alloc([128, 4])
            load_T(glub, glubn, dram_ap(glu_b, 0, [[128, 4], [1, 128]]), 4)
            GBt, GBtn = AB_.alloc([128, 4, 1024])
            MAt, MAtn = AB_.alloc([128, 4, 1024])
            Ybm, Ybn = AB_.alloc([128, 8, 512])
            yT, yTn = AB_.alloc([128, 4, 1024])
            mixB, mixBn = GBt, GBtn
            sg_pool = AB_.pool([128, 512], 2)
            xt_pool = AF_.pool([128, D], 2)
            ot_pool = AF_.pool([128, D], 2)
            jk_pool = AB_.pool([128, 512], 2)
            sm_pool = AF_.pool([128, 8], 4)
            for su in sus:
              u0 = su * 4
              kb.dma(X.rearrange("p g n -> p (g n)"), XSd[su], reads=['XSd%d' % su], writes=[Xn])
              for ul in range(4):
                  kb.dma(GBt[:, :, ul * 256:(ul + 1) * 256], GBd[u0 + ul].rearrange("p (c n) -> p c n", c=4),
                         reads=['GBd%d' % (u0 + ul)], writes=[GBtn])
                  for h in range(8):
                      kb.dma(MAt[(h % 2) * 64:(h % 2) * 64 + 64, h // 2, ul * 256:(ul + 1) * 256], MIXAd[u0 + ul][:, h * 256:(h + 1) * 256],
                             reads=['MIXAd%d' % (u0 + ul)], writes=['%s_%d_%d' % (MAtn, ul, h)])
              for gb in range(8):
                  pt, pn = pacc.get()
                  gh = gb % 2
                  hs = slice(gh * 64, (gh + 1) * 64)
                  firstmm = True
                  for gi in range(4):
                      gq = (gb // 2) * 4 + gi
                      osl = slice(gi * 128, (gi + 1) * 128)
                      for d in range(2):
                          for c in range(2):
                              if is_sample:
                                  off = (d * 32 + gq * 2 + c) * 257 + (su - 1) * 128 + (1 if d == 1 else 0)
                              else:
                                  off = 8192 + (d * 32 + gq * 2 + c) * 128
                              lhs = V(HL, off, [[1, 128]], p0=gh * 64, np_=64)
                              kb.mm(pt[:, osl], lhs, WCt[hs, d, gq, c, :], start=firstmm, stop=False,
                                    reads=['HL', WCtn], writes=[pn], signal=False, sgc=True)
                              firstmm = False
                  for gi in range(4):
                      gq = (gb // 2) * 4 + gi
                      g = 2 * gq + gh
                      osl = slice(gi * 128, (gi + 1) * 128)
                      kb.mm(pt[:, osl], X[:, g, 0:128], KMt[:, g, :], start=False, stop=True,
                            reads=[Xn, KMtn], writes=[pn], signal=(gi == 3), sgc=True)
                  outv = V(Ybm, (2 * ((gb // 2) * 4) + (gb % 2)) * 16, [[32, 4], [512, 8], [1, 16]])
                  inv = V(pt[:, :], 0, [[128, 4], [16, 8], [1, 16]])
                  kb.act(outv, inv, AF.Gelu_apprx_tanh, reads=[pn], writes=[Ybn])
              for cc in range(4):
                  ptt, ptn = ptr.get()
                  for t in range(8):
                      kb.tr(ptt[:, t * 128:(t + 1) * 128], Ybm[:, t, cc * 128:(cc + 1) * 128], ident,
                            reads=[Ybn, 'ident'], writes=[ptn], signal=(t == 7))
                  eng = 'act' if cc % 2 == 0 else 'dve'
                  kb.cp(eng, V(yT, cc * 1024, [[1, 8], [8, 128]]), ptt[:, :].rearrange("p (t j) -> p t j", t=8),
                        reads=[ptn], writes=[yTn])
              for nb in range(2):
                  cols = slice(nb * 512, (nb + 1) * 512)
                  for oc in range(4):
                      pt, pn = pacc.get()
                      for cc in range(4):
                          kb.mm(pt[:, :], WG[:, cc, oc * 128:(oc + 1) * 128], yT[:, cc, cols], start=(cc == 0), stop=(cc == 3),
                                reads=[WGn, yTn], writes=[pn])
                      sg, sgn = sg_pool.get()
                      kb.act(sg, pt[:, :], AF.Sigmoid, reads=[pn, glubn], writes=[sgn], bias=glub[:, oc:oc + 1])
                      kb.tt('dve', sg, sg, yT[:, oc, cols], ALU.mult, reads=[sgn, yTn], writes=[sgn])
                      kb.tt('dve', GBt[:, oc, cols], sg, GBt[:, oc, cols], ALU.mult, reads=[sgn, GBtn], writes=[GBtn])
              xsrc = xs if is_sample else xp
              row0 = (su - 1) * 1024 if is_sample else 0
              for t in range(8):
                  tc_ = slice(t * 128, (t + 1) * 128)
                  xt, xn = xt_pool.get()
                  kb.dma(xt, xsrc[row0 + t * 128: row0 + (t + 1) * 128, :], writes=[xn])
                  smt, sn_ = sm_pool.get()
                  kb.memset('dve', smt[:, 0:2], 0.0, writes=[sn_])
                  pts = []
                  for hf in range(2):
                      pt, pn = pacc.get()
                      oc = slice(hf * 512, (hf + 1) * 512)
                      for h in range(4):
                          kb.mm(pt[:, :], MAt[:, h, tc_], WOa[:, h, oc], start=(h == 0), stop=False, reads=['%s_%d_%d' % (MAtn, t // 2, 2 * h + e_) for e_ in range(2)] + [WOan], writes=[pn], signal=False)
                      for cc in range(4):
                          kb.mm(pt[:, :], mixB[:, cc, tc_], WOb[:, cc, oc], start=False, stop=(cc == 3), reads=[mixBn, WObn], writes=[pn], signal=(cc == 3))
                      jk, jn = jk_pool.get()
                      kb.act(jk, pt[:, :], AF.Square, reads=[pn, sn_], writes=[jn, sn_], accum_out=smt[:, hf:hf + 1])
                      pts.append((pt, pn))
                  kb.tt('dve', smt[:, 2:3], smt[:, 0:1], smt[:, 1:2], ALU.add, reads=[sn_], writes=[sn_])
                  kb.rstd(smt[:, 4:5], smt[:, 2:3], smt[:, 3:4], 1.0 / D, sn_)
                  ot, otn = ot_pool.get()
                  for hf in range(2):
                      pt, pn = pts[hf]
                      oc = slice(hf * 512, (hf + 1) * 512)
                      kb.stt('dve', ot[:, oc], pt[:, :], smt[:, 4:5], G2[:, oc], ALU.mult, ALU.mult, reads=[pn, sn_, 'G2'], writes=[otn])
                  kb.tt('dve', ot, ot, xt, ALU.add, reads=[otn, xn], writes=[otn])
                  kb.dma(out_ap[row0 + t * 128: row0 + (t + 1) * 128, :], ot, reads=[otn], writes=['X1_%d_%d' % (su, t)], q='act')

        L0_ONLY = (STAGE == 1)
        phase_A1([0], False)
        phase_A2a([0, 1, 2, 3], False)
        phase_A2b([0], False, yp if L0_ONLY else X1[0:1024, :])
        phase_A1([1, 2], True, hook=lambda: (load_cache(), convert_late()))
        phase_A2a(list(range(4, 12)), True)
        phase_A2b([1, 2], True, ys if L0_ONLY else X1[1024:NTOK, :])

        AA = HL[:, 0:16384].rearrange("p (t c n) -> p t c n", t=16, c=2)

        def phase_B1(sus, is_sample):
            new_phase()
            ci = 1 if is_sample else 0
            load_mod(1, ci)
            W1, W1n = AB_.alloc([128, 8, 2560])
            S.groups[W1n] = [W1n + '#0', W1n + '#1']
            w1v = cd_w_in_b.rearrange("(k p) n -> p k n", p=128)
            kb.dma(W1[:, 0:4, :], w1v[:, 0:4, :], reads=wkeys['cd_in'], writes=[W1n + '#0'])
            kb.dma(W1[:, 4:8, :], w1v[:, 4:8, :], reads=wkeys['cd_in'], writes=[W1n + '#1'], q='act')
            Wsb, Wsbn = AB_.alloc([128, 8, 128])
            kb.dma(Wsb, w_s_b.rearrange("h t s -> t h s"), reads=wkeys['ws'], writes=[Wsbn])
            WsT, WsTn = AB_.alloc([128, 8, 128])
            ptt, ptn = ptr.get()
            for hg in range(8):
                kb.tr(ptt[:, hg * 128:(hg + 1) * 128], Wsb[:, hg, :], ident, reads=[Wsbn, 'ident'], writes=[ptn], signal=(hg == 7))
            kb.cp('act', WsT.rearrange("p h n -> p (h n)"), ptt[:, :], reads=[ptn], writes=[WsTn])
            C128, C128n = AB_.alloc([128, 128])
            S128, S128n = AB_.alloc([128, 128])
            kb.dma(C128, c128_d, writes=[C128n])
            kb.dma(S128, s128_d, writes=[S128n])
            VG, VGn = AF_.alloc([128, 512])
            kb.dma(VG, bcast_rows(v_gain, 128), writes=[VGn])
            bs, bsn = AF_.alloc([128, 8])
            load_T(bs, bsn, b_s, 8)
            pools = (AF_.pool([128, D], 3), AB_.pool([128, D], 2), AB_.pool([128, D], 2), AF_.pool([128, 8], 4))
            sm_pool = pools[3]
            hnT_pool = AB_.pool([128, 8, 256], 2)
            cuG_pool = AB_.pool([128, 512], 2)
            cvg_pool = AF_.pool([128, 512], 2)
            cvn_pool = AB_.pool([128, 512], 2)
            gcS_pool = AB_.pool([128, 512], 2)
            oc_pool = AF_.pool([128, 512], 2)
            ocb_pool = AB_.pool([128, 512], 2)
            jk_pool = AB_.pool([128, 512], 2)
            mixC_pool = AB_.pool([128, 4, 256], 2)
            dzT_pool = AB_.pool([128, 4, 256], 2)
            GD_pool = AB_.pool([128, 4, 256], 2)
            def proj(hnT, hnTn, t, c0):
                pt, pn = pacc.get()
                for k in range(8):
                    kb.mm(pt[:, :], hnT[:, k, t * 128:(t + 1) * 128], W1[:, k, c0:c0 + 512], start=(k == 0), stop=(k == 7),
                          reads=[hnTn, W1n], writes=[pn])
                return pt, pn

            for ul4 in range(4 * len(sus)):
                su, ul = sus[ul4 // 4], ul4 % 4
                g0 = 0 if not is_sample else 1024 + (su - 1) * 1024
                u = su * 4 + ul
                hnT, hnTn = hnT_pool.get()
                for t in range(2):
                    r0 = g0 + ul * 256 + t * 128
                    hn_tile(X1[r0:r0 + 128, :], hnT, hnTn, t * 128, pools)
                mixC, mixCn = mixC_pool.get()
                tiles_ = []
                for t in range(2):
                    pt, pn = proj(hnT, hnTn, t, 0)
                    cuG, cuGn = cuG_pool.get()
                    kb.act(cuG, pt[:, :], AF.Gelu_apprx_tanh, reads=[pn], writes=[cuGn])
                    pt, pn = proj(hnT, hnTn, t, 512)
                    cvg, cvgn = cvg_pool.get()
                    kb.act(cvg, pt[:, :], AF.Gelu_apprx_tanh, reads=[pn], writes=[cvgn])
                    smt, sn_ = sm_pool.get()
                    kb.memset('dve', smt[:, 0:1], 0.0, writes=[sn_])
                    jk, jn = jk_pool.get()
                    kb.act(jk, cvg, AF.Square, reads=[cvgn, sn_], writes=[jn, sn_], accum_out=smt[:, 0:1])
                    kb.rstd(smt[:, 2:3], smt[:, 0:1], smt[:, 1:2], 1.0 / 512, sn_)
                    cvn, cvnn = cvn_pool.get()
                    kb.stt('dve', cvn, cvg, smt[:, 2:3], VG, ALU.mult, ALU.mult, reads=[cvgn, sn_, VGn], writes=[cvnn])
                    pt, pn = proj(hnT, hnTn, t, 1024)
                    gcS, gcSn = gcS_pool.get()
                    kb.act(gcS, pt[:, :], AF.Silu, reads=[pn], writes=[gcSn])
                    tiles_.append((cuG, cuGn, cvn, cvnn, gcS, gcSn))
                dzT, dzTn = dzT_pool.get()
                GD, GDn = GD_pool.get()
                for (c0, dst, dstn, fn_) in ((1536, dzT, dzTn, None), (2048, GD, GDn, AF.Silu)):
                    for cb in range(2):
                        pt, pn = pacc.get()
                        for ci_ in range(2):
                            cc = cb * 2 + ci_
                            for k in range(8):
                                kb.mm(pt[:, ci_ * 256:(ci_ + 1) * 256], W1[:, k, c0 + cc * 128:c0 + (cc + 1) * 128], hnT[:, k, :],
                                      start=(k == 0), stop=(k == 7), reads=[hnTn, W1n], writes=[pn], signal=(k == 7 and ci_ == 1))
                        dv = dst[:, cb * 2:(cb + 1) * 2, :].rearrange("p c n -> p (c n)")
                        if fn_ is None:
                            kb.cp('dve', dv, pt[:, :], reads=[pn], writes=[dstn])
                        else:
                            kb.act(dv, pt[:, :], fn_, reads=[pn], writes=[dstn])
                kb.dma(GDd[u], GD.rearrange("p c n -> p (c n)"), reads=[GDn], writes=['GDd%d' % u], q='act')
                for t in range(2):
                    cuG, cuGn, cvn, cvnn, gcS, gcSn = tiles_[t]
                    pt, pn = pacc.get()
                    for hg in range(8):
                        kb.mm(pt[:, hg * 64:(hg + 1) * 64], WsT[:, hg, :], cvn[:, hg * 64:(hg + 1) * 64], start=True, stop=True,
                              reads=[WsTn, cvnn], writes=[pn], signal=(hg == 7))
                    oc, ocn = oc_pool.get()
                    for hg in range(8):
                        kb.stt('dve', oc[:, hg * 64:(hg + 1) * 64], pt[:, hg * 64:(hg + 1) * 64], bs[:, hg:hg + 1],
                               cuG[:, hg * 64:(hg + 1) * 64], ALU.add, ALU.mult, reads=[pn, bsn, cuGn], writes=[ocn])
                    ocb, ocbn = ocb_pool.get()
                    kb.tt('dve', ocb, oc, gcS, ALU.mult, reads=[ocn, gcSn], writes=[ocbn])
                    tiles_[t] = (ocb, ocbn)
                for t in range(2):
                    tix = (ul * 2 + t) if not is_sample else ((su - 1) * 8 + ul * 2 + t)
                    for cs_, tab, tabn in ((0, C128, C128n), (1, S128, S128n)):
                        pt, pn = pacc.get()
                        for grp in range(4):
                            kb.mm(pt[:, grp * 128:(grp + 1) * 128], dzT[:, grp, t * 128:(t + 1) * 128], tab, start=True, stop=True,
                                  reads=[dzTn, tabn], writes=[pn], signal=(grp == 3))
                        eng = 'act' if cs_ == 0 else 'dve'
                        kb.cp(eng, AA[:, tix, cs_, :], pt[:, :], reads=[pn], writes=['AA'])
                for t in range(2):
                    ocb, ocbn = tiles_[t]
                    ptt, ptn = ptr.get()
                    for cc in range(4):
                        kb.tr(ptt[:, cc * 128:(cc + 1) * 128], ocb[:, cc * 128:(cc + 1) * 128], ident,
                              reads=[ocbn, 'ident'], writes=[ptn], signal=(cc == 3))
                    kb.cp('act', mixC[:, :, t * 128:(t + 1) * 128], ptt[:, 0:512].rearrange("p (c n) -> p c n", c=4),
                          reads=[ptn], writes=[mixCn])
                kb.dma(MIXCd[u], mixC.rearrange("p c n -> p (c n)"), reads=[mixCn], writes=['MIXCd%d' % u], q='act')

        def phase_B2(is_sample):
            new_phase()
            ci = 1 if is_sample else 0
            load_mod(1, ci)
            WF, WFn = AB_.alloc([128, 4, 512])
            WO, WOn = AB_.alloc([128, 8, 1024])
            S.groups[WOn] = [WOn + '#0', WOn + '#1']
            kb.dma(WF, fnet_w_b.rearrange("(k p) n -> p k n", p=128), reads=wkeys['fnet'], writes=[WFn])
            wo1v = cd_w_out_b.rearrange("(k p) n -> p k n", p=128)
            kb.dma(WO[:, 0:4, :], wo1v[:, 0:4, :], reads=wkeys['cd_out'], writes=[WOn + '#0'])
            kb.dma(WO[:, 4:8, :], wo1v[:, 4:8, :], reads=wkeys['cd_out'], writes=[WOn + '#1'], q='act')
            nst = 16 if is_sample else 2
            NT = 512 if is_sample else 256
            H2 = 2 if is_sample else 1
            nsh = nst // H2
            CL_pool = AB_.pool([128, nsh, NT], H2)
            SL_pool = AB_.pool([128, nsh, NT], H2)
            if not is_sample:
                CLt, CLn = CL_pool.get()
                SLt, SLn_ = SL_pool.get()
                kb.dma(CLt, clp_d.rearrange("(st p) t -> p st t", p=128), writes=[CLn])
                kb.dma(SLt, slp_d.rearrange("(st p) t -> p st t", p=128), writes=[SLn_])
            fzT_pool = AB_.pool([128, 4, NT], 2)
            mixC_pool = AB_.pool([128, 4, NT], 2)
            GD_pool = AB_.pool([128, 4, NT], 2)
            xt_pool = AF_.pool([128, D], 2)
            ot_pool = AF_.pool([128, D], 2)
            jk_pool = AB_.pool([128, 512], 2)
            sm_pool = AF_.pool([128, 8], 4)
            nblk = 4
            for blk in range(nblk):
                if is_sample:
                    units = [4 + blk * 2, 4 + blk * 2 + 1]
                    t0 = 0
                    g0 = 1024 + blk * 512
                    out_ap, orow0 = ys, blk * 512
                else:
                    units = [blk]
                    t0 = blk * 2
                    g0 = blk * 256
                    out_ap, orow0 = yp, blk * 256
                mixC, mixCn = mixC_pool.get()
                GD, GDn = GD_pool.get()
                for i_, u in enumerate(units):
                    kb.dma(mixC[:, :, i_ * 256:(i_ + 1) * 256], MIXCd[u].rearrange("p (c n) -> p c n", c=4), reads=['MIXCd%d' % u], writes=[mixCn])
                    kb.dma(GD[:, :, i_ * 256:(i_ + 1) * 256], GDd[u].rearrange("p (c n) -> p c n", c=4), reads=['GDd%d' % u], writes=[GDn])
                fzT, fzTn = fzT_pool.get()
                accs = [pacc.get() for _ in range(4)]
                for hf in range(H2):
                    if is_sample:
                        CLt, CLn = CL_pool.get()
                        SLt, SLn_ = SL_pool.get()
                        rows = slice(hf * nsh * 128, (hf + 1) * nsh * 128)
                        kb.dma(CLt, cls_d[rows, :].rearrange("(st p) t -> p st t", p=128)[:, :, blk * 512:(blk + 1) * 512], writes=[CLn])
                        kb.dma(SLt, sls_d[rows, :].rearrange("(st p) t -> p st t", p=128)[:, :, blk * 512:(blk + 1) * 512], writes=[SLn_])
                    for chk in range(4):
                        pt, pn = accs[chk]
                        for st in range(nsh):
                            for cs_, tab, tabn in ((0, CLt, CLn), (1, SLt, SLn_)):
                                first = (hf == 0 and st == 0 and cs_ == 0)
                                lastg = (st == nsh - 1 and cs_ == 1)
                                kb.mm(pt[:, 0:NT], AA[:, t0 + hf * nsh + st, cs_, chk * 128:(chk + 1) * 128], tab[:, st, :],
                                      start=first, stop=(lastg and hf == H2 - 1), reads=['AA', tabn], writes=[pn], signal=lastg)
                for chk in range(4):
                    pt, pn = accs[chk]
                    eng = 'act' if chk % 2 == 0 else 'dve'
                    kb.cp(eng, fzT[:, chk, :], pt[:, 0:NT], reads=[pn], writes=[fzTn])
                for oc in range(4):
                    pt, pn = pacc.get()
                    for chk in range(4):
                        kb.mm(pt[:, 0:NT], WF[:, chk, oc * 128:(oc + 1) * 128], fzT[:, chk, :], start=(chk == 0), stop=(chk == 3),
                              reads=[WFn, fzTn], writes=[pn])
                    kb.tt('dve', GD[:, oc, :], pt[:, 0:NT], GD[:, oc, :], ALU.mult, reads=[pn, GDn], writes=[GDn])
                for t in range(NT // 128):
                    tc_ = slice(t * 128, (t + 1) * 128)
                    xt, xn = xt_pool.get()
                    kb.dma(xt, X1[g0 + t * 128: g0 + (t + 1) * 128, :], writes=[xn])
                    smt, sn_ = sm_pool.get()
                    kb.memset('dve', smt[:, 0:2], 0.0, writes=[sn_])
                    pts = []
                    for hf in range(2):
                        pt, pn = pacc.get()
                        oc = slice(hf * 512, (hf + 1) * 512)
                        for kc in range(4):
                            kb.mm(pt[:, :], mixC[:, kc, tc_], WO[:, kc, oc], start=(kc == 0), stop=False, reads=[mixCn, WOn], writes=[pn], signal=False)
                        for kc in range(4):
                            kb.mm(pt[:, :], GD[:, kc, tc_], WO[:, 4 + kc, oc], start=False, stop=(kc == 3), reads=[GDn, WOn], writes=[pn], signal=(kc == 3))
                        jk, jn = jk_pool.get()
                        kb.act(jk, pt[:, :], AF.Square, reads=[pn, sn_], writes=[jn, sn_], accum_out=smt[:, hf:hf + 1])
                        pts.append((pt, pn))
                    kb.tt('dve', smt[:, 2:3], smt[:, 0:1], smt[:, 1:2], ALU.add, reads=[sn_], writes=[sn_])
                    kb.rstd(smt[:, 4:5], smt[:, 2:3], smt[:, 3:4], 1.0 / D, sn_)
                    ot, otn = ot_pool.get()
                    for hf in range(2):
                        pt, pn = pts[hf]
                        oc = slice(hf * 512, (hf + 1) * 512)
                        kb.stt('dve', ot[:, oc], pt[:, :], smt[:, 4:5], G2[:, oc], ALU.mult, ALU.mult, reads=[pn, sn_, 'G2'], writes=[otn])
                    kb.tt('dve', ot, ot, xt, ALU.add, reads=[otn, xn], writes=[otn])
                    kb.dma(out_ap[orow0 + t * 128: orow0 + (t + 1) * 128, :], ot, reads=[otn], q='act')

        if not L0_ONLY:
            phase_B1([0], False)
            phase_B2(False)
            phase_B1([1, 2], True)
            phase_B2(True)

        with nc.allow_low_precision("bf16 matmul operands, fp32 accumulation"):
            kb.S.run()
    return kb


_CACHE = {}


def _dft_tables():
    if 'dft' in _CACHE:
        return _CACHE['dft']
    bf = lambda a: np.ascontiguousarray(a.astype(np.float32)).astype(ml_dtypes.bfloat16)
    i128 = np.arange(128, dtype=np.int64)
    m = (i128[:, None] * i128[None, :]) % 128
    a = 2 * np.pi * m / 128.0
    out = {'c128': bf(np.cos(a)), 's128': bf(np.sin(a))}
    for nm, L in (('p', 256), ('s', SL)):
        i = np.arange(L, dtype=np.int64)
        m = (i[:, None] * i[None, :]) % L
        a = 2 * np.pi * m / float(L)
        sc = 1.0 / math.sqrt(128.0 * L)
        out['cl' + nm] = bf(np.cos(a) * sc)
        out['sl' + nm] = bf(-np.sin(a) * sc)
    _CACHE['dft'] = out
    return out


def kernel(**inp):
    if 'kb' not in _CACHE:
        _CACHE['kb'] = build()
    kb = _CACHE['kb']
    f = lambda a: np.ascontiguousarray(np.asarray(a, dtype=np.float32))
    x_prompt = f(inp['x_prompt'])
    x_sample = f(inp['x_sample'])
    c = f(inp['c'])
    c_ctx = f(inp['c_ctx'])
    ident = np.eye(128, dtype=np.float32)
    si = np.arange(128) // 16
    maskf = (si[None, :] >= si[:, None]).astype(np.float32)
    maskb = (si[:, None] >= si[None, :]).astype(np.float32)
    tpos = np.arange(SL)
    inv = (10000.0 ** (-np.arange(16, dtype=np.float32) / 16)).astype(np.float32)
    row = (tpos // 64).astype(np.float32)
    col = (tpos % 64).astype(np.float32)
    ang = np.concatenate([row[:, None] * inv[None, :], col[:, None] * inv[None, :]], 1).astype(np.float32)
    rope = np.concatenate([np.cos(ang), np.sin(ang)], 1).astype(np.float32)
    shared = {
        'ada_w': f(inp['ada_w']), 'ada_b': f(inp['ada_b']), 'norm_pre': f(inp['norm_pre']), 'norm_post': f(inp['norm_post']),
        'ab_w_in': f(inp['ab_w_in'])[0], 'q_gain': f(inp['ab_q_norm'])[0], 'k_gain': f(inp['ab_k_norm'])[0],
        'lam_re': f(inp['ssm_lambda_re'])[0].reshape(-1), 'lam_im': f(inp['ssm_lambda_im'])[0].reshape(-1),
        'log_dt': f(inp['ssm_log_dt'])[0].reshape(-1),
        'b_re': f(inp['ssm_b_re'])[0].reshape(-1), 'b_im': f(inp['ssm_b_im'])[0].reshape(-1),
        'c_re': f(inp['ssm_c_re'])[0].reshape(-1), 'c_im': f(inp['ssm_c_im'])[0].reshape(-1),
        'ssm_d': f(inp['ssm_d'])[0], 'glu_w': f(inp['ssm_glu_w'])[0], 'glu_b': f(inp['ssm_glu_b'])[0],
        'ab_w_out': f(inp['ab_w_out'])[0],
        'ident': ident.astype(ml_dtypes.bfloat16), 'identf': ident, 'maskf': maskf, 'maskb': maskb, 'rope': rope,
        'cd_w_in': f(inp['cd_w_in'])[0], 'v_gain': f(inp['gmlp_v_norm'])[0], 'w_s': f(inp['gmlp_w_s'])[0], 'b_s': f(inp['gmlp_b_s'])[0],
        'fnet_w': f(inp['fnet_w'])[0], 'cd_w_out': f(inp['cd_w_out'])[0],
    }
    shared.update(_dft_tables())
    used = set(kb.din.keys())
    maps = []
    for core in range(NCORES):
        b = core // 4
        m = dict(shared)
        m['xp'] = x_prompt[core * NPS:(core + 1) * NPS].reshape(NPS * SEQ, D)
        m['xs'] = x_sample[b]
        m['cvec'] = np.stack([c_ctx, c[b]], 0)
        m['ck'] = f(inp['cache_k'])[b, 0].reshape(512, 128)
        m['cv'] = f(inp['cache_v'])[b, 0].reshape(512, 128)
        m['h0'] = f(inp['state_ssm'])[b, 0].reshape(-1)
        maps.append({k: v for k, v in m.items() if k in used})
    res = run_bass_kernel_spmd(kb.nc, maps, core_ids=list(range(NCORES)))
    r = res.results
    B = 32
    cat = lambda name, shp: np.concatenate([np.asarray(r[c_][name], dtype=np.float32).reshape(shp) for c_ in range(NCORES)], 0)
    y_prompt = cat('yp', (NPS, SEQ, D))
    y_sample = np.stack([np.asarray(r[0]['ys'], dtype=np.float32), np.asarray(r[4]['ys'], dtype=np.float32)], 0).reshape(2, SL, D)
    nk = cat('nk', (NPS, 1, SEQ, 2, 64))
    nv = cat('nv', (NPS, 1, SEQ, 2, 64))
    ns = cat('ns', (NPS, 1, 2, 2, 32, 64))
    return (y_prompt, y_sample, nk, nv, ns)
```

```python
import contextlib
import numpy as np
import ml_dtypes
import concourse.bass as bass
import concourse.mybir as mybir
from concourse.bass_utils import run_bass_kernel_spmd

F32 = mybir.dt.float32
BF16 = mybir.dt.bfloat16
ALU = mybir.AluOpType
AF = mybir.ActivationFunctionType
AX = mybir.AxisListType

ENGS = ['pe', 'act', 'dve', 'pool', 'sp']
DMA_RING = 8
NCORES = 8
D = 1024
NPS = 4
SEQ = 256
EPS = 1e-6


class Sched:
    def __init__(self, nc):
        self.nc = nc
        self.ops = {e: [] for e in ENGS}
        self.lastw = {}
        self.readers = {}
        self.ndma = {e: 0 for e in ENGS}
        self.pending = {}
        self.groups = {}

    def expand(self, keys):
        out = []
        for k in keys:
            out.extend(self.groups.get(k, [k]))
        return out

    def barrier(self):
        deps = set()
        for e in ENGS:
            ops = self.ops[e]
            for i in range(len(ops) - 1, -1, -1):
                if not ops[i]['dma']:
                    deps.add((e, i))
                    break
            cnt = 0
            for i in range(len(ops) - 1, -1, -1):
                if ops[i]['dma']:
                    deps.add((e, i))
                    cnt += 1
                    if cnt >= DMA_RING:
                        break
        for e in ENGS:
            self.pending[e] = set(deps) | self.pending.get(e, set())

    def op(self, eng, fn, reads=(), writes=(), signal=True, dma=False):
        reads = self.expand(reads)
        writes = self.expand(writes)
        deps = set()
        if self.pending.get(eng):
            deps |= self.pending.pop(eng)
        for k in reads:
            if k in self.lastw:
                deps.add(self.lastw[k])
        for k in writes:
            if k in self.lastw:
                deps.add(self.lastw[k])
            for r in self.readers.get(k, ()):
                deps.add(r)
        idx = len(self.ops[eng])
        me = (eng, idx)
        deps.discard(me)
        if eng == 'pe':
            deps = {d for d in deps if d[0] != 'pe'}
        rec = dict(fn=fn, deps=deps, signal=signal or dma, dma=dma, dma_n=None)
        if dma:
            rec['dma_n'] = self.ndma[eng]
            self.ndma[eng] += 1
        self.ops[eng].append(rec)
        for k in writes:
            self.lastw[k] = me
            self.readers[k] = set()
        for k in reads:
            self.readers.setdefault(k, set()).add(me)
        return me

    def run(self):
        nc = self.nc
        with contextlib.ExitStack() as es:
            csem = {e: es.enter_context(nc.semaphore('c_' + e)) for e in ENGS}
            dsem = {e: [es.enter_context(nc.semaphore('d_%s%d' % (e, i))) for i in range(DMA_RING)]
                    for e in ENGS if self.ndma[e] > 0}
            ev = {}
            for e in ENGS:
                ops = self.ops[e]
                cnt = 0
                cum = []
                for o in ops:
                    if o['signal'] and not o['dma']:
                        cnt += 1
                    cum.append(cnt)
                nxt = None
                for i in range(len(ops) - 1, -1, -1):
                    o = ops[i]
                    if o['dma']:
                        n = o['dma_n']
                        ev[(e, i)] = (('d', e, n % DMA_RING), 16 * (n // DMA_RING + 1))
                    else:
                        if o['signal']:
                            nxt = cum[i]
                        assert nxt is not None, 'last compute op on engine %s must signal' % e
                        ev[(e, i)] = (('c', e), nxt)

            def semh(sid):
                return csem[sid[1]] if sid[0] == 'c' else dsem[sid[1]][sid[2]]

            final_waits = []
            for e in ENGS:
                n = self.ndma[e]
                for r in range(DMA_RING):
                    c = len(range(r, n, DMA_RING))
                    if c > 0:
                        final_waits.append((('d', e, r), 16 * c))

            def replay(e, eng):
                waited = {}
                ops = self.ops[e]
                for i, o in enumerate(ops):
                    need = {}
                    for d in o['deps']:
                        sid, val = ev[d]
                        if need.get(sid, 0) < val:
                            need[sid] = val
                    if o['dma'] and o['dma_n'] >= DMA_RING:
                        n0 = o['dma_n'] - DMA_RING
                        sid, val = ('d', e, n0 % DMA_RING), 16 * (n0 // DMA_RING + 1)
                        if need.get(sid, 0) < val:
                            need[sid] = val
                    for sid, val in need.items():
                        if waited.get(sid, 0) < val:
                            eng.wait_ge(semh(sid), val)
                            waited[sid] = val
                    ins = o['fn'](eng)
                    if o['dma']:
                        sid, val = ev[(e, i)]
                        ins.then_inc(semh(sid), 16)
                    elif o['signal']:
                        ins.then_inc(csem[e], 1)
                if e == 'sp':
                    for sid, val in final_waits:
                        if waited.get(sid, 0) < val:
                            eng.wait_ge(semh(sid), val)
                    for e2 in ENGS:
                        if e2 != 'sp':
                            c = sum(1 for o in self.ops[e2] if o['signal'] and not o['dma'])
                            if c > 0:
                                eng.wait_ge(csem[e2], c)

            with nc.Block() as block:
                @block.tensor
                def _(eng):
                    replay('pe', eng)

                @block.scalar
                def _(eng):
                    replay('act', eng)

                @block.vector
                def _(eng):
                    replay('dve', eng)

                @block.gpsimd
                def _(eng):
                    replay('pool', eng)

                @block.sync
                def _(eng):
                    replay('sp', eng)


class Pool:
    def __init__(self, kb, name, shape, dtype, n, psum=False):
        self.tiles = []
        for i in range(n):
            nm = '%s%d' % (name, i)
            t = kb.ps(nm, shape, dtype) if psum else kb.sb(nm, shape, dtype)
            self.tiles.append((t, nm))
        self.i = 0

    def get(self):
        t = self.tiles[self.i % len(self.tiles)]
        self.i += 1
        return t


class KB:
    def __init__(self):
        self.nc = bass.Bass("TRN2", target_bir_lowering=False)
        self.S = Sched(self.nc)
        self.es = contextlib.ExitStack()
        self.din = {}
        self.dout = {}

    def inp(self, name, shape, dtype=F32):
        t = self.nc.dram_tensor(name, list(shape), dtype, kind="ExternalInput")
        self.din[name] = t
        return t.ap()

    def outp(self, name, shape, dtype=F32):
        t = self.nc.dram_tensor(name, list(shape), dtype, kind="ExternalOutput")
        self.dout[name] = t
        return t.ap()

    def sb(self, name, shape, dtype):
        return self.es.enter_context(self.nc.sbuf_tensor(name, list(shape), dtype))

    def ps(self, name, shape, dtype):
        return self.es.enter_context(self.nc.psum_tensor(name, list(shape), dtype))

    def dma(self, out, in_, reads=(), writes=(), q='sp', slow=False):
        if slow:
            fn = lambda e: e.dma_start(out=out, in_=in_, allow_slow_non_contiguous=True)
        else:
            fn = lambda e: e.dma_start(out=out, in_=in_)
        return self.S.op(q, fn, reads=reads, writes=writes, dma=True)

    def mm(self, out, lhsT, rhs, start, stop, reads=(), writes=(), signal=None, sgc=False):
        if signal is None:
            signal = stop
        if sgc:
            return self.S.op('pe', lambda e: e.matmul(out, lhsT, rhs, start=start, stop=stop, skip_group_check=True),
                             reads=reads, writes=writes, signal=signal)
        return self.S.op('pe', lambda e: e.matmul(out, lhsT, rhs, start=start, stop=stop),
                         reads=reads, writes=writes, signal=signal)

    def tr(self, out, in_, ident, reads=(), writes=(), signal=True):
        return self.S.op('pe', lambda e: e.transpose(out=out, in_=in_, identity=ident),
                         reads=reads, writes=writes, signal=signal)

    def act(self, out, in_, func, reads=(), writes=(), bias=None, scale=None, accum_out=None):
        kw = {}
        if bias is not None:
            kw['bias'] = bias
        if scale is not None:
            kw['scale'] = scale
        if accum_out is not None:
            kw['accum_out'] = accum_out
        return self.S.op('act', lambda e: e.activation(out=out, in_=in_, func=func, **kw), reads=reads, writes=writes)

    def tt(self, eng, out, in0, in1, op, reads=(), writes=()):
        return self.S.op(eng, lambda e: e.tensor_tensor(out=out, in0=in0, in1=in1, op=op), reads=reads, writes=writes)

    def ts(self, eng, out, in0, s1, s2, op0, op1=None, reads=(), writes=()):
        if op1 is None:
            fn = lambda e: e.tensor_scalar(out=out, in0=in0, scalar1=s1, scalar2=0.0, op0=op0, op1=ALU.add)
        else:
            fn = lambda e: e.tensor_scalar(out=out, in0=in0, scalar1=s1, scalar2=s2, op0=op0, op1=op1)
        return self.S.op(eng, fn, reads=reads, writes=writes)

    def stt(self, eng, out, in0, scalar, in1, op0, op1, reads=(), writes=()):
        return self.S.op(eng, lambda e: e.scalar_tensor_tensor(out=out, in0=in0, scalar=scalar, in1=in1, op0=op0, op1=op1),
                         reads=reads, writes=writes)

    def cp(self, eng, out, in_, reads=(), writes=()):
        if eng == 'act':
            return self.S.op('act', lambda e: e.copy(out=out, in_=in_), reads=reads, writes=writes)
        return self.S.op(eng, lambda e: e.tensor_copy(out=out, in_=in_), reads=reads, writes=writes)

    def memset(self, eng, ap, val, writes=()):
        return self.S.op(eng, lambda e: e.memset(ap, val), writes=writes)

    def rstd(self, out, ss, tmp, inv_n, key):
        self.ts('dve', tmp, ss, inv_n, EPS, ALU.mult, ALU.add, reads=[key], writes=[key])
        self.S.op('act', lambda e: e.sqrt(out=tmp, in_=tmp), reads=[key], writes=[key])
        self.S.op('dve', lambda e: e.reciprocal(out=out, in_=tmp), reads=[key], writes=[key])

    def rsum(self, eng, out, in_, reads=(), writes=()):
        return self.S.op(eng, lambda e: e.reduce_sum(out=out, in_=in_, axis=AX.X), reads=reads, writes=writes)


def bcast_rows(ap_dram_row, nparts):
    t = ap_dram_row
    return bass.AP(tensor=t.tensor, offset=t.offset, ap=[[0, nparts]] + [list(x) for x in t.ap])


def V(ap, off, dims, p0=0, np_=None):
    pstride = ap.ap[0][0]
    npart = ap.ap[0][1] if np_ is None else np_
    return bass.AP(tensor=ap.tensor, offset=ap.offset + p0 * pstride + off,
                   ap=[[pstride, npart]] + [list(d) for d in dims])


class Arena:
    def __init__(self, kb, name, nelem, dtype):
        self.t = kb.sb(name, [128, nelem], dtype)
        self.n = nelem
        self.off = 0
        self.name = name
        self.cnt = 0

    def reset(self):
        self.off = 0

    def alloc(self, shape):
        n = 1
        for d in shape[1:]:
            n *= d
        a = self.off
        self.off = (a + n + 31) // 32 * 32
        assert self.off <= self.n, 'arena %s overflow: %d > %d' % (self.name, self.off, self.n)
        v = self.t[0:shape[0], a:a + n]
        if len(shape) > 2:
            names = 'abcdef'[:len(shape) - 1]
            pat = 'p (%s) -> p %s' % (' '.join(names), ' '.join(names))
            kw = {names[i]: shape[1 + i] for i in range(1, len(names))}
            v = v.rearrange(pat, **kw)
        self.cnt += 1
        return v, '%s_%d' % (self.name, self.cnt)

    def pool(self, shape, n):
        return APPool([self.alloc(shape) for _ in range(n)])


class APPool:
    def __init__(self, tiles):
        self.tiles = tiles
        self.i = 0

    def get(self):
        t = self.tiles[self.i % len(self.tiles)]
        self.i += 1
        return t

import math
import os

SL = 2048
NTOK = 1024 + SL
PI = math.pi
STAGE = int(os.environ.get('KSTAGE', '99'))


def dram_ap(t, off, dims):
    return bass.AP(tensor=t.tensor, offset=t.offset + off, ap=[list(d) for d in dims])


def build():
    kb = KB()
    nc = kb.nc
    S = kb.S
    xp = kb.inp('xp', [1024, D])
    xs = kb.inp('xs', [SL, D])
    cvec = kb.inp('cvec', [2, D])
    ck_d = kb.inp('ck', [512, 128])
    cv_d = kb.inp('cv', [512, 128])
    h0_d = kb.inp('h0', [2 * 2 * 32 * 64])
    ada_w = kb.inp('ada_w', [2, D, 3 * D])
    ada_b = kb.inp('ada_b', [2, 3 * D])
    norm_pre = kb.inp('norm_pre', [2, D])
    norm_post = kb.inp('norm_post', [2, D])
    ab_w_in = kb.inp('ab_w_in', [D, 2304])
    q_gain = kb.inp('q_gain', [64])
    k_gain = kb.inp('k_gain', [64])
    lam_re = kb.inp('lam_re', [4096])
    lam_im = kb.inp('lam_im', [4096])
    log_dt = kb.inp('log_dt', [64])
    b_re = kb.inp('b_re', [65536])
    b_im = kb.inp('b_im', [65536])
    c_re = kb.inp('c_re', [65536])
    c_im = kb.inp('c_im', [65536])
    ssm_d = kb.inp('ssm_d', [512])
    glu_w = kb.inp('glu_w', [512, 512])
    glu_b = kb.inp('glu_b', [512])
    ab_w_out = kb.inp('ab_w_out', [1024, 1024])
    ident_d = kb.inp('ident', [128, 128], BF16)
    identf_d = kb.inp('identf', [128, 128])
    maskf_d = kb.inp('maskf', [128, 128])
    maskb_d = kb.inp('maskb', [128, 128])
    rope_d = kb.inp('rope', [SL, 64])
    cd_w_in = kb.inp('cd_w_in', [D, 2560])
    v_gain = kb.inp('v_gain', [512])
    w_s = kb.inp('w_s', [8, 128, 128])
    b_s = kb.inp('b_s', [8, 128])
    fnet_w = kb.inp('fnet_w', [512, 512])
    cd_w_out = kb.inp('cd_w_out', [1024, 1024])
    c128_d = kb.inp('c128', [128, 128], BF16)
    s128_d = kb.inp('s128', [128, 128], BF16)
    clp_d = kb.inp('clp', [256, 256], BF16)
    slp_d = kb.inp('slp', [256, 256], BF16)
    cls_d = kb.inp('cls', [SL, SL], BF16)
    sls_d = kb.inp('sls', [SL, SL], BF16)
    yp = kb.outp('yp', [1024, D])
    ys = kb.outp('ys', [SL, D])
    nk = kb.outp('nk', [1024, 128])
    nv = kb.outp('nv', [1024, 128])
    ns = kb.outp('ns', [4 * 8192])
    MODS = nc.dram_tensor('MODS', [2, 2, 3, D], F32).ap()
    WBd = nc.dram_tensor('WBd', [128, 8192], BF16).ap()
    WCd = nc.dram_tensor('WCd', [128, 8192], BF16).ap()
    KMd = nc.dram_tensor('KMd', [128, 4096], BF16).ap()
    XSd = nc.dram_tensor('XSd', [3, 128, 4096], BF16).ap()
    MIXAd = nc.dram_tensor('MIXAd', [12, 64, 2048], BF16).ap()
    GBd = nc.dram_tensor('GBd', [12, 128, 1024], BF16).ap()
    X1 = nc.dram_tensor('X1', [NTOK, D], F32).ap()
    HNTd = nc.dram_tensor('HNTd', [3, 128, 8192], BF16).ap()
    ab_w_in_b = nc.dram_tensor('ab_w_in_b', [D, 2304], BF16).ap()
    ab_w_out_b = nc.dram_tensor('ab_w_out_b', [1024, 1024], BF16).ap()
    glu_w_b = nc.dram_tensor('glu_w_b', [512, 512], BF16).ap()
    cd_w_in_b = nc.dram_tensor('cd_w_in_b', [D, 2560], BF16).ap()
    cd_w_out_b = nc.dram_tensor('cd_w_out_b', [1024, 1024], BF16).ap()
    fnet_w_b = nc.dram_tensor('fnet_w_b', [512, 512], BF16).ap()
    w_s_b = nc.dram_tensor('w_s_b', [8, 128, 128], BF16).ap()
    MIXCd = nc.dram_tensor('MIXCd', [12, 128, 1024], BF16).ap()
    GDd = nc.dram_tensor('GDd', [12, 128, 1024], BF16).ap()

    with kb.es:
        sb = kb.sb
        ident = sb('ident_sb', [128, 128], BF16)[:]
        kb.dma(ident, ident_d, writes=['ident'])
        identf = sb('identf_sb', [128, 128], F32)[:]
        kb.dma(identf, identf_d, writes=['identf'])
        kg = sb('kg', [128, 64], F32)[:]
        kb.dma(kg, bcast_rows(k_gain, 128), writes=['kg'])
        qg = sb('qg', [128, 64], F32)[:]
        kb.dma(qg, bcast_rows(q_gain, 128), writes=['qg'])
        kb.ts('dve', qg, qg, 0.125, None, ALU.mult, reads=['qg'], writes=['qg'])
        ones_bf = sb('ones_bf', [128, 64], BF16)[:]
        kb.memset('dve', ones_bf, 1.0, writes=['ones_bf'])
        SHIFT = sb('SHIFT', [128, D], F32)[:]
        G1 = sb('G1', [128, D], F32)[:]
        G2 = sb('G2', [128, D], F32)[:]
        LLt = sb('LLt', [128, 64], F32)[:]
        LXt = sb('LXt', [128, 64], F32)[:]
        HL = sb('HL', [128, 2 * 32 * 257], BF16)[:]
        KT = sb('KT', [128, 2 * 2560], BF16)[:]
        VA = sb('VA', [128, 20 * 256], BF16)[:]
        AF_ = Arena(kb, 'AF', 12800, F32)
        AB_ = Arena(kb, 'AB', 46080, BF16)

        pacc = Pool(kb, 'pacc', [128, 512], F32, 6, psum=True)
        ptr = Pool(kb, 'ptr', [128, 1024], BF16, 2, psum=True)

        def new_phase():
            S.barrier()
            AF_.reset()
            AB_.reset()

        def load_T(dst, dstn, src_rows, n):
            st, stn = AF_.alloc([n, 128])
            kb.dma(st, src_rows, writes=[stn])
            pt, pn = pacc.get()
            kb.tr(pt[:, 0:n], st, identf[0:n, 0:n], reads=[stn, 'identf'], writes=[pn])
            kb.cp('act', dst, pt[:, 0:n], reads=[pn], writes=[dstn])

        condT, cTn = AF_.alloc([128, 8, 2])
        cst, cstn = AF_.alloc([2, D])
        kb.dma(cst, cvec, writes=[cstn])
        for k in range(8):
            pt, pn = pacc.get()
            kb.tr(pt[:, 0:2], cst[:, k * 128:(k + 1) * 128], identf[0:2, 0:2], reads=[cstn, 'identf'], writes=[pn])
            kb.cp('act', condT[:, k, :], pt[:, 0:2], reads=[pn], writes=[cTn])
        kb.act(condT, condT, AF.Silu, reads=[cTn], writes=[cTn])
        modkeys = []
        adaw_pool = AF_.pool([128, 8, 512], 2)
        mb_pool = AF_.pool([2, 512], 2)
        ab_pool = AF_.pool([2, 512], 2)
        np_pool = AF_.pool([2, 512], 2)
        for l in range(2):
            wv = ada_w[l].rearrange("(k p) n -> p k n", p=128)
            for cb in range(6):
                kind, hh = cb // 2, cb % 2
                wt, wn = adaw_pool.get()
                kb.dma(wt, wv[:, :, cb * 512:(cb + 1) * 512], writes=[wn])
                abt, abn = ab_pool.get()
                kb.dma(abt, bcast_rows(ada_b[l, cb * 512:(cb + 1) * 512], 2), writes=[abn])
                pt, pn = pacc.get()
                for k in range(8):
                    kb.mm(pt[0:2, :], condT[:, k, :], wt[:, k, :], start=(k == 0), stop=(k == 7),
                          reads=[cTn, wn], writes=[pn], signal=True)
                mb, mbn_ = mb_pool.get()
                kb.tt('dve', mb, pt[0:2, :], abt, ALU.add, reads=[pn, abn], writes=[mbn_])
                if kind > 0:
                    npt, npn = np_pool.get()
                    src = norm_pre if kind == 1 else norm_post
                    kb.dma(npt, bcast_rows(src[l, hh * 512:(hh + 1) * 512], 2), writes=[npn])
                    if kind == 1:
                        kb.stt('dve', mb, mb, 1.0, npt, ALU.add, ALU.mult, reads=[mbn_, npn], writes=[mbn_])
                    else:
                        kb.tt('dve', mb, mb, npt, ALU.mult, reads=[mbn_, npn], writes=[mbn_])
                mk_ = 'MODS_%d_%d' % (l, cb)
                modkeys.append(mk_)
                kb.dma(MODS[l, :, kind, hh * 512:(hh + 1) * 512], mb, reads=[mbn_], writes=[mk_], q='act')

        wkeys = {}

        def convert_w(name, src_w, dst_w, rows, after):
            ks = []
            for r in range(0, rows, 128):
                k_ = 'wc_%s_%d' % (name, r)
                kb.dma(dst_w[r:r + 128, :], src_w[r:r + 128, :], reads=after, writes=[k_], q='pool')
                ks.append(k_)
            wkeys[name] = ks


        def convert_late():
            convert_w('cd_in', cd_w_in, cd_w_in_b, 1024, [])
            convert_w('cd_out', cd_w_out, cd_w_out_b, 1024, [])
            convert_w('fnet', fnet_w, fnet_w_b, 512, [])
            convert_w('ws', w_s.rearrange("h t s -> (h t) s"), w_s_b.rearrange("h t s -> (h t) s"), 1024, [])

        def load_mod(l, ci):
            kb.dma(SHIFT, bcast_rows(MODS[l, ci, 0, :], 128), reads=modkeys, writes=['SHIFT'])
            kb.dma(G1, bcast_rows(MODS[l, ci, 1, :], 128), reads=modkeys, writes=['G1'])
            kb.dma(G2, bcast_rows(MODS[l, ci, 2, :], 128), reads=modkeys, writes=['G2'])

        new_phase()
        convert_w('ab_in', ab_w_in, ab_w_in_b, 1024, [])
        convert_w('ab_out', ab_w_out, ab_w_out_b, 1024, [])
        convert_w('glu', glu_w, glu_w_b, 512, [])

        def cmul(eng, o_r, o_i, a_r, a_i, b_r, b_i, t1, t2, rk, wk, tk, neg_im=False, t34=None):
            if t34 is not None:
                t3, t4 = t34
                k1, k2, k3, k4 = tk + '_1', tk + '_2', tk + '_3', tk + '_4'
                kb.tt(eng, t1, a_r, b_r, ALU.mult, reads=rk, writes=[k1])
                kb.tt(eng, t2, a_i, b_i, ALU.mult, reads=rk, writes=[k2])
                kb.tt(eng, t3, a_r, b_i, ALU.mult, reads=rk, writes=[k3])
                kb.tt(eng, t4, a_i, b_r, ALU.mult, reads=rk, writes=[k4])
                kb.tt(eng, o_r, t1, t2, ALU.subtract, reads=[k1, k2], writes=wk)
                kb.tt(eng, o_i, t3, t4, ALU.add, reads=[k3, k4], writes=wk)
                return
            kb.tt(eng, t1, a_r, b_r, ALU.mult, reads=rk, writes=[tk])
            kb.tt(eng, t2, a_i, b_i, ALU.mult, reads=rk, writes=[tk])
            kb.tt(eng, o_r, t1, t2, ALU.subtract, reads=[tk], writes=wk)
            kb.tt(eng, t1, a_r, b_i, ALU.mult, reads=rk + wk, writes=[tk])
            kb.tt(eng, t2, a_i, b_r, ALU.mult, reads=rk + wk, writes=[tk])
            if neg_im:
                kb.tt(eng, t1, t1, t2, ALU.add, reads=[tk], writes=[tk])
                kb.ts(eng, o_i, t1, -1.0, None, ALU.mult, reads=[tk], writes=wk)
            else:
                kb.tt(eng, o_i, t1, t2, ALU.add, reads=[tk], writes=wk)

        def sm(n=32):
            return AF_.alloc([128, n])

        LR, LRn = sm()
        LI, LIn = sm()
        LD, LDn = sm()
        load_T(LR, LRn, dram_ap(lam_re, 0, [[128, 32], [1, 128]]), 32)
        load_T(LI, LIn, dram_ap(lam_im, 0, [[128, 32], [1, 128]]), 32)
        ld0, ld0n = AF_.alloc([32, 2])
        kb.dma(ld0, dram_ap(log_dt, 0, [[2, 32], [1, 2]]), writes=[ld0n])
        ld1, ld1n = AF_.alloc([32, 128])
        kb.cp('dve', ld1.rearrange("p (g n) -> p g n", g=2), V(ld0, 0, [[1, 2], [0, 64]]), reads=[ld0n], writes=[ld1n])
        ptl, ptln = pacc.get()
        kb.tr(ptl[:, 0:32], ld1, identf[0:32, 0:32], reads=[ld1n, 'identf'], writes=[ptln])
        kb.cp('act', LD, ptl[:, 0:32], reads=[ptln], writes=[LDn])
        TK = 'tblsmall'
        dt_, _ = sm(); a_, _ = sm(); th, _ = sm(); mag, _ = sm(); arg, _ = sm(); sn, _ = sm(); cs, _ = sm()
        lbr, _ = sm(); lbi, _ = sm(); t1, _ = sm(); t2, _ = sm(); nre, _ = sm(); den, _ = sm()
        kr, _ = sm(); ki, _ = sm(); ibr, _ = sm(); ibi, _ = sm()
        R = [TK, LRn, LIn, LDn]
        W = [TK]
        kb.act(dt_, LD, AF.Exp, reads=R, writes=W)
        kb.tt('dve', a_, LR, dt_, ALU.mult, reads=R, writes=W)
        kb.tt('dve', th, LI, dt_, ALU.mult, reads=R, writes=W)
        kb.act(mag, a_, AF.Exp, reads=R, writes=W)
        kb.ts('dve', arg, th, 1.0 / 16, None, ALU.mult, reads=R, writes=W)
        kb.act(sn, arg, AF.Sin, reads=R, writes=W)
        kb.ts('dve', arg, arg, PI / 2, None, ALU.add, reads=R, writes=W)
        kb.act(cs, arg, AF.Sin, reads=R, writes=W)
        for _it in range(4):
            kb.tt('dve', t1, cs, cs, ALU.mult, reads=R, writes=W)
            kb.tt('dve', t2, sn, sn, ALU.mult, reads=R, writes=W)
            kb.tt('dve', sn, cs, sn, ALU.mult, reads=R, writes=W)
            kb.ts('dve', sn, sn, 2.0, None, ALU.mult, reads=R, writes=W)
            kb.tt('dve', cs, t1, t2, ALU.subtract, reads=R, writes=W)
        kb.tt('dve', lbr, mag, cs, ALU.mult, reads=R, writes=W)
        kb.tt('dve', lbi, mag, sn, ALU.mult, reads=R, writes=W)
        kb.ts('dve', nre, lbr, -1.0, None, ALU.add, reads=R, writes=W)
        kb.tt('dve', t1, LR, LR, ALU.mult, reads=R, writes=W)
        kb.tt('dve', t2, LI, LI, ALU.mult, reads=R, writes=W)
        kb.tt('dve', den, t1, t2, ALU.add, reads=R, writes=W)
        S.op('dve', lambda e: e.reciprocal(out=den, in_=den), reads=R, writes=W)
        kb.tt('dve', t1, nre, LR, ALU.mult, reads=R, writes=W)
        kb.tt('dve', t2, lbi, LI, ALU.mult, reads=R, writes=W)
        kb.tt('dve', t1, t1, t2, ALU.add, reads=R, writes=W)
        kb.tt('dve', kr, t1, den, ALU.mult, reads=R, writes=W)
        kb.tt('dve', t1, lbi, LR, ALU.mult, reads=R, writes=W)
        kb.tt('dve', t2, nre, LI, ALU.mult, reads=R, writes=W)
        kb.tt('dve', t1, t1, t2, ALU.subtract, reads=R, writes=W)
        kb.tt('dve', ki, t1, den, ALU.mult, reads=R, writes=W)
        kb.tt('dve', t1, lbr, lbr, ALU.mult, reads=R, writes=W)
        kb.tt('dve', t2, lbi, lbi, ALU.mult, reads=R, writes=W)
        kb.tt('dve', t1, t1, t2, ALU.add, reads=R, writes=W)
        S.op('dve', lambda e: e.reciprocal(out=t1, in_=t1), reads=R, writes=W)
        kb.tt('dve', ibr, lbr, t1, ALU.mult, reads=R, writes=W)
        kb.stt('dve', ibi, lbi, -1.0, t1, ALU.mult, ALU.mult, reads=R, writes=W)
        PR, _ = AF_.alloc([128, 9, 32]); PIm, _ = AF_.alloc([128, 9, 32])
        QR, _ = AF_.alloc([128, 8, 32]); QI, _ = AF_.alloc([128, 8, 32])
        kb.memset('dve', PR[:, 0, :], 1.0, writes=W)
        kb.memset('dve', PIm[:, 0, :], 0.0, writes=W)
        kb.memset('dve', QR[:, 0, :], 1.0, writes=W)
        kb.memset('dve', QI[:, 0, :], 0.0, writes=W)
        w1, _ = AF_.alloc([128, 4, 32]); w2, _ = AF_.alloc([128, 4, 32]); w3, _ = AF_.alloc([128, 4, 32]); w4, _ = AF_.alloc([128, 4, 32])

        def pw_double(XR, XI, base_r, base_i, nmax):
            kb.cp('dve', XR[:, 1, :], base_r, reads=[TK], writes=['PW'])
            kb.cp('dve', XI[:, 1, :], base_i, reads=[TK], writes=['PW'])
            steps = [(2, 1, 1), (3, 2, 2), (5, 4, nmax - 4)]
            for (d0, mi, n) in steps:
                br_ = V(XR, mi * 32, [[0, n], [1, 32]])
                bi_ = V(XI, mi * 32, [[0, n], [1, 32]])
                cmul('dve', XR[:, d0:d0 + n, :], XI[:, d0:d0 + n, :], XR[:, 1:1 + n, :], XI[:, 1:1 + n, :], br_, bi_,
                     w1[:, 0:n, :], w2[:, 0:n, :], ['PW'], ['PW'], 'pwt', t34=(w3[:, 0:n, :], w4[:, 0:n, :]))

        pw_double(PR, PIm, lbr, lbi, 8)
        pw_double(QR, QI, ibr, ibi, 7)
        kb.cp('dve', t1, t1, reads=['PW', TK], writes=[TK])
        LL3 = LLt.rearrange("p (a c) -> p a c", c=2)
        LX3 = LXt.rearrange("p (a c) -> p a c", c=2)
        kb.cp('dve', LL3[:, :, 0], PR[:, 8, :], reads=[TK], writes=['LLt'])
        kb.cp('dve', LL3[:, :, 1], PR[:, 8, :], reads=[TK], writes=['LLt'])
        kb.ts('dve', LX3[:, :, 0], PIm[:, 8, :], -1.0, None, ALU.mult, reads=[TK], writes=['LXt'])
        kb.cp('dve', LX3[:, :, 1], PIm[:, 8, :], reads=[TK], writes=['LXt'])
        BR, BRn = AF_.alloc([128, 32, 16]); BI, BIn = AF_.alloc([128, 32, 16])
        CR, CRn = AF_.alloc([128, 32, 16]); CI, CIn = AF_.alloc([128, 32, 16])
        pt1, _ = AF_.alloc([128, 16, 8, 16]); pt2, _ = AF_.alloc([128, 16, 8, 16])
        bst, bstn = V(pt1, 0, [[1, 2048]], np_=32), 'tbtmp'
        for (src_b, dstB, dstBn) in ((b_re, BR, BRn), (b_im, BI, BIn)):
            kb.dma(bst, dram_ap(src_b, 0, [[2048, 32], [1, 2048]]), writes=[bstn])
            pt, pn = pacc.get()
            for h in range(16):
                kb.tr(pt[:, h * 32:(h + 1) * 32], V(bst, h, [[16, 128]]), identf[0:32, 0:32], reads=[bstn, 'identf'], writes=[pn], signal=(h == 15))
            kb.cp('act', V(dstB, 0, [[1, 16], [16, 32]]), pt[:, :].rearrange("p (h a) -> p h a", h=16), reads=[pn], writes=[dstBn])
        for (src_c, dstC, dstn) in ((c_re, CR, CRn), (c_im, CI, CIn)):
            Cn, Cnn = AF_.alloc([128, 4, 128])
            for d in range(2):
                for gqb in range(2):
                    for gql in range(8):
                        kb.dma(Cn[gql * 16:(gql + 1) * 16, d * 2 + gqb, :],
                               dram_ap(src_c, d * 32768 + (gqb * 8 + gql) * 2048, [[64, 16], [1024, 2], [1, 64]]),
                               writes=['%s_%d_%d' % (Cnn, d * 2 + gqb, gql)])
            for blk in range(4):
                pt, pn = pacc.get()
                kb.tr(pt[:, 0:128], Cn[:, blk, :], identf, reads=['%s_%d_%d' % (Cnn, blk, q_) for q_ in range(8)] + ['identf'], writes=[pn])
                kb.cp('act', dstC.rearrange("p a h -> p (a h)")[:, blk * 128:(blk + 1) * 128], pt[:, 0:128], reads=[pn], writes=[dstn])
        BBR, _ = AF_.alloc([128, 32, 16]); BBI, _ = AF_.alloc([128, 32, 16])
        tb1 = V(pt2, 0, [[16, 32], [1, 16]])
        tb2 = V(pt2, 512, [[16, 32], [1, 16]])
        bc = lambda t, off, n: V(t, off, [[1, n], [0, 16]])
        cmul('dve', BBR, BBI, bc(kr, 0, 32), bc(ki, 0, 32), BR, BI, tb1, tb2, [TK, BRn, BIn], [TK], 'tbtmp')
        RaR, _ = AF_.alloc([128, 8, 32]); RaI, _ = AF_.alloc([128, 8, 32])
        RbR, _ = AF_.alloc([128, 8, 32]); RbI, _ = AF_.alloc([128, 8, 32])
        for k in range(8):
            kb.cp('dve', RaR[:, k, :], PR[:, 7 - k, :], reads=[TK], writes=['Rrev'])
            kb.cp('dve', RaI[:, k, :], PIm[:, 7 - k, :], reads=[TK], writes=['Rrev'])
            kb.cp('dve', RbR[:, k, :], PR[:, 8 - k, :], reads=[TK], writes=['Rrev'])
            kb.cp('dve', RbI[:, k, :], PIm[:, 8 - k, :], reads=[TK], writes=['Rrev'])
        TB, TBn = AB_.alloc([128, 2, 2, 16, 8, 16])
        AFt, AFn = AB_.alloc([128, 2, 16, 8, 16])
        BMF, BMFn = AB_.alloc([128, 2, 16, 8, 16])
        BMB, BMBn = AB_.alloc([128, 2, 16, 8, 16])
        WC, WCn = AB_.alloc([128, 2, 16, 2, 8, 16])

        def prod(o_r, o_i, pw_r, pw_i, koff, d, x_r, x_i, wkey, neg_im):
            pr = V(pw_r, koff * 32 + d * 16, [[1, 16], [32, 8], [0, 16]])
            pi_ = V(pw_i, koff * 32 + d * 16, [[1, 16], [32, 8], [0, 16]])
            xr = V(x_r, d * 256, [[16, 16], [0, 8], [1, 16]])
            xi = V(x_i, d * 256, [[16, 16], [0, 8], [1, 16]])
            cmul('dve', o_r, o_i, pr, pi_, xr, xi, pt1, pt2, [TK, CRn, CIn, 'Rrev'], [wkey], 'tbtmp', neg_im=neg_im)

        prod(TB[:, 0, 0], TB[:, 0, 1], RaR, RaI, 0, 0, BBR, BBI, TBn, False)
        prod(TB[:, 1, 0], TB[:, 1, 1], PR, PIm, 0, 1, BBR, BBI, TBn, False)
        prod(AFt[:, 0], AFt[:, 1], QR, QI, 0, 0, BBR, BBI, AFn, False)
        prod(BMF[:, 0], BMF[:, 1], PR, PIm, 0, 0, CR, CI, BMFn, True)
        prod(BMB[:, 0], BMB[:, 1], QR, QI, 0, 1, CR, CI, BMBn, True)
        prod(WC[:, 0, :, 0], WC[:, 0, :, 1], PR, PIm, 1, 0, CR, CI, WCn, True)
        prod(WC[:, 1, :, 0], WC[:, 1, :, 1], RbR, RbI, 0, 1, CR, CI, WCn, True)
        kb.dma(WCd, WC.rearrange("p d q c t h -> p (d q c t h)"), reads=[WCn], writes=['WCd'])
        WBt, WBn = AB_.alloc([128, 2, 16, 2, 128])
        for d in range(2):
            for qb in range(4):
                pt, pn = ptr.get()
                for qi in range(4):
                    gq = qb * 4 + qi
                    for c in range(2):
                        kb.tr(pt[:, (qi * 2 + c) * 128:(qi * 2 + c + 1) * 128],
                              TB[:, d, c, gq, :, :].rearrange("p s h -> p (s h)"), ident,
                              reads=[TBn, 'ident'], writes=[pn], signal=(qi == 3 and c == 1))
                kb.cp('act', WBt[:, d, qb * 4:(qb + 1) * 4, :, :].rearrange("p q c n -> p (q c n)"), pt[:, :],
                      reads=[pn], writes=[WBn])
        kb.dma(WBd, WBt.rearrange("p d q c n -> p (d q c n)"), reads=[WBn], writes=['WBd'])
        maskf, mfn = AF_.alloc([128, 128]); maskb, mbn = AF_.alloc([128, 128])
        kb.dma(maskf, maskf_d, writes=[mfn])
        kb.dma(maskb, maskb_d, writes=[mbn])
        dcol, dcn = AF_.alloc([128, 32])
        dc0, dc0n = AF_.alloc([32, 16])
        kb.dma(dc0, dram_ap(ssm_d, 0, [[16, 32], [1, 16]]), writes=[dc0n])
        dc1, dc1n = AF_.alloc([32, 128])
        kb.cp('dve', dc1.rearrange("p (s h) -> p s h", s=8), V(dc0, 0, [[0, 8], [1, 16]]), reads=[dc0n], writes=[dc1n])
        ptd, ptdn = pacc.get()
        kb.tr(ptd[:, 0:32], dc1, identf[0:32, 0:32], reads=[dc1n, 'identf'], writes=[ptdn])
        kb.cp('act', dcol, ptd[:, 0:32], reads=[ptdn], writes=[dcn])
        KM, KMn = AB_.alloc([128, 32, 128])
        km1, km1n = AF_.alloc([128, 128]); km2, km2n = AF_.alloc([128, 128])
        for g in range(32):
            gq, gh = g // 2, g % 2
            hs = slice(gh * 64, (gh + 1) * 64)
            pt, pn = pacc.get()
            fl = lambda t: t.rearrange("p s h -> p (s h)")
            kb.mm(pt[:, 0:128], fl(AFt[hs, 0, gq]), fl(BMF[hs, 0, gq]), True, False, reads=[AFn, BMFn], writes=[pn], signal=False)
            kb.mm(pt[:, 0:128], fl(AFt[hs, 1, gq]), fl(BMF[hs, 1, gq]), False, True, reads=[AFn, BMFn], writes=[pn], signal=False)
            kb.mm(pt[:, 128:256], fl(TB[hs, 1, 0, gq]), fl(BMB[hs, 0, gq]), True, False, reads=[TBn, BMBn], writes=[pn], signal=False)
            kb.mm(pt[:, 128:256], fl(TB[hs, 1, 1, gq]), fl(BMB[hs, 1, gq]), False, True, reads=[TBn, BMBn], writes=[pn], signal=True)
            kb.tt('dve', km1, pt[:, 0:128], maskf, ALU.mult, reads=[pn, mfn], writes=[km1n])
            kb.tt('dve', km2, pt[:, 128:256], maskb, ALU.mult, reads=[pn, mbn], writes=[km2n])
            kb.tt('dve', km1, km1, km2, ALU.add, reads=[km1n, km2n], writes=[km1n])
            kb.stt('dve', KM[:, g, :], identf, dcol[:, g:g + 1], km1, ALU.mult, ALU.add, reads=['identf', dcn, km1n], writes=[KMn])
        kb.dma(KMd, KM.rearrange("p g n -> p (g n)"), reads=[KMn], writes=['KMd'])

        HL4 = HL.rearrange("p (d a n) -> p d a n", d=2, a=32)

        def hn_tile(xrows, hnT, hnTn, col0, pools):
            xt_pool, junk_pool, hn_pool, sm_pool = pools
            xt, xn = xt_pool.get()
            kb.dma(xt, xrows, writes=[xn])
            jk, jn = junk_pool.get()
            smt, sn_ = sm_pool.get()
            kb.memset('dve', smt[:, 0:1], 0.0, writes=[sn_])
            kb.act(jk, xt, AF.Square, reads=[xn, sn_], writes=[jn, sn_], accum_out=smt[:, 0:1])
            kb.rstd(smt[:, 2:3], smt[:, 0:1], smt[:, 1:2], 1.0 / D, sn_)
            kb.stt('dve', xt, xt, smt[:, 2:3], G1, ALU.mult, ALU.mult, reads=[xn, sn_, 'G1'], writes=[xn])
            hn, hnn = hn_pool.get()
            kb.tt('dve', hn, xt, SHIFT, ALU.add, reads=[xn, 'SHIFT'], writes=[hnn])
            pt, pn = ptr.get()
            for k in range(8):
                kb.tr(pt[:, k * 128:(k + 1) * 128], hn[:, k * 128:(k + 1) * 128], ident,
                      reads=[hnn, 'ident'], writes=[pn], signal=(k == 7))
            kb.cp('act', hnT[:, :, col0:col0 + 128], pt[:, :].rearrange("p (k t) -> p k t", k=8),
                  reads=[pn], writes=[hnTn])

        def rope(eng, out, x, nh, cs_tile, t1_, t2_, rk, wk, tk):
            def xv(t, half):
                return V(t, half * 16, [[64, nh], [32, 2], [1, 16]])
            cosv = V(cs_tile, 0, [[0, nh], [16, 2], [1, 16]])
            sinv = V(cs_tile, 32, [[0, nh], [16, 2], [1, 16]])
            a = V(t1_, 0, [[32, nh], [16, 2], [1, 16]])
            b = V(t2_, 0, [[32, nh], [16, 2], [1, 16]])
            kb.tt(eng, a, xv(x, 0), cosv, ALU.mult, reads=rk, writes=[tk])
            kb.tt(eng, b, xv(x, 1), sinv, ALU.mult, reads=rk, writes=[tk])
            kb.tt(eng, xv(out, 0), a, b, ALU.subtract, reads=[tk], writes=wk)
            kb.tt(eng, a, xv(x, 0), sinv, ALU.mult, reads=rk + wk, writes=[tk])
            kb.tt(eng, b, xv(x, 1), cosv, ALU.mult, reads=rk + wk, writes=[tk])
            kb.tt(eng, xv(out, 1), a, b, ALU.add, reads=[tk], writes=wk)

        KT3 = KT.rearrange("p (h n) -> p h n", h=2)
        kb.memset('pool', KT[64:128, :], 0.0, writes=['KT'])
        VO = VA.rearrange("p (t h n) -> p t h n", h=2, n=128)
        kb.memset('pool', VO[:, :, :, 64:128], 1.0, writes=['VA'])

        def phase_A1(sus, is_sample, hook=None):
            new_phase()
            if hook is not None:
                hook()
            ci = 1 if is_sample else 0
            load_mod(0, ci)
            WA, WAn = AB_.alloc([128, 8, 768])
            S.groups[WAn] = [WAn + '#0', WAn + '#1']
            w0v = ab_w_in_b.rearrange("(k p) n -> p k n", p=128)
            kb.dma(WA[:, :, 0:256], w0v[:, :, 512:768], reads=wkeys['ab_in'], writes=[WAn + '#0'])
            kb.dma(WA[:, :, 256:768], w0v[:, :, 1280:1792], reads=wkeys['ab_in'], writes=[WAn + '#1'], q='act')
            WB, WBn_ = AB_.alloc([128, 2, 16, 2, 128])
            kb.dma(WB.rearrange("p d q c n -> p (d q c n)"), WBd, reads=['WBd'], writes=[WBn_])
            hnT, hnTn = AB_.alloc([128, 8, 1024])
            Ub, Ubn = AB_.alloc([128, 32, 8, 16])
            X, Xn = AB_.alloc([128, 32, 128])
            pools = (AF_.pool([128, D], 3), AB_.pool([128, D], 2), AB_.pool([128, D], 2), AF_.pool([128, 8], 4))
            sq_pool = AF_.pool([128, 128], 2)
            kvf_pool = AF_.pool([128, 256], 2)
            kb_pool = AB_.pool([128, 128], 8)
            rp_pool = AF_.pool([128, 64], 2)
            rt_pool = AF_.pool([128, 64], 2)
            sm_pool = pools[3]
            for su in sus:
                xsrc = xs if is_sample else xp
                row0 = (su - 1) * 1024 if is_sample else 0
                deferred = []
                for t in range(8):
                    hn_tile(xsrc[row0 + t * 128: row0 + (t + 1) * 128, :], hnT, hnTn, t * 128, pools)
                kb.dma(HNTd[su], hnT.rearrange("p k n -> p (k n)"), reads=[hnTn], writes=['HNTd%d' % su], q='act')
                for t in range(8):
                    r0 = row0 + t * 128
                    pt, pn = pacc.get()
                    for k in range(8):
                        kb.mm(pt[:, 0:256], hnT[:, k, t * 128:(t + 1) * 128], WA[:, k, 0:256], start=(k == 0), stop=(k == 7),
                              reads=[hnTn, WAn], writes=[pn])
                    sq, sqn = sq_pool.get()
                    smt, sn_ = sm_pool.get()
                    kb.act(sq, pt[:, 0:128], AF.Square, reads=[pn], writes=[sqn])
                    kb.rsum('dve', smt[:, 0:2], sq.rearrange("p (h d) -> p h d", h=2), reads=[sqn], writes=[sn_])
                    kb.rstd(smt[:, 4:6], smt[:, 0:2], smt[:, 2:4], 1.0 / 64, sn_)
                    kvf, kvn = kvf_pool.get()
                    for h in range(2):
                        kb.stt('dve', kvf[:, h * 64:(h + 1) * 64], pt[:, h * 64:(h + 1) * 64], smt[:, 4 + h:5 + h], kg,
                               ALU.mult, ALU.mult, reads=[pn, sn_, 'kg'], writes=[kvn])
                    kb.cp('act', kvf[:, 128:256], pt[:, 128:256], reads=[pn], writes=[kvn])
                    kbt, kbn = kb_pool.get()
                    if is_sample:
                        rp, rpn = rp_pool.get()
                        kb.dma(rp, rope_d[r0:r0 + 128, :], writes=[rpn])
                        rt, rtn = rt_pool.get()
                        ra, ran = rt_pool.get()
                        rope('dve', kbt, kvf[:, 0:128], 2, rp, rt, ra, [kvn, rpn], [kbn], rtn)
                        key0 = 512 + r0
                    else:
                        kb.dma(nk[r0:r0 + 128, :], kvf[:, 0:128], reads=[kvn], q='act')
                        kb.dma(nv[r0:r0 + 128, :], kvf[:, 128:256], reads=[kvn], q='act')
                        kb.cp('act', kbt, kvf[:, 0:128], reads=[kvn], writes=[kbn])
                        key0 = r0
                    deferred.append((kbt, kbn, key0))
                    kb.cp('dve', VO[:, key0 // 128, :, 0:64], kvf[:, 128:256].rearrange("p (h d) -> p h d", h=2), reads=[kvn], writes=['VA'])
                for s_ in range(8):
                    pt, pn = pacc.get()
                    for k in range(8):
                        lhs = V(hnT, k * 1024 + s_, [[8, 128]])
                        kb.mm(pt[:, :], lhs, WA[:, k, 256:768], start=(k == 0), stop=(k == 7), reads=[hnTn, WAn], writes=[pn])
                    eng = 'act' if s_ % 2 == 0 else 'dve'
                    kb.cp(eng, Ub[:, :, s_, :], pt[:, :].rearrange("p (g h) -> p g h", g=32), reads=[pn], writes=[Ubn])
                for (kbt, kbn, key0) in deferred:
                    ptt, ptn = ptr.get()
                    for h in range(2):
                        kb.tr(ptt[0:64, h * 128:(h + 1) * 128], kbt[:, h * 64:(h + 1) * 64], ident,
                              reads=[kbn, 'ident'], writes=[ptn], signal=(h == 1))
                    kb.cp('act', KT3[0:64, :, key0:key0 + 128], ptt[0:64, 0:256].rearrange("p (h n) -> p h n", h=2),
                          reads=[ptn], writes=['KT'])
                for gb in range(4):
                    pt, pn = ptr.get()
                    for gi in range(8):
                        g = gb * 8 + gi
                        kb.tr(pt[:, gi * 128:(gi + 1) * 128], Ub[:, g, :, :].rearrange("p s h -> p (s h)"), ident,
                              reads=[Ubn, 'ident'], writes=[pn], signal=(gi == 7))
                    eng = 'act' if gb % 2 == 0 else 'dve'
                    kb.cp(eng, X[:, gb * 8:(gb + 1) * 8, :].rearrange("p g n -> p (g n)"), pt[:, :], reads=[pn], writes=[Xn])
                kb.dma(XSd[su], X.rearrange("p g n -> p (g n)"), reads=[Xn], writes=['XSd%d' % su], q='act')
                if is_sample:
                    NC_ = 257
                else:
                    NC_ = 132
                for d in range(2):
                    for qb in range(8):
                        pt, pn = pacc.get()
                        for qi in range(2):
                            gq = qb * 2 + qi
                            for c in range(2):
                                col = (qi * 2 + c) * 128
                                for gh in range(2):
                                    kb.mm(pt[gh * 64:(gh + 1) * 64, col:col + 128], WB[:, d, gq, c, gh * 64:(gh + 1) * 64],
                                          X[:, 2 * gq + gh, :], start=True, stop=True, reads=[WBn_, Xn], writes=[pn],
                                          signal=(qi == 1 and c == 1 and gh == 1))
                        base = d * 32 * NC_ + (qb * 4) * NC_
                        if is_sample:
                            c0 = (su - 1) * 128 + (1 if d == 0 else 0)
                            outv = V(HL, base + c0, [[NC_, 4], [1, 128]])
                            inv = pt[:, :].rearrange("p (a n) -> p a n", a=4)
                        else:
                            outv = V(HL, (d * 32 + qb * 4) * 128, [[128, 4], [1, 128]])
                            inv = pt[:, :].rearrange("p (a n) -> p a n", a=4)
                        eng = 'act' if qb % 2 == 0 else 'dve'
                        kb.cp(eng, outv, inv, reads=[pn], writes=['HL'])

        def scan_body(is_sample):
            nseq = 1 if is_sample else 4
            J = 256 if is_sample else 32
            NC_ = 257 if is_sample else 132
            ST, STn = AF_.alloc([128, 2, 32, nseq])
            T1, T1n = AF_.alloc([128, 2, 32, nseq])
            T2, T2n = AF_.alloc([128, 2, 32, nseq])
            if is_sample:
                Hn, Hnn = AF_.alloc([64, 128])
                kb.dma(Hn, dram_ap(h0_d, 0, [[128, 64], [1, 128]]), writes=[Hnn])
                pt, pn = pacc.get()
                kb.tr(pt[:, 0:64], Hn, identf[0:64, 0:64], reads=[Hnn, 'identf'], writes=[pn])
                kb.cp('act', V(ST, 0, [[32, 2], [1, 2], [2, 16]]), pt[:, 0:64].rearrange("p (d c q) -> p d c q", d=2, c=2),
                      reads=[pn], writes=[STn])
            else:
                kb.memset('pool', ST, 0.0, writes=[STn])
            LLv = V(LLt, 0, [[32, 2], [1, 32], [0, nseq]])
            if is_sample:
                for d in range(2):
                    colv = V(HL, d * 32 * NC_ + (0 if d == 0 else J), [[NC_, 32], [J + 1, nseq]])
                    kb.cp('pool', colv, ST[:, d, :, :], reads=[STn], writes=['HL'])
            else:
                kb.memset('pool', V(HL, 8192, [[128, 32], [32, 4]]), 0.0, writes=['HL'])
                kb.memset('pool', V(HL, 8192 + 32 * 128 + 31, [[128, 32], [32, 4]]), 0.0, writes=['HL'])
            for i in range(J):
                if is_sample:
                    cf, cb_ = i + 1, J - 1 - i
                    hv_in = V(HL, cf, [[32 * NC_ + cb_ - cf, 2], [NC_, 32], [J + 1, nseq]])
                    hv_out = hv_in
                else:
                    hv_in = V(HL, i, [[32 * 128 + 31 - 2 * i, 2], [128, 32], [32, 4]])
                    hv_out = V(HL, 8192 + i + 1, [[32 * 128 + 29 - 2 * i, 2], [128, 32], [32, 4]]) if i < J - 1 else None
                kb.tt('pool', T1, ST, LLv, ALU.mult, reads=[STn, 'LLt'], writes=[T1n])
                for c in range(2):
                    stv = V(ST, (1 - c) * nseq, [[32 * nseq, 2], [2 * nseq, 16], [1, nseq]])
                    t2v = V(T2, c * nseq, [[32 * nseq, 2], [2 * nseq, 16], [1, nseq]])
                    lxv = V(LXt, c, [[32, 2], [2, 16], [0, nseq]])
                    kb.tt('pool', t2v, stv, lxv, ALU.mult, reads=[STn, 'LXt'], writes=[T2n])
                kb.tt('pool', T1, T1, T2, ALU.add, reads=[T1n, T2n], writes=[T1n])
                kb.tt('pool', ST, T1, hv_in, ALU.add, reads=[T1n, 'HL'], writes=[STn])
                if hv_out is not None:
                    kb.cp('pool', hv_out, ST, reads=[STn], writes=['HL'])
            return ST, STn

        def scan_finals(ST, STn):
            if True:
                STp, STpn = AF_.alloc([128, 256])
                nseq = 4
                for sq_ in range(4):
                    kb.cp('pool', V(STp, sq_ * 64, [[32, 2], [16, 2], [1, 16]]), V(ST, sq_, [[128, 2], [4, 2], [8, 16]]),
                          reads=[STn], writes=[STpn])
                FT, FTn = AF_.alloc([128, 256])
                for ch in range(2):
                    pt, pn = pacc.get()
                    kb.tr(pt[:, 0:128], STp[:, ch * 128:(ch + 1) * 128], identf, reads=[STpn, 'identf'], writes=[pn])
                    kb.cp('act', FT[:, ch * 128:(ch + 1) * 128], pt[:, 0:128], reads=[pn], writes=[FTn])
                    kb.dma(dram_ap(ns, ch * 16384, [[128, 128], [1, 128]]), FT[:, ch * 128:(ch + 1) * 128], reads=[FTn])

        pS = APPool(pacc.tiles[0:3])
        pO = APPool(pacc.tiles[3:5])
        pD = APPool(pacc.tiles[5:6])

        def load_cache():
            ckb, ckn = AB_.alloc([128, 4, 128])
            for t in range(4):
                kb.dma(ckb[:, t, :], ck_d[t * 128:(t + 1) * 128, :], writes=[ckn], q='pool')
                kb.dma(VO[:, t, :, 0:64], cv_d[t * 128:(t + 1) * 128, :].rearrange("p (h d) -> p h d", h=2), writes=['VA'], q='pool')
            for t in range(4):
                ptt, ptn = ptr.get()
                for h in range(2):
                    kb.tr(ptt[0:64, h * 128:(h + 1) * 128], ckb[:, t, h * 64:(h + 1) * 64], ident,
                          reads=[ckn, 'ident'], writes=[ptn], signal=(h == 1))
                kb.cp('act', KT3[0:64, :, t * 128:(t + 1) * 128], ptt[0:64, 0:256].rearrange("p (h n) -> p h n", h=2),
                      reads=[ptn], writes=['KT'])

        def phase_A2a(units, is_sample):
            new_phase()
            ci = 1 if is_sample else 0
            load_mod(0, ci)
            WQ, WQn = AB_.alloc([128, 8, 1536])
            S.groups[WQn] = [WQn + '#0', WQn + '#1', WQn + '#2']
            w0v = ab_w_in_b.rearrange("(k p) n -> p k n", p=128)
            kb.dma(WQ[:, :, 0:512], w0v[:, :, 0:512], reads=wkeys['ab_in'], writes=[WQn + '#0'])
            kb.dma(WQ[:, :, 512:1024], w0v[:, :, 768:1280], reads=wkeys['ab_in'], writes=[WQn + '#1'], q='act')
            kb.dma(WQ[:, :, 1024:1536], w0v[:, :, 1792:2304], reads=wkeys['ab_in'], writes=[WQn + '#2'])
            ST_, STn_ = scan_body(is_sample)
            NQ = 512 if is_sample else 256
            nu = NQ // 256
            cpb = 512 // NQ
            pools = (AF_.pool([128, D], 3), AB_.pool([128, D], 2), AB_.pool([128, D], 2), AF_.pool([128, 32], 4))
            sm_pool = pools[3]
            hnT_pool = AB_.pool([128, 8, NQ], 1)
            sq_pool = AF_.pool([128, 512], 2)
            qn_pool = AF_.pool([128, 512], 2)
            qb_pool = AB_.pool([128, 512], 2)
            rp_pool = AF_.pool([128, 64], 2)
            rt_pool = AF_.pool([128, 256], 2)
            qT_pool = AB_.pool([128, 8, NQ], 1)
            kb.memset('dve', qT_pool.tiles[0][0][64:128], 0.0, writes=[qT_pool.tiles[0][1]])
            GA_pool = AB_.pool([64, 8, NQ], 1)
            GB_pool = AB_.pool([128, 4, NQ], 1)
            MA_pool = AB_.pool([64, 8, NQ], 1)
            PT_pool = AB_.pool([128, 512], 6)
            of_pool = AF_.pool([128, NQ], 2)
            rd_pool = AF_.pool([64, NQ], 2)
            ot_pool = AF_.pool([64, NQ], 2)
            for gi in range(len(units) // nu):
                gu = units[gi * nu:(gi + 1) * nu]
                u = gu[0]
                row0 = (u - 4) * 256 if is_sample else u * 256
                xsrc = xs if is_sample else xp
                hnT, hnTn = hnT_pool.get()
                su_ = (1 + (u - 4) // 4) if is_sample else 0
                col0_ = ((u - 4) % 4) * 256 if is_sample else u * 256
                kb.dma(hnT, HNTd[su_].rearrange("p (k n) -> p k n", k=8)[:, :, col0_:col0_ + NQ], reads=['HNTd%d' % su_], writes=[hnTn])
                qT, qTn = qT_pool.get()
                for t in range(NQ // 128):
                    pt, pn = pS.get()
                    for k in range(8):
                        kb.mm(pt[:, :], hnT[:, k, t * 128:(t + 1) * 128], WQ[:, k, 0:512], start=(k == 0), stop=(k == 7),
                              reads=[hnTn, WQn], writes=[pn])
                    sq, sqn = sq_pool.get()
                    smt, sn_ = sm_pool.get()
                    kb.act(sq, pt[:, :], AF.Square, reads=[pn], writes=[sqn])
                    kb.rsum('dve', smt[:, 0:8], sq.rearrange("p (h d) -> p h d", h=8), reads=[sqn], writes=[sn_])
                    kb.rstd(smt[:, 16:24], smt[:, 0:8], smt[:, 8:16], 1.0 / 64, sn_)
                    qn, qnn = qn_pool.get()
                    kb.tt('dve', qn.rearrange("p (h d) -> p h d", h=8), pt[:, :].rearrange("p (h d) -> p h d", h=8),
                          V(smt, 16, [[1, 8], [0, 64]]), ALU.mult, reads=[pn, sn_], writes=[qnn])
                    qb, qbn = qb_pool.get()
                    if is_sample:
                        kb.tt('dve', qn.rearrange("p (h d) -> p h d", h=8), qn.rearrange("p (h d) -> p h d", h=8),
                              V(qg, 0, [[0, 8], [1, 64]]), ALU.mult, reads=[qnn, 'qg'], writes=[qnn])
                        rp, rpn = rp_pool.get()
                        kb.dma(rp, rope_d[row0 + t * 128: row0 + (t + 1) * 128, :], writes=[rpn])
                        rt, rtn = rt_pool.get()
                        ra, ran = rt_pool.get()
                        rope('dve', qb, qn, 8, rp, rt, ra, [qnn, rpn], [qbn], rtn)
                    else:
                        kb.tt('dve', qb.rearrange("p (h d) -> p h d", h=8), qn.rearrange("p (h d) -> p h d", h=8),
                              V(qg, 0, [[0, 8], [1, 64]]), ALU.mult, reads=[qnn, 'qg'], writes=[qbn])
                    ptt, ptn = ptr.get()
                    for h in range(8):
                        kb.tr(ptt[0:64, h * 128:(h + 1) * 128], qb[:, h * 64:(h + 1) * 64], ident,
                              reads=[qbn, 'ident'], writes=[ptn], signal=(h == 7))
                    kb.cp('act', qT[0:64, :, t * 128:(t + 1) * 128], ptt[0:64, :].rearrange("p (h n) -> p h n", h=8),
                          reads=[ptn], writes=[qTn])
                GA, GAn = GA_pool.get()
                for hb in range(8 // cpb):
                    pt, pn = pS.get()
                    for hi in range(cpb):
                        h = hb * cpb + hi
                        for k in range(8):
                            kb.mm(pt[0:64, hi * NQ:(hi + 1) * NQ], WQ[:, k, 512 + h * 64:512 + (h + 1) * 64], hnT[:, k, :],
                                  start=(k == 0), stop=(k == 7), reads=[hnTn, WQn], writes=[pn], signal=(k == 7 and hi == cpb - 1))
                    kb.act(GA[:, hb * cpb:(hb + 1) * cpb, :].rearrange("p h n -> p (h n)"), pt[0:64, :], AF.Silu, reads=[pn], writes=[GAn])
                GB, GBn = GB_pool.get()
                for cb in range(4 // cpb):
                    pt, pn = pS.get()
                    for ci_ in range(cpb):
                        cc = cb * cpb + ci_
                        for k in range(8):
                            kb.mm(pt[:, ci_ * NQ:(ci_ + 1) * NQ], WQ[:, k, 1024 + cc * 128:1024 + (cc + 1) * 128], hnT[:, k, :],
                                  start=(k == 0), stop=(k == 7), reads=[hnTn, WQn], writes=[pn], signal=(k == 7 and ci_ == cpb - 1))
                    kb.act(GB[:, cb * cpb:(cb + 1) * cpb, :].rearrange("p c n -> p (c n)"), pt[:, :], AF.Silu, reads=[pn], writes=[GBn])
                for i_, uu in enumerate(gu):
                    kb.dma(GBd[uu].rearrange("p (c n) -> p c n", c=4), GB[:, :, i_ * 256:(i_ + 1) * 256], reads=[GBn], writes=['GBd%d' % uu], q='act')
                if is_sample:
                    key0, nkc = 0, 20
                else:
                    key0, nkc = u * 256, 2
                MA, MAn = MA_pool.get()
                nb = nkc // cpb
                pending = [None]

                def emit_S(h, b):
                    kvh = h // 4
                    pt, pn = pS.get()
                    for j in range(cpb):
                        kk = key0 + (b * cpb + j) * 128
                        kb.mm(pt[:, j * NQ:(j + 1) * NQ], KT3[:, kvh, kk:kk + 128], qT[:, h, :], start=True, stop=True,
                              reads=['KT', qTn], writes=[pn], signal=(j == cpb - 1))
                    return pt, pn

                def make_epilogue(h, po, pon):
                    def epi():
                        of, ofn = of_pool.get()
                        kb.cp('dve', of, po[:, 0:NQ], reads=[pon], writes=[ofn])
                        pd, pdn = pD.get()
                        kb.mm(pd[0:64, 0:NQ], identf[:, 64:128], of, start=True, stop=True, reads=['identf', ofn], writes=[pdn])
                        rd, rdn = rd_pool.get()
                        S.op('dve', (lambda o_, i_: (lambda e: e.reciprocal(out=o_, in_=i_)))(rd, pd[0:64, 0:NQ]), reads=[pdn], writes=[rdn])
                        ot, otn = ot_pool.get()
                        kb.tt('dve', ot, of[0:64, :], rd, ALU.mult, reads=[ofn, rdn], writes=[otn])
                        kb.tt('dve', MA[:, h, :], ot, GA[:, h, :], ALU.mult, reads=[otn, GAn], writes=[MAn])
                    return epi

                for h in range(8):
                    kvh = h // 4
                    po, pon = pO.get()
                    sq_ = [emit_S(h, 0)]
                    for b_ in range(1, min(3, nb)):
                        sq_.append(emit_S(h, b_))
                    if pending[0] is not None:
                        pending[0]()
                        pending[0] = None
                    for b in range(nb):
                        pt, pn = sq_[b]
                        PT, PTn = PT_pool.get()
                        kb.act(PT, pt[:, :], AF.Exp, reads=[pn], writes=[PTn])
                        if b + 3 < nb:
                            sq_.append(emit_S(h, b + 3))
                        for j in range(cpb):
                            kc = b * cpb + j
                            vt = (key0 // 128) + kc
                            first, last = (kc == 0), (kc == nkc - 1)
                            kb.mm(po[:, 0:NQ], VO[:, vt, kvh, :], PT[:, j * NQ:(j + 1) * NQ],
                                  start=first, stop=last, reads=['VA', PTn], writes=[pon], signal=last)
                    pending[0] = make_epilogue(h, po, pon)
                pending[0]()
                for i_, uu in enumerate(gu):
                    kb.dma(MIXAd[uu].rearrange("p (h n) -> p h n", h=8), MA[:, :, i_ * 256:(i_ + 1) * 256], reads=[MAn], writes=['MIXAd%d' % uu], q='act')
            if not is_sample:
                scan_finals(ST_, STn_)

        def phase_A2b(sus, is_sample, out_ap):
            new_phase()
            ci = 1 if is_sample else 0
            load_mod(0, ci)
            NC_ = 257 if is_sample else 132
            WCt, WCtn = AB_.alloc([128, 2, 16, 2, 128])
            kb.dma(WCt.rearrange("p d q c n -> p (d q c n)"), WCd, reads=['WCd'], writes=[WCtn])
            KMt, KMtn = AB_.alloc([128, 32, 128])
            kb.dma(KMt.rearrange("p g n -> p (g n)"), KMd, reads=['KMd'], writes=[KMtn], q='act')
            X, Xn = AB_.alloc([128, 32, 128])
            WOa, WOan = AB_.alloc([128, 4, 1024])
            WOb, WObn = AB_.alloc([128, 4, 1024])
            WG, WGn = AB_.alloc([128, 4, 512])
            wov = ab_w_out_b.rearrange("(k p) n -> p k n", p=128)
            kb.dma(WOa, wov[:, 0:4, :], reads=wkeys['ab_out'], writes=[WOan])
            kb.dma(WOb, wov[:, 4:8, :], reads=wkeys['ab_out'], writes=[WObn], q='act')
            kb.dma(WG, glu_w_b.rearrange("(k p) n -> p k n", p=128), reads=wkeys['glu'], writes=[WGn])
            glub, glubn = AF_.alloc([128, 4])
            load_T(glub, glubn, dram_ap(glu_b, 0, [[128, 4], [1, 128]]), 4)
            GBt, GBtn = AB_.alloc([128, 4, 1024])
            MAt, MAtn = AB_.alloc([128, 4, 1024])
            Ybm, Ybn = AB_.alloc([128, 8, 512])
            yT, yTn = AB_.alloc([128, 4, 1024])
            mixB, mixBn = GBt, GBtn
            sg_pool = AB_.pool([128, 512], 2)
            xt_pool = AF_.pool([128, D], 2)
            ot_pool = AF_.pool([128, D], 2)
            jk_pool = AB_.pool([128, 512], 2)
            sm_pool = AF_.pool([128, 8], 4)
            for su in sus:
              u0 = su * 4
              kb.dma(X.rearrange("p g n -> p (g n)"), XSd[su], reads=['XSd%d' % su], writes=[Xn])
              for ul in range(4):
                  kb.dma(GBt[:, :, ul * 256:(ul + 1) * 256], GBd[u0 + ul].rearrange("p (c n) -> p c n", c=4),
                         reads=['GBd%d' % (u0 + ul)], writes=[GBtn])
                  for h in range(8):
                      kb.dma(MAt[(h % 2) * 64:(h % 2) * 64 + 64, h // 2, ul * 256:(ul + 1) * 256], MIXAd[u0 + ul][:, h * 256:(h + 1) * 256],
                             reads=['MIXAd%d' % (u0 + ul)], writes=['%s_%d_%d' % (MAtn, ul, h)])
              for gb in range(8):
                  pt, pn = pacc.get()
                  gh = gb % 2
                  hs = slice(gh * 64, (gh + 1) * 64)
                  firstmm = True
                  for gi in range(4):
                      gq = (gb // 2) * 4 + gi
                      osl = slice(gi * 128, (gi + 1) * 128)
                      for d in range(2):
                          for c in range(2):
                              if is_sample:
                                  off = (d * 32 + gq * 2 + c) * 257 + (su - 1) * 128 + (1 if d == 1 else 0)
                              else:
                                  off = 8192 + (d * 32 + gq * 2 + c) * 128
                              lhs = V(HL, off, [[1, 128]], p0=gh * 64, np_=64)
                              kb.mm(pt[:, osl], lhs, WCt[hs, d, gq, c, :], start=firstmm, stop=False,
                                    reads=['HL', WCtn], writes=[pn], signal=False, sgc=True)
                              firstmm = False
                  for gi in range(4):
                      gq = (gb // 2) * 4 + gi
                      g = 2 * gq + gh
                      osl = slice(gi * 128, (gi + 1) * 128)
                      kb.mm(pt[:, osl], X[:, g, 0:128], KMt[:, g, :], start=False, stop=True,
                            reads=[Xn, KMtn], writes=[pn], signal=(gi == 3), sgc=True)
                  outv = V(Ybm, (2 * ((gb // 2) * 4) + (gb % 2)) * 16, [[32, 4], [512, 8], [1, 16]])
                  inv = V(pt[:, :], 0, [[128, 4], [16, 8], [1, 16]])
                  kb.act(outv, inv, AF.Gelu_apprx_tanh, reads=[pn], writes=[Ybn])
              for cc in range(4):
                  ptt, ptn = ptr.get()
                  for t in range(8):
                      kb.tr(ptt[:, t * 128:(t + 1) * 128], Ybm[:, t, cc * 128:(cc + 1) * 128], ident,
                            reads=[Ybn, 'ident'], writes=[ptn], signal=(t == 7))
                  eng = 'act' if cc % 2 == 0 else 'dve'
                  kb.cp(eng, V(yT, cc * 1024, [[1, 8], [8, 128]]), ptt[:, :].rearrange("p (t j) -> p t j", t=8),
                        reads=[ptn], writes=[yTn])
              for nb in range(2):
                  cols = slice(nb * 512, (nb + 1) * 512)
                  for oc in range(4):
                      pt, pn = pacc.get()
                      for cc in range(4):
                          kb.mm(pt[:, :], WG[:, cc, oc * 128:(oc + 1) * 128], yT[:, cc, cols], start=(cc == 0), stop=(cc == 3),
                                reads=[WGn, yTn], writes=[pn])
                      sg, sgn = sg_pool.get()
                      kb.act(sg, pt[:, :], AF.Sigmoid, reads=[pn, glubn], writes=[sgn], bias=glub[:, oc:oc + 1])
                      kb.tt('dve', sg, sg, yT[:, oc, cols], ALU.mult, reads=[sgn, yTn], writes=[sgn])
                      kb.tt('dve', GBt[:, oc, cols], sg, GBt[:, oc, cols], ALU.mult, reads=[sgn, GBtn], writes=[GBtn])
              xsrc = xs if is_sample else xp
              row0 = (su - 1) * 1024 if is_sample else 0
              for t in range(8):
                  tc_ = slice(t * 128, (t + 1) * 128)
                  xt, xn = xt_pool.get()
                  kb.dma(xt, xsrc[row0 + t * 128: row0 + (t + 1) * 128, :], writes=[xn])
                  smt, sn_ = sm_pool.get()
                  kb.memset('dve', smt[:, 0:2], 0.0, writes=[sn_])
                  pts = []
                  for hf in range(2):
                      pt, pn = pacc.get()
                      oc = slice(hf * 512, (hf + 1) * 512)
                      for h in range(4):
                          kb.mm(pt[:, :], MAt[:, h, tc_], WOa[:, h, oc], start=(h == 0), stop=False, reads=['%s_%d_%d' % (MAtn, t // 2, 2 * h + e_) for e_ in range(2)] + [WOan], writes=[pn], signal=False)
                      for cc in range(4):
                          kb.mm(pt[:, :], mixB[:, cc, tc_], WOb[:, cc, oc], start=False, stop=(cc == 3), reads=[mixBn, WObn], writes=[pn], signal=(cc == 3))
                      jk, jn = jk_pool.get()
                      kb.act(jk, pt[:, :], AF.Square, reads=[pn, sn_], writes=[jn, sn_], accum_out=smt[:, hf:hf + 1])
                      pts.append((pt, pn))
                  kb.tt('dve', smt[:, 2:3], smt[:, 0:1], smt[:, 1:2], ALU.add, reads=[sn_], writes=[sn_])
                  kb.rstd(smt[:, 4:5], smt[:, 2:3], smt[:, 3:4], 1.0 / D, sn_)
                  ot, otn = ot_pool.get()
                  for hf in range(2):
                      pt, pn = pts[hf]
                      oc = slice(hf * 512, (hf + 1) * 512)
                      kb.stt('dve', ot[:, oc], pt[:, :], smt[:, 4:5], G2[:, oc], ALU.mult, ALU.mult, reads=[pn, sn_, 'G2'], writes=[otn])
                  kb.tt('dve', ot, ot, xt, ALU.add, reads=[otn, xn], writes=[otn])
                  kb.dma(out_ap[row0 + t * 128: row0 + (t + 1) * 128, :], ot, reads=[otn], writes=['X1_%d_%d' % (su, t)], q='act')

        L0_ONLY = (STAGE == 1)
        phase_A1([0], False)
        phase_A2a([0, 1, 2, 3], False)
        phase_A2b([0], False, yp if L0_ONLY else X1[0:1024, :])
        phase_A1([1, 2], True, hook=lambda: (load_cache(), convert_late()))
        phase_A2a(list(range(4, 12)), True)
        phase_A2b([1, 2], True, ys if L0_ONLY else X1[1024:NTOK, :])

        AA = HL[:, 0:16384].rearrange("p (t c n) -> p t c n", t=16, c=2)

        def phase_B1(sus, is_sample):
            new_phase()
            ci = 1 if is_sample else 0
            load_mod(1, ci)
            W1, W1n = AB_.alloc([128, 8, 2560])
            S.groups[W1n] = [W1n + '#0', W1n + '#1']
            w1v = cd_w_in_b.rearrange("(k p) n -> p k n", p=128)
            kb.dma(W1[:, 0:4, :], w1v[:, 0:4, :], reads=wkeys['cd_in'], writes=[W1n + '#0'])
            kb.dma(W1[:, 4:8, :], w1v[:, 4:8, :], reads=wkeys['cd_in'], writes=[W1n + '#1'], q='act')
            Wsb, Wsbn = AB_.alloc([128, 8, 128])
            kb.dma(Wsb, w_s_b.rearrange("h t s -> t h s"), reads=wkeys['ws'], writes=[Wsbn])
            WsT, WsTn = AB_.alloc([128, 8, 128])
            ptt, ptn = ptr.get()
            for hg in range(8):
                kb.tr(ptt[:, hg * 128:(hg + 1) * 128], Wsb[:, hg, :], ident, reads=[Wsbn, 'ident'], writes=[ptn], signal=(hg == 7))
            kb.cp('act', WsT.rearrange("p h n -> p (h n)"), ptt[:, :], reads=[ptn], writes=[WsTn])
            C128, C128n = AB_.alloc([128, 128])
            S128, S128n = AB_.alloc([128, 128])
            kb.dma(C128, c128_d, writes=[C128n])
            kb.dma(S128, s128_d, writes=[S128n])
            VG, VGn = AF_.alloc([128, 512])
            kb.dma(VG, bcast_rows(v_gain, 128), writes=[VGn])
            bs, bsn = AF_.alloc([128, 8])
            load_T(bs, bsn, b_s, 8)
            pools = (AF_.pool([128, D], 3), AB_.pool([128, D], 2), AB_.pool([128, D], 2), AF_.pool([128, 8], 4))
            sm_pool = pools[3]
            hnT_pool = AB_.pool([128, 8, 256], 2)
            cuG_pool = AB_.pool([128, 512], 2)
            cvg_pool = AF_.pool([128, 512], 2)
            cvn_pool = AB_.pool([128, 512], 2)
            gcS_pool = AB_.pool([128, 512], 2)
            oc_pool = AF_.pool([128, 512], 2)
            ocb_pool = AB_.pool([128, 512], 2)
            jk_pool = AB_.pool([128, 512], 2)
            mixC_pool = AB_.pool([128, 4, 256], 2)
            dzT_pool = AB_.pool([128, 4, 256], 2)
            GD_pool = AB_.pool([128, 4, 256], 2)
            def proj(hnT, hnTn, t, c0):
                pt, pn = pacc.get()
                for k in range(8):
                    kb.mm(pt[:, :], hnT[:, k, t * 128:(t + 1) * 128], W1[:, k, c0:c0 + 512], start=(k == 0), stop=(k == 7),
                          reads=[hnTn, W1n], writes=[pn])
                return pt, pn

            for ul4 in range(4 * len(sus)):
                su, ul = sus[ul4 // 4], ul4 % 4
                g0 = 0 if not is_sample else 1024 + (su - 1) * 1024
                u = su * 4 + ul
                hnT, hnTn = hnT_pool.get()
                for t in range(2):
                    r0 = g0 + ul * 256 + t * 128
                    hn_tile(X1[r0:r0 + 128, :], hnT, hnTn, t * 128, pools)
                mixC, mixCn = mixC_pool.get()
                tiles_ = []
                for t in range(2):
                    pt, pn = proj(hnT, hnTn, t, 0)
                    cuG, cuGn = cuG_pool.get()
                    kb.act(cuG, pt[:, :], AF.Gelu_apprx_tanh, reads=[pn], writes=[cuGn])
                    pt, pn = proj(hnT, hnTn, t, 512)
                    cvg, cvgn = cvg_pool.get()
                    kb.act(cvg, pt[:, :], AF.Gelu_apprx_tanh, reads=[pn], writes=[cvgn])
                    smt, sn_ = sm_pool.get()
                    kb.memset('dve', smt[:, 0:1], 0.0, writes=[sn_])
                    jk, jn = jk_pool.get()
                    kb.act(jk, cvg, AF.Square, reads=[cvgn, sn_], writes=[jn, sn_], accum_out=smt[:, 0:1])
                    kb.rstd(smt[:, 2:3], smt[:, 0:1], smt[:, 1:2], 1.0 / 512, sn_)
                    cvn, cvnn = cvn_pool.get()
                    kb.stt('dve', cvn, cvg, smt[:, 2:3], VG, ALU.mult, ALU.mult, reads=[cvgn, sn_, VGn], writes=[cvnn])
                    pt, pn = proj(hnT, hnTn, t, 1024)
                    gcS, gcSn = gcS_pool.get()
                    kb.act(gcS, pt[:, :], AF.Silu, reads=[pn], writes=[gcSn])
                    tiles_.append((cuG, cuGn, cvn, cvnn, gcS, gcSn))
                dzT, dzTn = dzT_pool.get()
                GD, GDn = GD_pool.get()
                for (c0, dst, dstn, fn_) in ((1536, dzT, dzTn, None), (2048, GD, GDn, AF.Silu)):
                    for cb in range(2):
                        pt, pn = pacc.get()
                        for ci_ in range(2):
                            cc = cb * 2 + ci_
                            for k in range(8):
                                kb.mm(pt[:, ci_ * 256:(ci_ + 1) * 256], W1[:, k, c0 + cc * 128:c0 + (cc + 1) * 128], hnT[:, k, :],
                                      start=(k == 0), stop=(k == 7), reads=[hnTn, W1n], writes=[pn], signal=(k == 7 and ci_ == 1))
                        dv = dst[:, cb * 2:(cb + 1) * 2, :].rearrange("p c n -> p (c n)")
                        if fn_ is None:
                            kb.cp('dve', dv, pt[:, :], reads=[pn], writes=[dstn])
                        else:
                            kb.act(dv, pt[:, :], fn_, reads=[pn], writes=[dstn])
                kb.dma(GDd[u], GD.rearrange("p c n -> p (c n)"), reads=[GDn], writes=['GDd%d' % u], q='act')
                for t in range(2):
                    cuG, cuGn, cvn, cvnn, gcS, gcSn = tiles_[t]
                    pt, pn = pacc.get()
                    for hg in range(8):
                        kb.mm(pt[:, hg * 64:(hg + 1) * 64], WsT[:, hg, :], cvn[:, hg * 64:(hg + 1) * 64], start=True, stop=True,
                              reads=[WsTn, cvnn], writes=[pn], signal=(hg == 7))
                    oc, ocn = oc_pool.get()
                    for hg in range(8):
                        kb.stt('dve', oc[:, hg * 64:(hg + 1) * 64], pt[:, hg * 64:(hg + 1) * 64], bs[:, hg:hg + 1],
                               cuG[:, hg * 64:(hg + 1) * 64], ALU.add, ALU.mult, reads=[pn, bsn, cuGn], writes=[ocn])
                    ocb, ocbn = ocb_pool.get()
                    kb.tt('dve', ocb, oc, gcS, ALU.mult, reads=[ocn, gcSn], writes=[ocbn])
                    tiles_[t] = (ocb, ocbn)
                for t in range(2):
                    tix = (ul * 2 + t) if not is_sample else ((su - 1) * 8 + ul * 2 + t)
                    for cs_, tab, tabn in ((0, C128, C128n), (1, S128, S128n)):
                        pt, pn = pacc.get()
                        for grp in range(4):
                            kb.mm(pt[:, grp * 128:(grp + 1) * 128], dzT[:, grp, t * 128:(t + 1) * 128], tab, start=True, stop=True,
                                  reads=[dzTn, tabn], writes=[pn], signal=(grp == 3))
                        eng = 'act' if cs_ == 0 else 'dve'
                        kb.cp(eng, AA[:, tix, cs_, :], pt[:, :], reads=[pn], writes=['AA'])
                for t in range(2):
                    ocb, ocbn = tiles_[t]
                    ptt, ptn = ptr.get()
                    for cc in range(4):
                        kb.tr(ptt[:, cc * 128:(cc + 1) * 128], ocb[:, cc * 128:(cc + 1) * 128], ident,
                              reads=[ocbn, 'ident'], writes=[ptn], signal=(cc == 3))
                    kb.cp('act', mixC[:, :, t * 128:(t + 1) * 128], ptt[:, 0:512].rearrange("p (c n) -> p c n", c=4),
                          reads=[ptn], writes=[mixCn])
                kb.dma(MIXCd[u], mixC.rearrange("p c n -> p (c n)"), reads=[mixCn], writes=['MIXCd%d' % u], q='act')

        def phase_B2(is_sample):
            new_phase()
            ci = 1 if is_sample else 0
            load_mod(1, ci)
            WF, WFn = AB_.alloc([128, 4, 512])
            WO, WOn = AB_.alloc([128, 8, 1024])
            S.groups[WOn] = [WOn + '#0', WOn + '#1']
            kb.dma(WF, fnet_w_b.rearrange("(k p) n -> p k n", p=128), reads=wkeys['fnet'], writes=[WFn])
            wo1v = cd_w_out_b.rearrange("(k p) n -> p k n", p=128)
            kb.dma(WO[:, 0:4, :], wo1v[:, 0:4, :], reads=wkeys['cd_out'], writes=[WOn + '#0'])
            kb.dma(WO[:, 4:8, :], wo1v[:, 4:8, :], reads=wkeys['cd_out'], writes=[WOn + '#1'], q='act')
            nst = 16 if is_sample else 2
            NT = 512 if is_sample else 256
            H2 = 2 if is_sample else 1
            nsh = nst // H2
            CL_pool = AB_.pool([128, nsh, NT], H2)
            SL_pool = AB_.pool([128, nsh, NT], H2)
            if not is_sample:
                CLt, CLn = CL_pool.get()
                SLt, SLn_ = SL_pool.get()
                kb.dma(CLt, clp_d.rearrange("(st p) t -> p st t", p=128), writes=[CLn])
                kb.dma(SLt, slp_d.rearrange("(st p) t -> p st t", p=128), writes=[SLn_])
            fzT_pool = AB_.pool([128, 4, NT], 2)
            mixC_pool = AB_.pool([128, 4, NT], 2)
            GD_pool = AB_.pool([128, 4, NT], 2)
            xt_pool = AF_.pool([128, D], 2)
            ot_pool = AF_.pool([128, D], 2)
            jk_pool = AB_.pool([128, 512], 2)
            sm_pool = AF_.pool([128, 8], 4)
            nblk = 4
            for blk in range(nblk):
                if is_sample:
                    units = [4 + blk * 2, 4 + blk * 2 + 1]
                    t0 = 0
                    g0 = 1024 + blk * 512
                    out_ap, orow0 = ys, blk * 512
                else:
                    units = [blk]
                    t0 = blk * 2
                    g0 = blk * 256
                    out_ap, orow0 = yp, blk * 256
                mixC, mixCn = mixC_pool.get()
                GD, GDn = GD_pool.get()
                for i_, u in enumerate(units):
                    kb.dma(mixC[:, :, i_ * 256:(i_ + 1) * 256], MIXCd[u].rearrange("p (c n) -> p c n", c=4), reads=['MIXCd%d' % u], writes=[mixCn])
                    kb.dma(GD[:, :, i_ * 256:(i_ + 1) * 256], GDd[u].rearrange("p (c n) -> p c n", c=4), reads=['GDd%d' % u], writes=[GDn])
                fzT, fzTn = fzT_pool.get()
                accs = [pacc.get() for _ in range(4)]
                for hf in range(H2):
                    if is_sample:
                        CLt, CLn = CL_pool.get()
                        SLt, SLn_ = SL_pool.get()
                        rows = slice(hf * nsh * 128, (hf + 1) * nsh * 128)
                        kb.dma(CLt, cls_d[rows, :].rearrange("(st p) t -> p st t", p=128)[:, :, blk * 512:(blk + 1) * 512], writes=[CLn])
                        kb.dma(SLt, sls_d[rows, :].rearrange("(st p) t -> p st t", p=128)[:, :, blk * 512:(blk + 1) * 512], writes=[SLn_])
                    for chk in range(4):
                        pt, pn = accs[chk]
                        for st in range(nsh):
                            for cs_, tab, tabn in ((0, CLt, CLn), (1, SLt, SLn_)):
                                first = (hf == 0 and st == 0 and cs_ == 0)
                                lastg = (st == nsh - 1 and cs_ == 1)
                                kb.mm(pt[:, 0:NT], AA[:, t0 + hf * nsh + st, cs_, chk * 128:(chk + 1) * 128], tab[:, st, :],
                                      start=first, stop=(lastg and hf == H2 - 1), reads=['AA', tabn], writes=[pn], signal=lastg)
                for chk in range(4):
                    pt, pn = accs[chk]
                    eng = 'act' if chk % 2 == 0 else 'dve'
                    kb.cp(eng, fzT[:, chk, :], pt[:, 0:NT], reads=[pn], writes=[fzTn])
                for oc in range(4):
                    pt, pn = pacc.get()
                    for chk in range(4):
                        kb.mm(pt[:, 0:NT], WF[:, chk, oc * 128:(oc + 1) * 128], fzT[:, chk, :], start=(chk == 0), stop=(chk == 3),
                              reads=[WFn, fzTn], writes=[pn])
                    kb.tt('dve', GD[:, oc, :], pt[:, 0:NT], GD[:, oc, :], ALU.mult, reads=[pn, GDn], writes=[GDn])
                for t in range(NT // 128):
                    tc_ = slice(t * 128, (t + 1) * 128)
                    xt, xn = xt_pool.get()
                    kb.dma(xt, X1[g0 + t * 128: g0 + (t + 1) * 128, :], writes=[xn])
                    smt, sn_ = sm_pool.get()
                    kb.memset('dve', smt[:, 0:2], 0.0, writes=[sn_])
                    pts = []
                    for hf in range(2):
                        pt, pn = pacc.get()
                        oc = slice(hf * 512, (hf + 1) * 512)
                        for kc in range(4):
                            kb.mm(pt[:, :], mixC[:, kc, tc_], WO[:, kc, oc], start=(kc == 0), stop=False, reads=[mixCn, WOn], writes=[pn], signal=False)
                        for kc in range(4):
                            kb.mm(pt[:, :], GD[:, kc, tc_], WO[:, 4 + kc, oc], start=False, stop=(kc == 3), reads=[GDn, WOn], writes=[pn], signal=(kc == 3))
                        jk, jn = jk_pool.get()
                        kb.act(jk, pt[:, :], AF.Square, reads=[pn, sn_], writes=[jn, sn_], accum_out=smt[:, hf:hf + 1])
                        pts.append((pt, pn))
                    kb.tt('dve', smt[:, 2:3], smt[:, 0:1], smt[:, 1:2], ALU.add, reads=[sn_], writes=[sn_])
                    kb.rstd(smt[:, 4:5], smt[:, 2:3], smt[:, 3:4], 1.0 / D, sn_)
                    ot, otn = ot_pool.get()
                    for hf in range(2):
                        pt, pn = pts[hf]
                        oc = slice(hf * 512, (hf + 1) * 512)
                        kb.stt('dve', ot[:, oc], pt[:, :], smt[:, 4:5], G2[:, oc], ALU.mult, ALU.mult, reads=[pn, sn_, 'G2'], writes=[otn])
                    kb.tt('dve', ot, ot, xt, ALU.add, reads=[otn, xn], writes=[otn])
                    kb.dma(out_ap[orow0 + t * 128: orow0 + (t + 1) * 128, :], ot, reads=[otn], q='act')

        if not L0_ONLY:
            phase_B1([0], False)
            phase_B2(False)
            phase_B1([1, 2], True)
            phase_B2(True)

        with nc.allow_low_precision("bf16 matmul operands, fp32 accumulation"):
            kb.S.run()
    return kb


_CACHE = {}


def _dft_tables():
    if 'dft' in _CACHE:
        return _CACHE['dft']
    bf = lambda a: np.ascontiguousarray(a.astype(np.float32)).astype(ml_dtypes.bfloat16)
    i128 = np.arange(128, dtype=np.int64)
    m = (i128[:, None] * i128[None, :]) % 128
    a = 2 * np.pi * m / 128.0
    out = {'c128': bf(np.cos(a)), 's128': bf(np.sin(a))}
    for nm, L in (('p', 256), ('s', SL)):
        i = np.arange(L, dtype=np.int64)
        m = (i[:, None] * i[None, :]) % L
        a = 2 * np.pi * m / float(L)
        sc = 1.0 / math.sqrt(128.0 * L)
        out['cl' + nm] = bf(np.cos(a) * sc)
        out['sl' + nm] = bf(-np.sin(a) * sc)
    _CACHE['dft'] = out
    return out


def kernel(**inp):
    if 'kb' not in _CACHE:
        _CACHE['kb'] = build()
    kb = _CACHE['kb']
    f = lambda a: np.ascontiguousarray(np.asarray(a, dtype=np.float32))
    x_prompt = f(inp['x_prompt'])
    x_sample = f(inp['x_sample'])
    c = f(inp['c'])
    c_ctx = f(inp['c_ctx'])
    ident = np.eye(128, dtype=np.float32)
    si = np.arange(128) // 16
    maskf = (si[None, :] >= si[:, None]).astype(np.float32)
    maskb = (si[:, None] >= si[None, :]).astype(np.float32)
    tpos = np.arange(SL)
    inv = (10000.0 ** (-np.arange(16, dtype=np.float32) / 16)).astype(np.float32)
    row = (tpos // 64).astype(np.float32)
    col = (tpos % 64).astype(np.float32)
    ang = np.concatenate([row[:, None] * inv[None, :], col[:, None] * inv[None, :]], 1).astype(np.float32)
    rope = np.concatenate([np.cos(ang), np.sin(ang)], 1).astype(np.float32)
    shared = {
        'ada_w': f(inp['ada_w']), 'ada_b': f(inp['ada_b']), 'norm_pre': f(inp['norm_pre']), 'norm_post': f(inp['norm_post']),
        'ab_w_in': f(inp['ab_w_in'])[0], 'q_gain': f(inp['ab_q_norm'])[0], 'k_gain': f(inp['ab_k_norm'])[0],
        'lam_re': f(inp['ssm_lambda_re'])[0].reshape(-1), 'lam_im': f(inp['ssm_lambda_im'])[0].reshape(-1),
        'log_dt': f(inp['ssm_log_dt'])[0].reshape(-1),
        'b_re': f(inp['ssm_b_re'])[0].reshape(-1), 'b_im': f(inp['ssm_b_im'])[0].reshape(-1),
        'c_re': f(inp['ssm_c_re'])[0].reshape(-1), 'c_im': f(inp['ssm_c_im'])[0].reshape(-1),
        'ssm_d': f(inp['ssm_d'])[0], 'glu_w': f(inp['ssm_glu_w'])[0], 'glu_b': f(inp['ssm_glu_b'])[0],
        'ab_w_out': f(inp['ab_w_out'])[0],
        'ident': ident.astype(ml_dtypes.bfloat16), 'identf': ident, 'maskf': maskf, 'maskb': maskb, 'rope': rope,
        'cd_w_in': f(inp['cd_w_in'])[0], 'v_gain': f(inp['gmlp_v_norm'])[0], 'w_s': f(inp['gmlp_w_s'])[0], 'b_s': f(inp['gmlp_b_s'])[0],
        'fnet_w': f(inp['fnet_w'])[0], 'cd_w_out': f(inp['cd_w_out'])[0],
    }
    shared.update(_dft_tables())
    used = set(kb.din.keys())
    maps = []
    for core in range(NCORES):
        b = core // 4
        m = dict(shared)
        m['xp'] = x_prompt[core * NPS:(core + 1) * NPS].reshape(NPS * SEQ, D)
        m['xs'] = x_sample[b]
        m['cvec'] = np.stack([c_ctx, c[b]], 0)
        m['ck'] = f(inp['cache_k'])[b, 0].reshape(512, 128)
        m['cv'] = f(inp['cache_v'])[b, 0].reshape(512, 128)
        m['h0'] = f(inp['state_ssm'])[b, 0].reshape(-1)
        maps.append({k: v for k, v in m.items() if k in used})
    res = run_bass_kernel_spmd(kb.nc, maps, core_ids=list(range(NCORES)))
    r = res.results
    B = 32
    cat = lambda name, shp: np.concatenate([np.asarray(r[c_][name], dtype=np.float32).reshape(shp) for c_ in range(NCORES)], 0)
    y_prompt = cat('yp', (NPS, SEQ, D))
    y_sample = np.stack([np.asarray(r[0]['ys'], dtype=np.float32), np.asarray(r[4]['ys'], dtype=np.float32)], 0).reshape(2, SL, D)
    nk = cat('nk', (NPS, 1, SEQ, 2, 64))
    nv = cat('nv', (NPS, 1, SEQ, 2, 64))
    ns = cat('ns', (NPS, 1, 2, 2, 32, 64))
    return (y_prompt, y_sample, nk, nv, ns)
```
